# Optimizing a Trainium2 kernel written in Bass

```python
import math
import jax, jax.numpy as jnp
from jax import lax
import numpy as np


D_MODEL = 1024
BATCH = 8
SEQ = 2048
DEPTH = 4
DEC_BATCH = 128
DEC_SEQ = 4
PAST_LEN = 8192
PAGE_SIZE = 128

HEAD_DIM = 64
MIX_WIDTH = D_MODEL
H_A = MIX_WIDTH // 256
H_B = MIX_WIDTH // 128
KV_HEADS = 2
GQA_GROUP = H_B // KV_HEADS
H_C = MIX_WIDTH // 256
W_A = H_A * HEAD_DIM
W_B = H_B * HEAD_DIM
W_KV = KV_HEADS * HEAD_DIM
W_C = H_C * HEAD_DIM
CHUNK = 128
WINDOW = 128
CONV_W = 3
D_FF = ((8 * D_MODEL) // 3 + 255) // 256 * 256
EPS = 1e-6
NEG = -1e30
SCALE = HEAD_DIM ** -0.5
PROJ_SIZES = [W_A, W_A, W_B, W_KV, W_KV, W_C, W_C, W_C]
IN_WIDTH = sum(PROJ_SIZES)
PROJ_SPLITS = np.cumsum(PROJ_SIZES)[:-1].tolist()

kernel_name = "hymba_style_gmlp_swa_conv_macaron_step"


def rms_norm(x, g):
    xf = x.astype(jnp.float32)
    y = xf * lax.rsqrt(jnp.mean(xf * xf, axis=-1, keepdims=True) + EPS)
    return (y * g.astype(jnp.float32)).astype(x.dtype)


def head_layer_norm(v, g):
    vf = v.astype(jnp.float32)
    mu = jnp.mean(vf, axis=-1, keepdims=True)
    var = jnp.mean(jnp.square(vf - mu), axis=-1, keepdims=True)
    return ((vf - mu) * lax.rsqrt(var + EPS) * g.astype(jnp.float32)).astype(v.dtype)


def swiglu(x, w_gu, w_down):
    gate, up = jnp.split(x @ w_gu, 2, axis=-1)
    return (jax.nn.silu(gate) * up) @ w_down


def half_ffn(x, g_pre, g_post, w_gu, w_down):
    return x + 0.5 * rms_norm(swiglu(rms_norm(x, g_pre), w_gu, w_down), g_post)


def alibi_slopes():
    return jnp.asarray(2.0 ** (-8.0 * np.arange(1, H_B + 1) / H_B), dtype=jnp.float32)


def sink_softmax(scores, sink):
    sink = sink.astype(jnp.float32)
    m = jnp.maximum(jnp.max(scores, axis=-1, keepdims=True), sink)
    p = jnp.exp(scores - m)
    return p / (jnp.sum(p, axis=-1, keepdims=True) + jnp.exp(sink - m))


def chunk_mlp(ua, va, w_sgu, b_sgu, g_sgu):
    B, S, _ = ua.shape
    L = min(S, CHUNK)
    n = S // L
    u = jax.nn.gelu(ua, approximate=False).reshape(B, n, L, H_A, HEAD_DIM)
    v = head_layer_norm(jax.nn.gelu(va, approximate=False).reshape(B, S, H_A, HEAD_DIM),
                        g_sgu.reshape(H_A, HEAD_DIM))
    w = jnp.where(jnp.tril(jnp.ones((L, L), dtype=bool)), w_sgu[:, :L, :L], 0)
    mix = jnp.einsum('hts,bnshd->bnthd', w, v.reshape(B, n, L, H_A, HEAD_DIM))
    mix = mix + b_sgu[:, :L].T[None, None, :, :, None]
    return (u * mix).reshape(B, S, W_A), v


def swa_prompt(q, k, v, sinks, slopes):
    B, S = q.shape[:2]
    nb = S // WINDOW
    qb = q.reshape(B, nb, WINDOW, KV_HEADS, GQA_GROUP, HEAD_DIM)

    def band(t):
        tb = t.reshape(B, nb, WINDOW, KV_HEADS, HEAD_DIM)
        prev = jnp.pad(tb[:, :-1], ((0, 0), (1, 0), (0, 0), (0, 0), (0, 0)))
        return jnp.concatenate([prev, tb], axis=2)

    kb, vb = band(k), band(v)
    kpos = jnp.arange(2 * WINDOW)
    dist = (WINDOW + jnp.arange(WINDOW))[:, None] - kpos[None, :]
    valid = ((dist >= 0) & (dist < WINDOW))[None] & \
        ((jnp.arange(nb) > 0)[:, None, None] | (kpos >= WINDOW)[None, None, :])
    scores = jnp.einsum('bnqkgd,bnskd->bnkgqs', qb, kb,
                        preferred_element_type=jnp.float32) * SCALE
    scores = scores - slopes.reshape(KV_HEADS, GQA_GROUP)[:, :, None, None] * dist.astype(jnp.float32)
    scores = jnp.where(valid[None, :, None, None], scores, NEG)
    probs = sink_softmax(scores, sinks.reshape(KV_HEADS, GQA_GROUP)[:, :, None, None])
    out = jnp.einsum('bnkgqs,bnskd->bnqkgd', probs.astype(vb.dtype), vb)
    return out.reshape(B, S, W_B)


def swa_sample(q, k, v, k_buf, v_buf, sinks, slopes):
    Bd, T = q.shape[:2]
    k_all = jnp.concatenate([k_buf, k], axis=1)
    v_all = jnp.concatenate([v_buf, v], axis=1)
    dist = (WINDOW + jnp.arange(T))[:, None] - jnp.arange(WINDOW + T)[None, :]
    valid = (dist >= 0) & (dist < WINDOW)
    scores = jnp.einsum('btkgd,bskd->bkgts', q, k_all,
                        preferred_element_type=jnp.float32) * SCALE
    scores = scores - slopes.reshape(KV_HEADS, GQA_GROUP)[:, :, None, None] * dist.astype(jnp.float32)
    scores = jnp.where(valid[None, None, None], scores, NEG)
    probs = sink_softmax(scores, sinks.reshape(KV_HEADS, GQA_GROUP)[:, :, None, None])
    out = jnp.einsum('bkgts,bskd->btkgd', probs.astype(v_all.dtype), v_all)
    return out.reshape(Bd, T, W_B), k_all[:, T:], v_all[:, T:]


def short_conv(zp, w_conv, T):
    y = zp[:, 0:T] * w_conv[0]
    for j in range(1, CONV_W):
        y = y + zp[:, j:j + T] * w_conv[j]
    return y


def merge_groups(ya, yb, yc, g_out, w_out):
    ga, gb, gc = jnp.split(g_out, [W_A, W_A + W_B])
    y = jnp.concatenate([rms_norm(ya, ga), rms_norm(yb, gb), rms_norm(yc, gc)], axis=-1)
    return y @ w_out


def mixer_prompt(h, w_in, w_out, g_out, w_sgu, b_sgu, g_sgu, sinks, w_conv, slopes):
    B, S, _ = h.shape
    ua, va, q, k, v, gb, gc, hc = jnp.split(h @ w_in, PROJ_SPLITS, axis=-1)
    ya, _ = chunk_mlp(ua, va, w_sgu, b_sgu, g_sgu)
    k = k.reshape(B, S, KV_HEADS, HEAD_DIM)
    v = v.reshape(B, S, KV_HEADS, HEAD_DIM)
    yb = swa_prompt(q.reshape(B, S, KV_HEADS, GQA_GROUP, HEAD_DIM), k, v, sinks, slopes)
    z = gc * hc
    zp = jnp.pad(z, ((0, 0), (CONV_W - 1, 0), (0, 0)))
    yc = gb * short_conv(zp, w_conv, S)
    y = merge_groups(ya, yb, yc, g_out, w_out)
    return y, k[:, S - WINDOW:], v[:, S - WINDOW:], z[:, S - (CONV_W - 1):]


def mixer_sample(h, k_buf, v_buf, conv_buf, w_in, w_out, g_out, w_sgu, b_sgu, g_sgu, sinks, w_conv, slopes):
    Bd, T, _ = h.shape
    ua, va, q, k, v, gb, gc, hc = jnp.split(h @ w_in, PROJ_SPLITS, axis=-1)
    ya, v_sgu = chunk_mlp(ua, va, w_sgu, b_sgu, g_sgu)
    yb, k_new, v_new = swa_sample(q.reshape(Bd, T, KV_HEADS, GQA_GROUP, HEAD_DIM),
                                  k.reshape(Bd, T, KV_HEADS, HEAD_DIM),
                                  v.reshape(Bd, T, KV_HEADS, HEAD_DIM),
                                  k_buf, v_buf, sinks, slopes)
    z = gc * hc
    zp = jnp.concatenate([conv_buf, z], axis=1)
    yc = gb * short_conv(zp, w_conv, T)
    y = merge_groups(ya, yb, yc, g_out, w_out)
    return y, v_sgu, k_new, v_new, zp[:, T:]


def setup_inputs(seed: int = 0) -> dict:
    key = jax.random.key(seed)
    ks = jax.random.split(key, 16)
    f32 = jnp.float32

    def nrm(k, shape, s):
        return jax.random.normal(k, shape, f32) * s

    return {
        "x_prompt": nrm(ks[0], (BATCH, SEQ, D_MODEL), 1.0),
        "x_sample": nrm(ks[1], (DEC_BATCH, DEC_SEQ, D_MODEL), 1.0),
        "cache_swa_k": nrm(ks[2], (DEPTH, DEC_BATCH, WINDOW, KV_HEADS, HEAD_DIM), 1.0),
        "cache_swa_v": nrm(ks[3], (DEPTH, DEC_BATCH, WINDOW, KV_HEADS, HEAD_DIM), 1.0),
        "cache_conv": nrm(ks[4], (DEPTH, DEC_BATCH, CONV_W - 1, W_C), 1.0),
        "norm_g": 1.0 + nrm(ks[5], (DEPTH, 6, D_MODEL), 0.05),
        "w_ffn_gu": nrm(ks[6], (DEPTH, 2, D_MODEL, 2 * D_FF), D_MODEL ** -0.5),
        "w_ffn_down": nrm(ks[7], (DEPTH, 2, D_FF, D_MODEL), D_FF ** -0.5),
        "w_mix_in": nrm(ks[8], (DEPTH, D_MODEL, IN_WIDTH), D_MODEL ** -0.5),
        "w_mix_out": nrm(ks[9], (DEPTH, MIX_WIDTH, D_MODEL), MIX_WIDTH ** -0.5),
        "g_mix_out": 1.0 + nrm(ks[10], (DEPTH, MIX_WIDTH), 0.05),
        "w_sgu": nrm(ks[11], (DEPTH, H_A, CHUNK, CHUNK), CHUNK ** -0.5),
        "b_sgu": 1.0 + nrm(ks[12], (DEPTH, H_A, CHUNK), 0.1),
        "g_sgu": 1.0 + nrm(ks[13], (DEPTH, W_A), 0.05),
        "attn_sinks": nrm(ks[14], (DEPTH, H_B), 1.0),
        "w_conv": nrm(ks[15], (DEPTH, CONV_W, W_C), CONV_W ** -0.5),
    }


def reference(x_prompt, x_sample, cache_swa_k, cache_swa_v, cache_conv, norm_g, w_ffn_gu, w_ffn_down,
              w_mix_in, w_mix_out, g_mix_out, w_sgu, b_sgu, g_sgu, attn_sinks, w_conv):
    slopes = alibi_slopes()
    xp, xs = x_prompt, x_sample
    sgu_s, kp_l, vp_l, ks_l, vs_l, cp_l, cs_l = [], [], [], [], [], [], []
    for l in range(DEPTH):
        g = norm_g[l]
        xp = half_ffn(xp, g[0], g[1], w_ffn_gu[l, 0], w_ffn_down[l, 0])
        xs = half_ffn(xs, g[0], g[1], w_ffn_gu[l, 0], w_ffn_down[l, 0])
        mp, kp, vp, cp = mixer_prompt(rms_norm(xp, g[2]), w_mix_in[l], w_mix_out[l], g_mix_out[l],
                                      w_sgu[l], b_sgu[l], g_sgu[l], attn_sinks[l], w_conv[l], slopes)
        xp = xp + rms_norm(mp, g[3])
        ms, vsg, kn, vn, cn = mixer_sample(rms_norm(xs, g[2]), cache_swa_k[l], cache_swa_v[l], cache_conv[l],
                                           w_mix_in[l], w_mix_out[l], g_mix_out[l], w_sgu[l], b_sgu[l],
                                           g_sgu[l], attn_sinks[l], w_conv[l], slopes)
        xs = xs + rms_norm(ms, g[3])
        xp = half_ffn(xp, g[4], g[5], w_ffn_gu[l, 1], w_ffn_down[l, 1])
        xs = half_ffn(xs, g[4], g[5], w_ffn_gu[l, 1], w_ffn_down[l, 1])
        sgu_s.append(vsg)
        kp_l.append(kp)
        vp_l.append(vp)
        ks_l.append(kn)
        vs_l.append(vn)
        cp_l.append(cp)
        cs_l.append(cn)
    return (xp, xs, jnp.stack(sgu_s), jnp.stack(kp_l), jnp.stack(vp_l), jnp.stack(ks_l),
            jnp.stack(vs_l), jnp.stack(cp_l), jnp.stack(cs_l))
```

```python
import numpy as np
import concourse.bass as bass
import concourse.mybir as mybir
from concourse.bass_utils import run_bass_kernel_spmd

F32 = mybir.dt.float32
BF16 = mybir.dt.bfloat16
AF = mybir.ActivationFunctionType
ALU = mybir.AluOpType
AX = mybir.AxisListType

ENGS = ("pe", "act", "dve", "pool", "sp")
EPS = 1e-6
NEGB = -30000.0
TM = 1088
NSLOT_A = 4
RAW_ONLY = False
NCORE = 8


class Reg:
    __slots__ = ("w", "r", "name")

    def __init__(self, name=""):
        self.w = None
        self.r = []
        self.name = name


class Grp:
    __slots__ = ("sem", "n")

    def __init__(self, sem):
        self.sem = sem
        self.n = 0


class Ins:
    __slots__ = ("eng", "fn", "deps", "signal", "count", "dma", "dval", "tiny")

    def __init__(self, eng, fn, deps, dma=None):
        self.tiny = False
        self.eng = eng
        self.fn = fn
        self.deps = deps
        self.signal = False
        self.count = 0
        self.dma = dma
        self.dval = 0


class K:
    def __init__(self, nc, same_engine_sync=True):
        self.nc = nc
        self.streams = {e: [] for e in ENGS}
        self.all = []
        self.sems = {e: nc.alloc_semaphore("s_" + e) for e in ENGS}
        self.same = same_engine_sync
        self.tiny_ctx = False
        self.out_grps = []
        self._n = 0

    def sb(self, shape, dt, name=None):
        self._n += 1
        return self.nc.alloc_sbuf_tensor("sb_" + (name or f"t{self._n}"), list(shape), dt)

    def ps(self, shape, dt=F32, name=None):
        self._n += 1
        return self.nc.alloc_psum_tensor(name or f"p{self._n}", list(shape), dt)

    def grp(self):
        self._n += 1
        return Grp(self.nc.alloc_semaphore(f"g{self._n}"))

    def _deps(self, eng, reads, writes):
        deps = []
        raw = set()
        for b in reads:
            if b.w is not None:
                deps.append(b.w)
                raw.add(id(b.w))
        for b in writes:
            if b.w is not None:
                deps.append(b.w)
            deps.extend(b.r)
        out = []
        seen = set()
        for d in deps:
            if id(d) in seen:
                continue
            seen.add(id(d))
            if d.dma is None and d.eng == eng:
                if eng == "pe" or not (self.same or d.tiny) or (RAW_ONLY and id(d) not in raw):
                    continue
            out.append((d, 16 * d.dma.n if d.dma is not None else 0))
        return out

    def I(self, eng, fn, reads=(), writes=(), out=None):
        ins = Ins(eng, fn, self._deps(eng, reads, writes))
        ins.tiny = self.tiny_ctx
        if out is not None:
            n = 1
            for d_ in out.shape[1:]:
                n *= d_
            if n < 256:
                ins.tiny = True
        for b in reads:
            b.r.append(ins)
        for b in writes:
            b.w = ins
            b.r = []
        self.streams[eng].append(ins)
        self.all.append(ins)
        return ins

    def dma(self, q, out_ap, in_ap, reads=(), writes=(), grp=None, is_out=False, **kw):
        if grp is None:
            grp = self.grp()
        deps = self._deps("__dma__", reads, writes)

        def fn(e, out_ap=out_ap, in_ap=in_ap, kw=kw):
            return e.dma_start(out=out_ap, in_=in_ap, **kw)
        ins = Ins(q, fn, deps, dma=grp)
        grp.n += 1
        ins.dval = 16 * grp.n
        for b in reads:
            b.r.append(ins)
        for b in writes:
            b.w = ins
            b.r = []
        self.streams[q].append(ins)
        self.all.append(ins)
        if is_out and grp not in self.out_grps:
            self.out_grps.append(grp)
        return ins

    def finish(self):
        nc = self.nc
        for ins in self.all:
            for d, _ in ins.deps:
                if d.dma is None:
                    d.signal = True
        cnt = {e: 0 for e in ENGS}
        for ins in self.all:
            if ins.dma is None and ins.signal:
                cnt[ins.eng] += 1
                ins.count = cnt[ins.eng]
        engmap = {"pe": "tensor", "act": "scalar", "dve": "vector", "pool": "gpsimd", "sp": "sync"}
        with nc.Block() as block:
            for e in ENGS:
                stream = self.streams[e]
                final = [(g.sem, 16 * g.n) for g in self.out_grps] if e == "sp" else []

                def body(eng, stream=stream, e=e, final=final):
                    seen = {}
                    for ins in stream:
                        waits = {}
                        for d, dv_ in ins.deps:
                            if d.dma is not None:
                                s, v = d.dma.sem, dv_
                            else:
                                s, v = self.sems[d.eng], d.count
                            if seen.get(s.num, 0) >= v:
                                continue
                            if s.num not in waits or waits[s.num][1] < v:
                                waits[s.num] = (s, v)
                        for s, v in waits.values():
                            eng.wait_ge(s, v)
                            seen[s.num] = v
                        bi = ins.fn(eng)
                        if ins.dma is not None:
                            bi.then_inc(ins.dma.sem, 16)
                        elif ins.signal:
                            bi.then_inc(self.sems[e], 1)
                    for s, v in final:
                        eng.wait_ge(s, v)
                getattr(block, engmap[e])(body)
        return nc


class Rot:
    def __init__(self, items):
        self.items = items
        self.i = 0

    def next(self):
        it = self.items[self.i % len(self.items)]
        self.i += 1
        return it


class WStream:
    def __init__(self, k, seq, slots, loader):
        self.k, self.seq, self.slots, self.loader = k, seq, slots, loader
        self.pos = 0
        self.emitted = 0

    def get(self, key, held=0):
        assert self.seq[self.pos] == key, (self.seq[self.pos], key)
        n = len(self.slots)
        while self.emitted < min(len(self.seq), self.pos - held + n):
            t, r, g = self.slots[self.emitted % n]
            self.loader(self.seq[self.emitted], t, r, g)
            self.emitted += 1
        s = self.slots[self.pos % n]
        self.pos += 1
        return s[0], s[1]


C_ID, C_TRIL, C_MS, C_E = 0, 128, 256, 320
C_SEL = 384
C_TOT = 386
C_BIAS, C_BSC, C_BSN = 0, 2048, 2560
B_TOT = 3072


def _consts():
    cf = np.zeros((128, C_TOT), np.float32)
    cbias = np.zeros((128, B_TOT), np.float32)
    cf[:, C_ID:C_ID + 128] = np.eye(128, dtype=np.float32)
    s = np.arange(128)[:, None]
    t = np.arange(128)[None, :]
    cf[:, C_TRIL:C_TRIL + 128] = (s <= t).astype(np.float32)
    slopes = 2.0 ** (-8.0 * np.arange(1, 9) / 8)
    for g in range(2):
        for hp in range(4):
            sl = slopes[4 * g + hp]
            cur = np.where(s <= t, -sl * (t - s), NEGB)
            prev = np.where(s > t, -sl * (128 + t - s), NEGB)
            o = C_BIAS + g * 512 + hp * 128
            cbias[:, o:o + 128] = cur
            o = C_BIAS + 1024 + g * 512 + hp * 128
            cbias[:, o:o + 128] = prev
    bsc = np.zeros((128, 16, 8, 4), np.float32)
    ss = np.arange(128)[:, None]
    tt = np.arange(4)[None, :]
    for h in range(8):
        bsc[:, :, h, :] = np.where(ss > tt, -slopes[h] * (128 + tt - ss), NEGB)[:, None, :]
    cbias[:, C_BSC:C_BSC + 512] = bsc.reshape(128, 512)
    bsn = np.full((16, 4, 16, 8, 4), NEGB, np.float32)
    for b in range(16):
        for tp in range(4):
            for t_ in range(4):
                if tp <= t_:
                    bsn[b, tp, b, :, t_] = -slopes * (t_ - tp)
    cbias[0:64, C_BSN:C_BSN + 512] = bsn.reshape(64, 512)
    ms = np.zeros((16, 4, 16, 4), np.float32)
    for b in range(16):
        for s_ in range(4):
            for t_ in range(4):
                if s_ <= t_:
                    ms[b, s_, b, t_] = 1.0
    cf[0:64, C_MS:C_MS + 64] = ms.reshape(64, 64)
    e = np.zeros((4, 16, 4), np.float32)
    for s_ in range(4):
        e[s_, :, s_] = 1.0
    cf[0:4, C_E:C_E + 64] = e.reshape(4, 64)
    cf[0, C_SEL] = 1.0
    cf[1, C_SEL + 1] = 1.0
    return cf, cbias


P_GV, P_GA, P_GB, P_GC, P_WC, P_SK = 0, 192, 208, 240, 248, 272
P_TOT = 304


def _prm(norm_g, g_mix_out, w_conv, attn_sinks):
    p = np.zeros((128, P_TOT), np.float32)
    p[:, P_GV:P_GV + 192] = norm_g.reshape(4, 6, 8, 128).transpose(3, 0, 1, 2).reshape(128, 192)
    p[0:64, P_GA:P_GA + 16] = g_mix_out[:, 0:256].reshape(4, 4, 64).transpose(2, 0, 1).reshape(64, 16)
    p[0:64, P_GB:P_GB + 32] = g_mix_out[:, 256:768].reshape(4, 8, 64).transpose(2, 0, 1).reshape(64, 32)
    p[:, P_GC:P_GC + 8] = g_mix_out[:, 768:1024].reshape(4, 2, 128).transpose(2, 0, 1).reshape(128, 8)
    p[:, P_WC:P_WC + 24] = w_conv.reshape(4, 3, 2, 128).transpose(3, 0, 1, 2).reshape(128, 24)
    p[:, P_SK:P_SK + 32] = np.broadcast_to(attn_sinks.reshape(1, 32), (128, 32))
    return p


def build(depth=4, tiles=(0, 1), same_sync=True, nsteps=None, mix_stop=99):
    nc = bass.Bass("TRN2", target_bir_lowering=False)
    k = K(nc, same_engine_sync=same_sync)

    def din(name, shape):
        return nc.dram_tensor(name, list(shape), F32, kind="ExternalInput").ap()

    def dout(name, shape):
        return nc.dram_tensor(name, list(shape), F32, kind="ExternalOutput").ap()

    xT = din("xT", [1024, 2112])
    ck = din("ck", [4, 16, 128, 128])
    cv = din("cv", [4, 16, 128, 128])
    cc_in = din("cc", [4, 16, 2, 256])
    w_gu = din("w_gu", [depth, 2, 1024, 5632])
    w_dn = din("w_dn", [depth, 2, 2816, 1024])
    w_in = din("w_in", [depth, 1024, 2048])
    w_out = din("w_out", [depth, 1024, 1024])
    w_sgu = din("w_sgu", [4, 4, 128, 128])
    b_sgu = din("b_sgu", [4, 4, 128])
    g_sgu = din("g_sgu", [4, 256])
    prm_d = din("prm", [128, P_TOT])
    cf_d = din("cf", [128, C_TOT])
    cbias_d = din("cbias", [128, B_TOT])

    yT = dout("yT", [1024, 2112])
    o_sgu = dout("o_sgu", [4, 64, 256])
    o_kp = dout("o_kp", [4, 128, 128])
    o_vp = dout("o_vp", [4, 128, 128])
    o_ks = dout("o_ks", [4, 16, 128, 128])
    o_vs = dout("o_vs", [4, 16, 128, 128])
    o_cp = dout("o_cp", [4, 2, 256])
    o_cs = dout("o_cs", [4, 32, 256])

    X = k.sb([128, 8, TM], F32, "X")
    H = k.sb([128, 8, TM], BF16, "H")
    ACTt = k.sb([128, 22 * TM], BF16, "ACT")
    Dt = k.sb([128, 8720], F32, "D")
    XR = [[Reg() for _ in range(3)] for _ in range(8)]
    HR = [[Reg() for _ in range(3)] for _ in range(8)]
    AR = [[Reg() for _ in range(3)] for _ in range(22)]
    DR = [[Reg() for _ in range(3)] for _ in range(8)]

    def actv(f, off, w):
        return ACTt[:, f * TM + off: f * TM + off + w]

    def dv(c, off, w):
        return Dt[:, c * TM + off: c * TM + off + w]

    ZW = TM + 2

    def zv(cc, a, b):
        return Dt[:, cc * ZW + a: cc * ZW + b]
    Dbf = Dt[:, 2 * ZW: 8720].bitcast(BF16)

    def uv(h, off, w):
        return Dbf[0:64, h * TM + off: h * TM + off + w]

    def qv(h, off, w):
        return Abf[0:64, 4352 + h * TM + off: 4352 + h * TM + off + w]

    def kdv(kv, off, w):
        return Dbf[0:64, 8704 + kv * TM + off: 8704 + kv * TM + off + w]

    def gbv(cc, off, w):
        return Dbf[:, 10880 + cc * TM + off: 10880 + cc * TM + off + w]
    Abf = ACTt.ap()

    def yav(h, off, w):
        return Abf[0:64, h * TM + off: h * TM + off + w]

    Hflat = H.ap().rearrange("p c t -> p (c t)")

    def ybv(h, off, w):
        return Hflat[0:64, h * TM + off: h * TM + off + w]

    def ycv(cc, off, w):
        return Abf[:, 13056 + cc * TM + off: 13056 + cc * TM + off + w]

    def vta(blk, p0, p1, a, b):
        return Abf[p0:p1, 15232 + blk * 256 + a: 15232 + blk * 256 + b]

    def vtb(blk, p0, p1, a, b):
        return Abf[p0:p1, 17536 + blk * 128 + a: 17536 + blk * 128 + b]

    def ktc(b, kv):
        return Abf[0:64, 18688 + (b * 2 + kv) * 128: 18688 + (b * 2 + kv + 1) * 128]
    qs_ap = Abf[0:64, 22784: 22784 + 512]
    ks_ap = Abf[0:64, 23296: 23296 + 128]

    UR = [[Reg() for _ in range(3)] for _ in range(4)]
    QR = [[Reg() for _ in range(3)] for _ in range(8)]
    QU = [[Reg() for _ in range(3)] for _ in range(4)]
    UU = [[Reg() for _ in range(3)] for _ in range(2)]
    KDR = [[Reg() for _ in range(3)] for _ in range(2)]
    GBR = [[Reg() for _ in range(3)] for _ in range(2)]
    ZR = [[Reg() for _ in range(3)] for _ in range(2)]
    ZHR = [Reg() for _ in range(2)]
    YAR = [[Reg() for _ in range(3)] for _ in range(4)]
    YBR = HR
    YCR = [[Reg() for _ in range(3)] for _ in range(2)]
    VAR = [Reg() for _ in range(9)]
    VBR = [Reg() for _ in range(9)]
    KTR = Reg()
    QSR = Reg()
    KSR = Reg()

    cf = k.sb([128, C_TOT], F32, "cf")
    cfR = Reg()
    prm = k.sb([128, P_TOT], F32, "prm")
    prmR = Reg()
    cb = k.sb([128, 128 * 5 + 64], BF16, "cb")
    cbR = Reg()
    CB_ID, CB_O1024, CB_O256, CB_O512, CB_ONE, CB_E = 0, 128, 256, 384, 512, 640
    gh = k.sb([128, 4, 2, 8], F32, "gh")
    ghR = Reg()

    def gvec(l, i, c):
        o = P_GV + (l * 6 + i) * 8 + c
        return prm[:, o:o + 1]

    k.dma("sp", cf.ap(), cf_d, writes=[cfR])
    cbt = k.sb([128, B_TOT], BF16, "cbt")
    cbtR = Reg()
    k.dma("pool", cbt.ap(), cbias_d, writes=[cbtR])
    k.dma("sp", prm.ap(), prm_d, writes=[prmR])
    k.I("dve", lambda e: e.tensor_copy(out=cb[:, CB_ID:CB_ID + 128], in_=cf[:, C_ID:C_ID + 128]), reads=[cfR], writes=[cbR])
    for o, v in ((CB_O1024, 1.0 / 1024), (CB_O256, 1.0 / 256), (CB_O512, 1.0 / 512), (CB_ONE, 1.0)):
        k.I("dve", lambda e, o=o, v=v: e.memset(cb[:, o:o + 128], v), writes=[cbR])
    k.I("dve", lambda e: e.tensor_copy(out=cb[0:4, CB_E:CB_E + 64], in_=cf[0:4, C_E:C_E + 64]), reads=[cfR], writes=[cbR])
    for j, i in enumerate((1, 5)):
        for l in range(4):
            o = P_GV + (l * 6 + i) * 8
            k.I("dve", lambda e, l=l, j=j, o=o: e.tensor_single_scalar(out=gh[:, l, j, :], in_=prm[:, o:o + 8], scalar=0.5, op=ALU.mult),
                reads=[prmR], writes=[ghR])

    epsc = k.sb([128, 1], F32, "epsc")
    k.I("dve", lambda e: e.memset(epsc.ap(), EPS), writes=[cbR])
    f32pool = Rot([(k.sb([128, 512], F32), Reg()) for _ in range(3)])
    bfpool = Rot([(k.sb([128, 512], BF16), Reg()) for _ in range(4)])
    rspool = Rot([(k.sb([128, 512], F32), Reg()) for _ in range(2)])
    small = k.sb([128, 64], F32, "small")
    smallR = Reg()
    small2 = [small[:, 0:32], small[:, 32:64]]
    small2R = [Reg(), Reg()]
    stg = Rot([(k.sb([128, 256], F32), Reg(), k.grp()) for _ in range(2)])
    vgpool = Rot(f32pool.items[0:2])
    kprev = k.sb([64, 4, 2, 128], BF16, "kprev")
    vprev = k.sb([128, 4, 128], BF16, "vprev")
    zprev = k.sb([128, 4, 2, 2], F32, "zprev")
    KPR = [Reg() for _ in range(4)]
    VPR = [Reg() for _ in range(4)]
    ZPR = [Reg() for _ in range(4)]
    gsg = k.sb([128, 256], F32, "gsg")
    gsgR = Reg()
    gsgG = k.grp()
    bsgG = k.grp()
    bsg = k.sb([1, 4, 128], BF16, "bsg")
    bsgR = Reg()
    bss = k.sb([1, 4, 64], BF16, "bss")
    bssR = Reg()
    wsg32 = k.sb([128, 4, 128], BF16, "wsgb")
    wsg32R = Reg()
    wsgG = k.grp()
    WT = k.sb([128, 4, 128], BF16, "WT")
    WTR = Reg()
    Xs = k.sb([4, 4, 64], BF16, "Xs")
    XsR = Reg()
    Wblk = k.sb([64, 4, 64], BF16, "Wblk")
    WblkR = Reg()
    sexp = k.sb([64, 8], F32, "sexp")
    sexpR = Reg()
    sk_hb = k.sb([2, 8], BF16, "sk_hb")
    sk_w = k.sb([2, 24], F32, "sk_w")
    skR = Reg()
    skP = k.sb([2, 8, 128], BF16, "skP")
    skS = k.sb([2, 512], BF16, "skS")
    skPR = Reg()
    kc = k.sb([128, 16, 128], BF16, "kc")
    vc = k.sb([128, 16, 128], BF16, "vc")
    kcR, vcR = Reg(), Reg()
    kcG, vcG = k.grp(), k.grp()
    zs = k.sb([128, 2, 16, 6], F32, "zs")
    zsR = Reg()
    cc32 = k.sb([32, 256], F32, "cc32")
    cc32R = Reg()
    ccG = k.grp()

    pb = [(k.ps([128, 512], F32), Reg()) for _ in range(8)]
    qG = [k.grp() for _ in range(4)]
    uG = [k.grp() for _ in range(2)]
    outG = k.grp()
    miscG = k.grp()

    seqA, seqB = [], []
    MIN_ORDER = [1, 4, 2, 3, 0, 5, 6, 7]
    steps = {}
    for tile in tiles:
        st_ = []
        for l in range(depth):
            st_ += [("ffn", l, 0), ("mix", l, 0), ("ffn", l, 1)]
        if nsteps is not None:
            st_ = st_[:nsteps]
        steps[tile] = st_
        for kind, l, i in st_:
            if kind == "ffn":
                seqA += [("gu", l, i, f) for f in range(22)]
                seqB += [("dn", l, i, c) for c in range(8)]
            else:
                seqA += [("min", l, j) for j in MIN_ORDER]
                seqB += [("mo", l, c) for c in range(8)]

    def loadA(key, t, r, g):
        if key[0] == "gu":
            _, l, i, f = key
            src = w_gu[l, i].rearrange("(kc p) f -> p kc f", p=128)
            k.dma("pool", t[:, :, 0:128], src[:, :, f * 128:(f + 1) * 128], writes=[r], grp=g)
            k.dma("pool", t[:, :, 128:256], src[:, :, 2816 + f * 128:2816 + (f + 1) * 128], writes=[r], grp=g)
        else:
            _, l, j = key
            src = w_in[l].rearrange("(kc p) f -> p kc f", p=128)[:, :, j * 256:(j + 1) * 256]
            k.dma("pool", t.ap(), src, writes=[r], grp=g)

    def loadB(key, t, r, g):
        if key[0] == "dn":
            _, l, i, c = key
            src = w_dn[l, i].rearrange("(fc p) d -> p fc d", p=128)[:, :, c * 128:(c + 1) * 128]
            k.dma("pool", t.ap(), src, writes=[r], grp=g)
        else:
            _, l, c = key
            sa = w_out[l, 0:256, c * 128:(c + 1) * 128].rearrange("(h d) f -> d h f", d=64)
            sbb = w_out[l, 256:768, c * 128:(c + 1) * 128].rearrange("(h d) f -> d h f", d=64)
            sc_ = w_out[l, 768:1024, c * 128:(c + 1) * 128].rearrange("(cc p) f -> p cc f", p=128)
            k.dma("pool", t[0:64, 0:4, :], sa, writes=[r], grp=g)
            k.dma("pool", t[0:64, 4:12, :], sbb, writes=[r], grp=g)
            k.dma("pool", t[:, 12:14, :], sc_, writes=[r], grp=g)

    wA = WStream(k, seqA, [(k.sb([128, 8, 256], BF16), Reg(), k.grp()) for _ in range(NSLOT_A)], loadA)
    wB = WStream(k, seqB, [(k.sb([128, 22, 128], BF16), Reg(), k.grp()) for _ in range(2)], loadB)

    def mm(out, lhsT, rhs, start, stop, reads, writes):
        k.I("pe", lambda e: e.matmul(out, lhsT=lhsT, rhs=rhs, start=start, stop=stop), reads=reads, writes=writes)

    def act(out, in_, func, reads, writes, **kw):
        k.I("act", lambda e: e.activation(out=out, in_=in_, func=func, **kw), reads=reads, writes=writes, out=out)

    def tt(out, in0, in1, op, reads, writes):
        k.I("dve", lambda e: e.tensor_tensor(out=out, in0=in0, in1=in1, op=op), reads=reads, writes=writes, out=out)

    def stt(out, in0, scalar, in1, op0, op1, reads, writes):
        k.I("dve", lambda e: e.scalar_tensor_tensor(out=out, in0=in0, scalar=scalar, in1=in1, op0=op0, op1=op1),
            reads=reads, writes=writes, out=out)

    def ts(out, in0, s1, s2, op0, op1, reads, writes):
        if s2 is None:
            k.I("dve", lambda e: e.tensor_single_scalar(out=out, in_=in0, scalar=s1, op=op0), reads=reads, writes=writes, out=out)
        else:
            k.I("dve", lambda e: e.tensor_scalar(out=out, in0=in0, scalar1=s1, scalar2=s2, op0=op0, op1=op1),
                reads=reads, writes=writes, out=out)

    def rsqrt_eps(out, in_, reads, writes):
        act(out, in_, AF.Sqrt, reads, writes, bias=epsc[0:out.shape[0], 0:1])
        k.I("dve", lambda e: e.reciprocal(out=out, in_=out), reads=writes, writes=writes, out=out)

    def rms_rstd(chunks, ones_off, pin, pout, w):
        pst, psr = pb[6]
        n = len(chunks)
        for i, (ap, rg) in enumerate(chunks):
            sq, sqr = bfpool.next()
            act(sq[0:pin, 0:w], ap, AF.Square, [rg], [sqr])
            mm(pst[0:pout, 0:w], cb[0:pin, ones_off:ones_off + pout], sq[0:pin, 0:w], i == 0, i == n - 1, [cbR, sqr], [psr])
        rs, rsr = rspool.next()
        rsqrt_eps(rs[0:pout, 0:w], pst[0:pout, 0:w], [psr], [rsr])
        return rs, rsr

    def prenorm(l, gi, NT):
        for nt, (off, w) in enumerate(NT):
            rs, rsr = rms_rstd([(X[:, c, off:off + w], XR[c][nt]) for c in range(8)], CB_O1024, 128, 128, w)
            for c in range(8):
                stt(H[:, c, off:off + w], X[:, c, off:off + w], gvec(l, gi, c), rs[:, 0:w], ALU.mult, ALU.mult,
                    [XR[c][nt], rsr, prmR], [HR[c][nt]])

    def postnorm_residual(l, gi, half, NT):
        for nt, (off, w) in enumerate(NT):
            rs, rsr = rms_rstd([(dv(c, off, w), DR[c][nt]) for c in range(8)], CB_O1024, 128, 128, w)
            for c in range(8):
                if half:
                    g = gh[:, l, 0 if gi == 1 else 1, c:c + 1]
                    gr = ghR
                else:
                    g = gvec(l, gi, c)
                    gr = prmR
                stt(dv(c, off, w), dv(c, off, w), g, rs[:, 0:w], ALU.mult, ALU.mult, [DR[c][nt], rsr, gr], [DR[c][nt]])
            for c in range(8):
                tt(X[:, c, off:off + w], X[:, c, off:off + w], dv(c, off, w), ALU.add, [XR[c][nt], DR[c][nt]], [XR[c][nt]])

    bankGU = Rot(pb[0:4])
    bankD = Rot(pb[4:6])

    def ffn(l, i, NT):
        prenorm(l, 0 if i == 0 else 4, NT)
        for f in range(22):
            st, sr = wA.get(("gu", l, i, f))
            for nt, (off, w) in enumerate(NT):
                bg, bgr = bankGU.next()
                bu, bur = bankGU.next()
                for kc in range(8):
                    mm(bg[:, 0:w], st[:, kc, 0:128], H[:, kc, off:off + w], kc == 0, kc == 7, [sr, HR[kc][nt]], [bgr])
                for kc in range(8):
                    mm(bu[:, 0:w], st[:, kc, 128:256], H[:, kc, off:off + w], kc == 0, kc == 7, [sr, HR[kc][nt]], [bur])
                sg, sgr = f32pool.next()
                act(sg[:, 0:w], bg[:, 0:w], AF.Silu, [bgr], [sgr])
                tt(actv(f, off, w), bu[:, 0:w], sg[:, 0:w], ALU.mult, [bur, sgr], [AR[f][nt]])
        for c in range(8):
            st, sr = wB.get(("dn", l, i, c))
            for nt, (off, w) in enumerate(NT):
                bd, bdr = bankD.next()
                for f in range(22):
                    mm(bd[:, 0:w], st[:, f, :], actv(f, off, w), f == 0, f == 21, [sr, AR[f][nt]], [bdr])
                act(dv(c, off, w), bd[:, 0:w], AF.Copy, [bdr], [DR[c][nt]])
        postnorm_residual(l, 1 if i == 0 else 5, True, NT)

    def layer_params(l):
        k.tiny_ctx = True
        _layer_params(l)
        k.tiny_ctx = False

    def _layer_params(l):
        k.dma("sp", gsg.ap(), g_sgu[l].partition_broadcast(128), writes=[gsgR], grp=gsgG)
        k.dma("pool", bsg.ap(), b_sgu[l:l + 1], writes=[bsgR], grp=bsgG)
        k.dma("pool", wsg32.ap(), w_sgu[l].rearrange("h t s -> t h s"), writes=[wsg32R], grp=wsgG)
        k.I("dve", lambda e: e.tensor_copy(out=bss.ap().rearrange("o h (b t) -> o h b t", t=4),
                                          in_=bsg[:, :, 0:4].unsqueeze(2).to_broadcast([1, 4, 16, 4])),
            reads=[bsgR], writes=[bssR])
        for h in range(4):
            pt, pr = pb[7]
            ptb_ = pt.ap().bitcast(BF16)
            k.I("pe", lambda e, h=h, ptb_=ptb_: e.transpose(ptb_[:, 0:128], wsg32[:, h, :], cb[:, CB_ID:CB_ID + 128]),
                reads=[wsg32R, cbR], writes=[pr])
            tt(WT[:, h, :], ptb_[:, 0:128], cf[:, C_TRIL:C_TRIL + 128], ALU.mult, [pr, cfR], [WTR])
        k.I("dve", lambda e: e.tensor_copy(out=Xs.ap().rearrange("s h (b t) -> s h b t", t=4),
                                          in_=WT[0:4, :, 0:4].unsqueeze(2).to_broadcast([4, 4, 16, 4])),
            reads=[WTR], writes=[XsR])
        for h in range(4):
            pt, pr = pb[7]
            mm(pt[0:64, 0:64], cb[0:4, CB_E:CB_E + 64], Xs[:, h, :], True, True, [cbR, XsR], [pr])
            tt(Wblk[:, h, :], pt[0:64, 0:64], cf[0:64, C_MS:C_MS + 64], ALU.mult, [pr, cfR], [WblkR])
        o = P_SK + l * 8
        act(sexp.ap(), prm[0:64, o:o + 8], AF.Exp, [prmR], [sexpR])
        k.I("dve", lambda e: e.tensor_copy(out=sk_hb.ap(), in_=sexp[0:2, :]), reads=[sexpR], writes=[skR])
        k.I("dve", lambda e: e.tensor_copy(out=sk_w[:, 0:8], in_=sk_hb.ap()), reads=[skR], writes=[skR])
        tt(sk_w[:, 8:16], sexp[0:2, :], sk_w[:, 0:8], ALU.subtract, [sexpR, skR], [skR])
        ts(sk_w[:, 0:8], sk_w[:, 0:8], cf[0:2, C_SEL:C_SEL + 1], None, ALU.mult, None, [skR, cfR], [skR])
        stt(sk_w[:, 16:24], sk_w[:, 8:16], cf[0:2, C_SEL + 1:C_SEL + 2], sk_w[:, 0:8], ALU.mult, ALU.add, [skR, cfR], [skR])
        k.I("dve", lambda e: e.tensor_copy(out=skP.ap(), in_=sk_w[:, 16:24].unsqueeze(2).to_broadcast([2, 8, 128])),
            reads=[skR], writes=[skPR])
        k.I("dve", lambda e: e.tensor_copy(out=skS.ap().rearrange("p (b h t) -> p b h t", b=16, h=8),
                                          in_=sk_w[:, 16:24].unsqueeze(1).unsqueeze(3).to_broadcast([2, 16, 8, 4])),
            reads=[skR], writes=[skPR])

    bankM = Rot(pb[0:6])

    def mixer(l, tile, NT):
        B = (tile == 1)
        nblk = 8
        has_s = B and len(NT) == 3
        layer_params(l)
        if has_s:
            k.dma("pool", kc.ap(), ck[l].rearrange("b s c -> s b c"), writes=[kcR], grp=kcG)
            k.dma("pool", vc.ap(), cv[l].rearrange("b s c -> s b c"), writes=[vcR], grp=vcG)
            k.dma("sp", cc32.ap(), cc_in[l].rearrange("b j c -> (b j) c"), writes=[cc32R], grp=ccG)
            k.dma("sp", o_ks[l, :, 0:124, :], ck[l, :, 4:128, :], grp=miscG, is_out=True)
            k.dma("sp", o_vs[l, :, 0:124, :], cv[l, :, 4:128, :], grp=miscG, is_out=True)
        prenorm(l, 2, NT)

        def ntof(blk):
            return blk // 4 if blk < 8 else 2

        def bail(stage):
            if mix_stop <= stage:
                while wA.pos < len(wA.seq) and wA.seq[wA.pos][0] == "min" and wA.seq[wA.pos][1] == l:
                    wA.pos += 1
                while wA.emitted < wA.pos:
                    wA.emitted += 1
                while wB.pos < len(wB.seq) and wB.seq[wB.pos][0] == "mo" and wB.seq[wB.pos][1] == l:
                    wB.pos += 1
                while wB.emitted < wB.pos:
                    wB.emitted += 1
                return True
            return False
        if bail(0):
            return
        sva, svar = wA.get(("min", l, 1))
        skv, skvr = wA.get(("min", l, 4), held=1)
        blocks = list(range(nblk)) + ([8] if has_s else [])
        lnq = []

        def ln_stats(blk, P, vg, vgr):
            par = blk % 2
            sm = small2[par]
            smR = small2R[par]
            v3 = vg[0:P, 0:256].rearrange("p (h d) -> p h d", d=64)
            s1, s2, mean, msq = sm[0:P, 0:4], sm[0:P, 4:8], sm[0:P, 8:12], sm[0:P, 12:16]
            var, rst, nb = sm[0:P, 16:20], sm[0:P, 20:24], sm[0:P, 24:28]
            k.tiny_ctx = True
            k.I("dve", lambda e: e.reduce_sum(out=s1, in_=v3, axis=AX.X), reads=[vgr], writes=[smR])
            t2, t2r = f32pool.items[2]
            act(t2[0:P, 0:256], vg[0:P, 0:256], AF.Square, [vgr], [t2r])
            k.I("dve", lambda e: e.reduce_sum(out=s2, in_=t2[0:P, 0:256].rearrange("p (h d) -> p h d", d=64), axis=AX.X),
                reads=[t2r], writes=[smR])
            k.I("dve", lambda e: e.tensor_single_scalar(out=mean, in_=s1, scalar=1.0 / 64, op=ALU.mult), reads=[smR], writes=[smR])
            tt(msq, mean, mean, ALU.mult, [smR], [smR])
            stt(var, s2, 1.0 / 64, msq, ALU.mult, ALU.subtract, [smR], [smR])
            k.tiny_ctx = False

            def finish():
                k.tiny_ctx = True
                rsqrt_eps(rst, var, [smR], [smR])
                stt(nb, mean, -1.0, rst, ALU.mult, ALU.mult, [smR], [smR])
                k.tiny_ctx = False
                for h in range(4):
                    act(vg[0:P, h * 64:(h + 1) * 64], vg[0:P, h * 64:(h + 1) * 64], AF.Identity, [vgr, smR], [vgr],
                        scale=rst[:, h:h + 1], bias=nb[:, h:h + 1])
                if blk == 8:
                    sg_, sgr_, sgg = stg.next()
                    tt(sg_[0:64, 0:256], vg[0:64, 0:256], gsg[0:64, :], ALU.mult, [vgr, gsgR], [sgr_])
                    k.dma("sp", o_sgu[l], sg_[0:64, 0:256], reads=[sgr_], grp=sgg, is_out=True)
                    k.I("dve", lambda e: e.tensor_copy(out=vta(8, 0, 64, 0, 256), in_=sg_[0:64, 0:256]), reads=[sgr_], writes=[VAR[8]])
                else:
                    tt(vta(blk, 0, P, 0, 256), vg[0:P, 0:256], gsg[0:P, :], ALU.mult, [vgr, gsgR], [VAR[blk]])
            return finish

        for blk in blocks:
            P = 128 if blk < 8 else 64
            t0 = blk * 128
            nt = ntof(blk)
            pt, pr = bankM.next()
            for kc_ in range(8):
                mm(pt[0:P, 0:256], H[:, kc_, t0:t0 + P], sva[:, kc_, :], kc_ == 0, kc_ == 7, [svar, HR[kc_][nt]], [pr])
            for kc_ in range(8):
                mm(pt[0:P, 256:512], H[:, kc_, t0:t0 + P], skv[:, kc_, :], kc_ == 0, kc_ == 7, [skvr, HR[kc_][nt]], [pr])
            vg, vgr = vgpool.next()
            act(vg[0:P, 0:256], pt[0:P, 0:256], AF.Gelu, [pr], [vgr])
            act(vtb(blk, 0, P, 0, 128), pt[0:P, 384:512], AF.Copy, [pr], [VBR[blk]])
            if blk == 8 or (B and blk == 7):
                sg_, sgr_, sgg = stg.next()
                act(sg_[0:P, 0:256], pt[0:P, 256:512], AF.Copy, [pr], [sgr_])
                if blk == 7:
                    k.dma("sp", o_kp[l], sg_[:, 0:128], reads=[sgr_], grp=sgg, is_out=True)
                    k.dma("sp", o_vp[l], sg_[:, 128:256], reads=[sgr_], grp=sgg, is_out=True)
                else:
                    for b_ in range(16):
                        k.dma("sp", o_ks[l, b_, 124:128, :], sg_[4 * b_:4 * b_ + 4, 0:128], reads=[sgr_], grp=sgg, is_out=True)
                        k.dma("sp", o_vs[l, b_, 124:128, :], sg_[4 * b_:4 * b_ + 4, 128:256], reads=[sgr_], grp=sgg, is_out=True)
            pend = lnq.pop(0) if lnq else None
            lnq.append(ln_stats(blk, P, vg, vgr))
            if pend is not None:
                pend()
        while lnq:
            lnq.pop(0)()

        if bail(1):
            return
        for kv in range(2):
            for nt, (off, w) in enumerate(NT):
                pt, pr = bankM.next()
                if nt < 2:
                    for kc_ in range(8):
                        mm(pt[0:64, 0:w], skv[:, kc_, kv * 64:(kv + 1) * 64], H[:, kc_, off:off + w], kc_ == 0, kc_ == 7, [skvr, HR[kc_][nt]], [pr])
                    act(kdv(kv, off, w), pt[0:64, 0:w], AF.Copy, [pr], [KDR[kv][nt]])
                else:
                    for kc_ in range(8):
                        mm(pt[0:64, 0:64], skv[:, kc_, kv * 64:(kv + 1) * 64], H[:, kc_, off:off + 64], kc_ == 0, kc_ == 7, [skvr, HR[kc_][nt]], [pr])
                    act(ks_ap[:, kv * 64:(kv + 1) * 64], pt[0:64, 0:64], AF.Copy, [pr], [KSR])

        for jq in range(2):
            st, sr = wA.get(("min", l, 2 + jq))
            for ci in range(2):
                he, ho = jq * 4 + 2 * ci, jq * 4 + 2 * ci + 1
                for nt, (off, w) in enumerate(NT):
                    if nt == 2:
                        continue
                    pt, pr = bankM.next()
                    for kc_ in range(8):
                        mm(pt[:, 0:w], st[:, kc_, ci * 128:(ci + 1) * 128], H[:, kc_, off:off + w], kc_ == 0, kc_ == 7, [sr, HR[kc_][nt]], [pr])
                    act(Abf[0:128, 4352 + he * TM + off: 4352 + he * TM + off + w], pt[:, 0:w], AF.Copy, [pr],
                        [QR[he][nt], QU[he // 2][nt]], scale=0.125)
                k.dma("sp", Abf[0:64, 4352 + ho * TM: 4352 + ho * TM + 1024], Abf[64:128, 4352 + he * TM: 4352 + he * TM + 1024],
                      reads=[QU[he // 2][0], QU[he // 2][1]], writes=[QR[ho][0], QR[ho][1]], grp=qG[he // 2])
            if has_s:
                for hh in range(4):
                    h = jq * 4 + hh
                    pt, pr = bankM.next()
                    for kc_ in range(8):
                        mm(pt[0:64, 0:64], st[:, kc_, hh * 64:(hh + 1) * 64], H[:, kc_, 1024:1088], kc_ == 0, kc_ == 7, [sr, HR[kc_][2]], [pr])
                    act(qs_ap.rearrange("p (b h t) -> p b h t", b=16, h=8)[:, :, h, :], pt[0:64, 0:64].rearrange("p (b t) -> p b t", t=4),
                        AF.Copy, [pr], [QSR], scale=0.125)
        units = []
        bankU = Rot([pb[6], pb[7]])
        SL = {}

        def ua_pair(ci):
            if 'ua' not in SL:
                SL['ua'] = wA.get(("min", l, 0))
            st, sr = SL['ua']
            he, ho = 2 * ci, 2 * ci + 1
            for nt, (off, w) in enumerate(NT):
                pt, pr = bankM.next()
                for kc_ in range(8):
                    mm(pt[:, 0:w], st[:, kc_, ci * 128:(ci + 1) * 128], H[:, kc_, off:off + w], kc_ == 0, kc_ == 7, [sr, HR[kc_][nt]], [pr])
                act(Dbf[0:128, he * TM + off: he * TM + off + w], pt[:, 0:w], AF.Gelu, [pr], [UR[he][nt], UU[ci][nt]])
            tw = NT[-1][0] + NT[-1][1]
            k.dma("sp", Dbf[0:64, ho * TM: ho * TM + tw], Dbf[64:128, he * TM: he * TM + tw],
                  reads=[UU[ci][nt] for nt in range(len(NT))], writes=[UR[ho][nt] for nt in range(len(NT))], grp=uG[ci])
        for ci in range(2):
            ua_pair(ci)

        def gb_unit(cc, nt, off, w):
            if 'gb' not in SL:
                SL['gb'] = wA.get(("min", l, 5))
            st, sr = SL['gb']
            pt, pr = bankM.next()
            for kc_ in range(8):
                mm(pt[:, 0:w], st[:, kc_, cc * 128:(cc + 1) * 128], H[:, kc_, off:off + w], kc_ == 0, kc_ == 7, [sr, HR[kc_][nt]], [pr])
            act(gbv(cc, off, w), pt[:, 0:w], AF.Copy, [pr], [GBR[cc][nt]])
        for cc in range(2):
            for nt, (off, w) in enumerate(NT):
                gb_unit(cc, nt, off, w)

        def z_setup():
            SL['gc'] = wA.get(("min", l, 6))
            SL['hc'] = wA.get(("min", l, 7), held=1)
            if not B:
                for cc in range(2):
                    k.I("dve", lambda e, cc=cc: e.memset(zv(cc, 0, 2), 0.0), writes=[ZHR[cc]])
            else:
                for cc in range(2):
                    k.I("dve", lambda e, cc=cc: e.tensor_copy(out=zv(cc, 0, 2), in_=zprev[:, l, cc, :]), reads=[ZPR[l]], writes=[ZHR[cc]])
            if has_s:
                for cc in range(2):
                    pt, pr = pb[7]
                    k.I("pe", lambda e, cc=cc, pt=pt: e.transpose(pt[:, 0:32], cc32[:, cc * 128:(cc + 1) * 128], cf[0:32, C_ID:C_ID + 32]),
                        reads=[cc32R, cfR], writes=[pr])
                    act(zs[:, cc, :, 0:2], pt[:, 0:32].rearrange("p (b j) -> p b j", j=2), AF.Copy, [pr], [zsR])
        z_setup()

        def z_unit(cc, nt, off, w):
            sgc, sgcr = SL['gc']
            shc, shcr = SL['hc']
            pg, pgr = bankM.next()
            ph, phr = bankM.next()
            for kc_ in range(8):
                mm(pg[:, 0:w], sgc[:, kc_, cc * 128:(cc + 1) * 128], H[:, kc_, off:off + w], kc_ == 0, kc_ == 7, [sgcr, HR[kc_][nt]], [pgr])
            for kc_ in range(8):
                mm(ph[:, 0:w], shc[:, kc_, cc * 128:(cc + 1) * 128], H[:, kc_, off:off + w], kc_ == 0, kc_ == 7, [shcr, HR[kc_][nt]], [phr])
            gt, gtr = f32pool.next()
            act(gt[:, 0:w], pg[:, 0:w], AF.Copy, [pgr], [gtr])
            if nt < 2:
                tt(zv(cc, 2 + off, 2 + off + w), ph[:, 0:w], gt[:, 0:w], ALU.mult, [phr, gtr], [ZR[cc][nt]])
            else:
                tt(zs[:, cc, :, 2:6], ph[:, 0:64].rearrange("p (b t) -> p b t", t=4),
                   gt[:, 0:64].rearrange("p (b t) -> p b t", t=4), ALU.mult, [phr, gtr, zsR], [zsR])
        for cc in range(2):
            for nt, (off, w) in enumerate(NT):
                z_unit(cc, nt, off, w)

        def z_tail():
            if not B:
                for cc in range(2):
                    k.I("dve", lambda e, cc=cc: e.tensor_copy(out=zprev[:, l, cc, :], in_=zv(cc, 1024, 1026)), reads=[ZR[cc][1]], writes=[ZPR[l]])
            else:
                sg_, sgr_, sgg = stg.next()
                for cc in range(2):
                    pt, pr = pb[7]
                    k.I("pe", lambda e, cc=cc, pt=pt: e.transpose(pt[0:2, 0:128], zv(cc, 1024, 1026), cf[:, C_ID:C_ID + 128]),
                        reads=[ZR[cc][1], cfR], writes=[pr])
                    act(sg_[0:2, cc * 128:(cc + 1) * 128], pt[0:2, 0:128], AF.Copy, [pr], [sgr_])
                k.dma("sp", o_cp[l], sg_[0:2, 0:256], reads=[sgr_], grp=sgg, is_out=True)
                if has_s:
                    sg_, sgr_, sgg = stg.next()
                    for cc in range(2):
                        pt, pr = pb[7]
                        zt, ztr = f32pool.next()
                        act(zt[:, 0:32].rearrange("p (b j) -> p b j", j=2), zs[:, cc, :, 4:6], AF.Copy, [zsR], [ztr])
                        k.I("pe", lambda e, cc=cc, pt=pt, zt=zt: e.transpose(pt[0:32, 0:128], zt[:, 0:32], cf[:, C_ID:C_ID + 128]),
                            reads=[ztr, cfR], writes=[pr])
                        act(sg_[0:32, cc * 128:(cc + 1) * 128], pt[0:32, 0:128], AF.Copy, [pr], [sgr_])
                    k.dma("sp", o_cs[l], sg_[0:32, 0:256], reads=[sgr_], grp=sgg, is_out=True)

        units.append(z_tail)

        def wcv(j, cc):
            o = P_WC + (l * 3 + j) * 2 + cc
            return prm[:, o:o + 1]
        def conv_unit(cc, nt, off, w):
            a, ar = f32pool.next()
            if nt < 2:
                zr = [ZR[cc][nt], ZHR[cc]] + ([ZR[cc][0]] if nt == 1 else [])
                ts(a[:, 0:w], zv(cc, 2 + off, 2 + off + w), wcv(2, cc), None, ALU.mult, ALU.bypass, zr + [prmR], [ar])
                stt(a[:, 0:w], zv(cc, 1 + off, 1 + off + w), wcv(1, cc), a[:, 0:w], ALU.mult, ALU.add, zr + [prmR, ar], [ar])
                stt(a[:, 0:w], zv(cc, off, off + w), wcv(0, cc), a[:, 0:w], ALU.mult, ALU.add, zr + [prmR, ar], [ar])
                tt(ycv(cc, off, w), a[:, 0:w], gbv(cc, off, w), ALU.mult, [ar, GBR[cc][nt]], [YCR[cc][nt]])
            else:
                a3 = a[:, 0:64].rearrange("p (b t) -> p b t", t=4)
                ts(a3, zs[:, cc, :, 2:6], wcv(2, cc), None, ALU.mult, ALU.bypass, [zsR, prmR], [ar])
                stt(a3, zs[:, cc, :, 1:5], wcv(1, cc), a3, ALU.mult, ALU.add, [zsR, prmR, ar], [ar])
                stt(a3, zs[:, cc, :, 0:4], wcv(0, cc), a3, ALU.mult, ALU.add, [zsR, prmR, ar], [ar])
                tt(ycv(cc, off, 64), a[:, 0:64], gbv(cc, off, 64), ALU.mult, [ar, GBR[cc][nt]], [YCR[cc][nt]])

        for cc in range(2):
            for nt, (off, w) in enumerate(NT):
                units.append(lambda cc=cc, nt=nt, off=off, w=w: conv_unit(cc, nt, off, w))

        def gmlp_unit(blk):
            P = 128 if blk < 8 else 64
            t0 = blk * 128
            nt = ntof(blk)
            pt, pr = pb[7]
            for h in range(4):
                if blk < 8:
                    mm(pt[0:64, h * 128:(h + 1) * 128], vta(blk, 0, 128, h * 64, (h + 1) * 64), WT[:, h, :], True, False, [VAR[blk], WTR], [pr])
                    mm(pt[0:64, h * 128:(h + 1) * 128], cb[0:1, CB_ONE:CB_ONE + 64], bsg[0:1, h, :], False, True, [cbR, bsgR], [pr])
                else:
                    mm(pt[0:64, h * 128:h * 128 + 64], vta(8, 0, 64, h * 64, (h + 1) * 64), Wblk[:, h, :], True, False, [VAR[8], WblkR], [pr])
                    mm(pt[0:64, h * 128:h * 128 + 64], cb[0:1, CB_ONE:CB_ONE + 64], bss[0:1, h, :], False, True, [cbR, bssR], [pr])
            uo = Dbf[0:64, t0: t0 + 4 * TM].rearrange("p (h t) -> p h t", t=TM)[:, :, 0:P]
            yo = Abf[0:64, t0: t0 + 4 * TM].rearrange("p (h t) -> p h t", t=TM)[:, :, 0:P]
            tt(yo, uo, pt[0:64, 0:512].rearrange("p (h t) -> p h t", t=128)[:, :, 0:P], ALU.mult,
               [pr] + [UR[h][nt] for h in range(4)], [YAR[h][nt] for h in range(4)])

        for blk in blocks:
            units.append(lambda blk=blk: gmlp_unit(blk))

        bankS = Rot(pb[0:4])
        items = [(blk, g) for blk in range(nblk) for g in range(2)]

        def kcur(blk, g):
            return kdv(g, blk * 128, 128), KDR[g][ntof(blk)]

        def emit_scores(blk, g):
            res = []
            gblk = tile * 8 + blk
            for which in (0, 1):
                if which == 1 and gblk == 0:
                    res.append(None)
                    continue
                pt, pr = bankS.next()
                bo = C_BIAS + which * 1024 + g * 512
                mm(pt.ap(), cb[:, CB_ID:CB_ID + 128], cbt[:, bo:bo + 512], True, False, [cbR, cbtR], [pr])
                for hp in range(4):
                    if which == 0:
                        ka, kr = kcur(blk, g)
                    elif blk > 0:
                        ka, kr = kcur(blk - 1, g)
                    else:
                        ka, kr = kprev[:, l, g, :], KPR[l]
                    mm(pt[:, hp * 128:(hp + 1) * 128], ka, qv(4 * g + hp, blk * 128, 128), False, hp == 3,
                       [kr, QR[4 * g + hp][ntof(blk)]], [pr])
                res.append((pt, pr))
            return res

        def emit_soft(blk, g, sc):
            ps_ = []
            for which in (0, 1):
                if sc[which] is None:
                    ps_.append(None)
                    continue
                pt, pr = sc[which]
                p_, p_r = bfpool.next()
                act(p_.ap(), pt.ap(), AF.Exp, [pr], [p_r])
                ps_.append((p_, p_r))
            return ps_

        def emit_pv(blk, g, ps_):
            po, por = pb[4]
            pd, pdr = pb[5]
            lst = [(w_, ps_[w_]) for w_ in (0, 1) if ps_[w_] is not None]
            for i, (which, (p_, p_r)) in enumerate(lst):
                if which == 0:
                    va_, vr_ = vtb(blk, 0, 128, g * 64, (g + 1) * 64), VBR[blk]
                elif blk > 0:
                    va_, vr_ = vtb(blk - 1, 0, 128, g * 64, (g + 1) * 64), VBR[blk - 1]
                else:
                    va_, vr_ = vprev[:, l, g * 64:(g + 1) * 64], VPR[l]
                mm(po[0:64, :], va_, p_.ap(), i == 0, i == len(lst) - 1, [vr_, p_r], [por])
            for i, (which, (p_, p_r)) in enumerate(lst):
                mm(pd[0:64, :], cb[:, CB_ONE:CB_ONE + 64], p_.ap(), i == 0, False, [cbR, p_r], [pdr])
            mm(pd[0:64, :], cb[0:2, CB_ONE:CB_ONE + 64], skP[0:2, 4 * g:4 * g + 4, :].rearrange("p h t -> p (h t)"), False, True,
               [cbR, skPR], [pdr])
            d_, d_r = f32pool.next()
            k.I("dve", lambda e, d_=d_, pd=pd: e.reciprocal(out=d_[0:64, :], in_=pd[0:64, :]), reads=[pdr], writes=[d_r])
            t0 = blk * 128
            yo = H[0:64, 4 * g:4 * g + 4, t0:t0 + 128]
            tt(yo, po[0:64, :].rearrange("p (h t) -> p h t", t=128), d_[0:64, :].rearrange("p (h t) -> p h t", t=128), ALU.mult,
               [por, d_r], [YBR[4 * g + hp][ntof(blk)] for hp in range(4)])

        upos = [0]
        per_item = (len(units) + len(items) - 1) // len(items)

        def pop_units(n):
            for _ in range(n):
                if upos[0] < len(units):
                    units[upos[0]]()
                    upos[0] += 1
        n_it = len(items)
        scs = {0: emit_scores(*items[0])}
        if n_it > 1:
            scs[1] = emit_scores(*items[1])
        pss = {0: emit_soft(items[0][0], items[0][1], scs.pop(0))}
        for i, (blk, g) in enumerate(items):
            if i + 2 < n_it:
                scs[i + 2] = emit_scores(*items[i + 2])
            if i + 1 < n_it:
                pss[i + 1] = emit_soft(items[i + 1][0], items[i + 1][1], scs.pop(i + 1))
            pop_units(per_item)
            emit_pv(blk, g, pss.pop(i))
        pop_units(len(units))

        if not B:
            for g in range(2):
                k.I("act", lambda e, g=g: e.activation(out=kprev[:, l, g, :], in_=kdv(g, 896, 128), func=AF.Copy),
                    reads=[KDR[g][1]], writes=[KPR[l]])
            k.I("act", lambda e: e.activation(out=vprev[:, l, :], in_=vtb(7, 0, 128, 0, 128), func=AF.Copy), reads=[VBR[7]], writes=[VPR[l]])

        if bail(4):
            return
        if has_s:
            for grp8 in range(4):
                pt, pr = bankM.next()
                ptb = pt.ap().bitcast(BF16)
                for j in range(8):
                    idx = grp8 * 8 + j
                    b, kv = idx // 2, idx % 2
                    k.I("pe", lambda e, ptb=ptb, j=j, b=b, kv=kv: e.transpose(ptb[0:64, j * 128:(j + 1) * 128], kc[:, b, kv * 64:(kv + 1) * 64], cb[:, CB_ID:CB_ID + 128]),
                        reads=[kcR, cbR], writes=[pr])
                act(Abf[0:64, 18688 + grp8 * 1024: 18688 + (grp8 + 1) * 1024], ptb[0:64, :], AF.Copy, [pr], [KTR])
            psc, pscr = pb[0]
            psn = [pb[1], pb[2]]
            pos = [pb[3], pb[4]]
            pd, pdr = pb[5]
            for b in range(16):
                for kv in range(2):
                    o_ = b * 32 + kv * 16
                    mm(psc[:, o_:o_ + 16], ktc(b, kv), qs_ap[:, o_:o_ + 16], True, True, [KTR, QSR], [pscr])
            for kv in range(2):
                mm(psn[kv][0][0:64, :], ks_ap[:, kv * 64:(kv + 1) * 64], qs_ap, True, True, [KSR, QSR], [psn[kv][1]])
            s_, s_r = f32pool.next()
            tt(s_.ap(), psc.ap(), cbt[:, C_BSC:C_BSC + 512], ALU.add, [pscr, cbtR], [s_r])
            pc_, pc_r = bfpool.next()
            act(pc_.ap(), s_.ap(), AF.Exp, [s_r], [pc_r])
            s2_, s2_r = f32pool.next()

            def kvv(ap, kv):
                return ap.rearrange("p (b x) -> p b x", x=32)[:, :, kv * 16:(kv + 1) * 16]
            for kv in range(2):
                tt(kvv(s2_[0:64, :], kv), kvv(psn[kv][0][0:64, :], kv), kvv(cbt[0:64, C_BSN:C_BSN + 512], kv), ALU.add,
                   [psn[kv][1], cbtR], [s2_r])
            pn_, pn_r = bfpool.next()
            act(pn_[0:64, :], s2_[0:64, :], AF.Exp, [s2_r], [pn_r])
            for kv in range(2):
                po, por = pos[kv]
                mm(po[0:64, :], vtb(8, 0, 64, kv * 64, (kv + 1) * 64), pn_[0:64, :], True, False, [VBR[8], pn_r], [por])
                for b in range(16):
                    o_ = b * 32 + kv * 16
                    mm(po[0:64, o_:o_ + 16], vc[:, b, kv * 64:(kv + 1) * 64], pc_[:, o_:o_ + 16], False, b == 15, [vcR, pc_r], [por])
            mm(pd[0:64, :], cb[:, CB_ONE:CB_ONE + 64], pc_.ap(), True, False, [cbR, pc_r], [pdr])
            mm(pd[0:64, :], cb[0:64, CB_ONE:CB_ONE + 64], pn_[0:64, :], False, False, [cbR, pn_r], [pdr])
            mm(pd[0:64, :], cb[0:2, CB_ONE:CB_ONE + 64], skS.ap(), False, True, [cbR, skPR], [pdr])
            d_, d_r = f32pool.next()
            k.I("dve", lambda e, d_=d_, pd=pd: e.reciprocal(out=d_[0:64, :], in_=pd[0:64, :]), reads=[pdr], writes=[d_r])
            for kv in range(2):
                po, por = pos[kv]
                yo = H[0:64, 4 * kv:4 * kv + 4, 1024:1088].rearrange("p h (b t) -> p b h t", t=4)
                tt(yo, po[0:64, :].rearrange("p (b h t) -> p b h t", b=16, h=8)[:, :, 4 * kv:4 * kv + 4, :],
                   d_[0:64, :].rearrange("p (b h t) -> p b h t", b=16, h=8)[:, :, 4 * kv:4 * kv + 4, :],
                   ALU.mult, [por, d_r], [YBR[4 * kv + h][2] for h in range(4)])

        if bail(5):
            return
        def gnorm(view, regs, nh, pin, ones_off, goff):
            for nt, (off, w) in enumerate(NT):
                rs, rsr = rms_rstd([(view(h, off, w), regs[h][nt]) for h in range(nh)], ones_off, pin, pin, w)
                for h in range(nh):
                    o = goff + l * nh + h
                    stt(view(h, off, w), view(h, off, w), prm[0:pin, o:o + 1], rs[0:pin, 0:w], ALU.mult, ALU.mult,
                        [regs[h][nt], rsr, prmR], [regs[h][nt]])
        gnorm(yav, YAR, 4, 64, CB_O256, P_GA)
        gnorm(ybv, YBR, 8, 64, CB_O512, P_GB)
        gnorm(ycv, YCR, 2, 128, CB_O256, P_GC)

        if bail(6):
            return
        for c in range(8):
            st, sr = wB.get(("mo", l, c))
            for nt, (off, w) in enumerate(NT):
                bd, bdr = bankD.next()
                ops = [(st[0:64, h, :], yav(h, off, w), YAR[h][nt]) for h in range(4)]
                ops += [(st[0:64, 4 + h, :], ybv(h, off, w), YBR[h][nt]) for h in range(8)]
                ops += [(st[:, 12 + cc, :], ycv(cc, off, w), YCR[cc][nt]) for cc in range(2)]
                for i, (lh, rh, rr) in enumerate(ops):
                    mm(bd[:, 0:w], lh, rh, i == 0, i == len(ops) - 1, [sr, rr], [bdr])
                act(dv(c, off, w), bd[:, 0:w], AF.Copy, [bdr], [DR[c][nt]])
        postnorm_residual(l, 3, False, NT)

    for tile in tiles:
        NT = [(0, 512), (512, 512)] + ([(1024, 64)] if tile == 1 else [])
        for c in range(8):
            k.dma("sp", X[:, c, 0:1024], xT[c * 128:(c + 1) * 128, tile * 1024:(tile + 1) * 1024], writes=[XR[c][0], XR[c][1]])
            if tile == 1:
                k.dma("sp", X[:, c, 1024:1088], xT[c * 128:(c + 1) * 128, 2048:2112], writes=[XR[c][2]])
        for kind, l, i in steps[tile]:
            if kind == "ffn":
                ffn(l, i, NT)
            else:
                mixer(l, tile, NT)
        for c in range(8):
            k.dma("sp", yT[c * 128:(c + 1) * 128, tile * 1024:(tile + 1) * 1024], X[:, c, 0:1024], reads=[XR[c][0], XR[c][1]], grp=outG, is_out=True)
            if tile == 1:
                k.dma("sp", yT[c * 128:(c + 1) * 128, 2048:2112], X[:, c, 1024:1088], reads=[XR[c][2]], grp=outG, is_out=True)
    k.finish()
    return nc, k


def _in_maps(x_prompt, x_sample, cache_swa_k, cache_swa_v, cache_conv, norm_g, w_ffn_gu, w_ffn_down,
             w_mix_in, w_mix_out, g_mix_out, w_sgu, b_sgu, g_sgu, attn_sinks, w_conv, depth=4):
    f = lambda a: np.ascontiguousarray(np.asarray(a, dtype=np.float32))
    x_prompt, x_sample = f(x_prompt), f(x_sample)
    cf, cbias = _consts()
    prm = _prm(f(norm_g), f(g_mix_out), f(w_conv), f(attn_sinks))
    shared = {"w_gu": f(w_ffn_gu[:depth]), "w_dn": f(w_ffn_down[:depth]), "w_in": f(w_mix_in[:depth]), "w_out": f(w_mix_out[:depth]),
              "w_sgu": f(w_sgu), "b_sgu": f(b_sgu), "g_sgu": f(g_sgu), "prm": prm, "cf": cf, "cbias": cbias}
    ck_all = f(cache_swa_k).reshape(4, 128, 128, 128)
    cv_all = f(cache_swa_v).reshape(4, 128, 128, 128)
    cc_all = f(cache_conv)
    maps = []
    for c in range(NCORE):
        xt = np.empty((1024, 2112), np.float32)
        xt[:, 0:2048] = x_prompt[c].T
        xt[:, 2048:2112] = x_sample[c * 16:(c + 1) * 16].reshape(64, 1024).T
        m = dict(shared)
        m["xT"] = xt
        m["ck"] = np.ascontiguousarray(ck_all[:, c * 16:(c + 1) * 16])
        m["cv"] = np.ascontiguousarray(cv_all[:, c * 16:(c + 1) * 16])
        m["cc"] = np.ascontiguousarray(cc_all[:, c * 16:(c + 1) * 16])
        maps.append(m)
    return maps


def _assemble(results):
    y_p = np.empty((8, 2048, 1024), np.float32)
    y_s = np.empty((128, 4, 1024), np.float32)
    sgu = np.empty((4, 128, 4, 4, 64), np.float32)
    kp = np.empty((4, 8, 128, 2, 64), np.float32)
    vp = np.empty((4, 8, 128, 2, 64), np.float32)
    ks = np.empty((4, 128, 128, 2, 64), np.float32)
    vs = np.empty((4, 128, 128, 2, 64), np.float32)
    cp = np.empty((4, 8, 2, 256), np.float32)
    cs = np.empty((4, 128, 2, 256), np.float32)
    for c, r in enumerate(results):
        yt = r["yT"]
        y_p[c] = yt[:, 0:2048].T
        y_s[c * 16:(c + 1) * 16] = yt[:, 2048:2112].T.reshape(16, 4, 1024)
        sgu[:, c * 16:(c + 1) * 16] = r["o_sgu"].reshape(4, 16, 4, 4, 64)
        kp[:, c] = r["o_kp"].reshape(4, 128, 2, 64)
        vp[:, c] = r["o_vp"].reshape(4, 128, 2, 64)
        ks[:, c * 16:(c + 1) * 16] = r["o_ks"].reshape(4, 16, 128, 2, 64)
        vs[:, c * 16:(c + 1) * 16] = r["o_vs"].reshape(4, 16, 128, 2, 64)
        cp[:, c] = r["o_cp"]
        cs[:, c * 16:(c + 1) * 16] = r["o_cs"].reshape(4, 16, 2, 256)
    return (y_p, y_s, sgu, kp, vp, ks, vs, cp, cs)


def kernel(**inputs):
    maps = _in_maps(**inputs)
    nc, _ = build(depth=4)
    res = run_bass_kernel_spmd(nc, maps, core_ids=list(range(NCORE)))
    return _assemble(res.results)
```

```python
import numpy as np
import concourse.bass as bass
import concourse.mybir as mybir
from concourse.bass_utils import run_bass_kernel_spmd

F32 = mybir.dt.float32
BF16 = mybir.dt.bfloat16
AF = mybir.ActivationFunctionType
ALU = mybir.AluOpType
AX = mybir.AxisListType

ENGS = ("pe", "act", "dve", "pool", "sp")
EPS = 1e-6
NEGB = -30000.0
TM = 1088
NSLOT_A = 4
RAW_ONLY = False
NCORE = 8


class Reg:
    __slots__ = ("w", "r", "name")

    def __init__(self, name=""):
        self.w = None
        self.r = []
        self.name = name


class Grp:
    __slots__ = ("sem", "n")

    def __init__(self, sem):
        self.sem = sem
        self.n = 0


class Ins:
    __slots__ = ("eng", "fn", "deps", "signal", "count", "dma", "dval", "tiny")

    def __init__(self, eng, fn, deps, dma=None):
        self.tiny = False
        self.eng = eng
        self.fn = fn
        self.deps = deps
        self.signal = False
        self.count = 0
        self.dma = dma
        self.dval = 0


class K:
    def __init__(self, nc, same_engine_sync=True):
        self.nc = nc
        self.streams = {e: [] for e in ENGS}
        self.all = []
        self.sems = {e: nc.alloc_semaphore("s_" + e) for e in ENGS}
        self.same = same_engine_sync
        self.tiny_ctx = False
        self.out_grps = []
        self._n = 0

    def sb(self, shape, dt, name=None):
        self._n += 1
        return self.nc.alloc_sbuf_tensor("sb_" + (name or f"t{self._n}"), list(shape), dt)

    def ps(self, shape, dt=F32, name=None):
        self._n += 1
        return self.nc.alloc_psum_tensor(name or f"p{self._n}", list(shape), dt)

    def grp(self):
        self._n += 1
        return Grp(self.nc.alloc_semaphore(f"g{self._n}"))

    def _deps(self, eng, reads, writes):
        deps = []
        raw = set()
        for b in reads:
            if b.w is not None:
                deps.append(b.w)
                raw.add(id(b.w))
        for b in writes:
            if b.w is not None:
                deps.append(b.w)
            deps.extend(b.r)
        out = []
        seen = set()
        for d in deps:
            if id(d) in seen:
                continue
            seen.add(id(d))
            if d.dma is None and d.eng == eng:
                if eng == "pe" or not (self.same or d.tiny) or (RAW_ONLY and id(d) not in raw):
                    continue
            out.append((d, 16 * d.dma.n if d.dma is not None else 0))
        return out

    def I(self, eng, fn, reads=(), writes=(), out=None):
        ins = Ins(eng, fn, self._deps(eng, reads, writes))
        ins.tiny = self.tiny_ctx
        if out is not None:
            n = 1
            for d_ in out.shape[1:]:
                n *= d_
            if n < 256:
                ins.tiny = True
        for b in reads:
            b.r.append(ins)
        for b in writes:
            b.w = ins
            b.r = []
        self.streams[eng].append(ins)
        self.all.append(ins)
        return ins

    def dma(self, q, out_ap, in_ap, reads=(), writes=(), grp=None, is_out=False, **kw):
        if grp is None:
            grp = self.grp()
        deps = self._deps("__dma__", reads, writes)

        def fn(e, out_ap=out_ap, in_ap=in_ap, kw=kw):
            return e.dma_start(out=out_ap, in_=in_ap, **kw)
        ins = Ins(q, fn, deps, dma=grp)
        grp.n += 1
        ins.dval = 16 * grp.n
        for b in reads:
            b.r.append(ins)
        for b in writes:
            b.w = ins
            b.r = []
        self.streams[q].append(ins)
        self.all.append(ins)
        if is_out and grp not in self.out_grps:
            self.out_grps.append(grp)
        return ins

    def finish(self):
        nc = self.nc
        for ins in self.all:
            for d, _ in ins.deps:
                if d.dma is None:
                    d.signal = True
        cnt = {e: 0 for e in ENGS}
        for ins in self.all:
            if ins.dma is None and ins.signal:
                cnt[ins.eng] += 1
                ins.count = cnt[ins.eng]
        engmap = {"pe": "tensor", "act": "scalar", "dve": "vector", "pool": "gpsimd", "sp": "sync"}
        with nc.Block() as block:
            for e in ENGS:
                stream = self.streams[e]
                final = [(g.sem, 16 * g.n) for g in self.out_grps] if e == "sp" else []

                def body(eng, stream=stream, e=e, final=final):
                    seen = {}
                    for ins in stream:
                        waits = {}
                        for d, dv_ in ins.deps:
                            if d.dma is not None:
                                s, v = d.dma.sem, dv_
                            else:
                                s, v = self.sems[d.eng], d.count
                            if seen.get(s.num, 0) >= v:
                                continue
                            if s.num not in waits or waits[s.num][1] < v:
                                waits[s.num] = (s, v)
                        for s, v in waits.values():
                            eng.wait_ge(s, v)
                            seen[s.num] = v
                        bi = ins.fn(eng)
                        if ins.dma is not None:
                            bi.then_inc(ins.dma.sem, 16)
                        elif ins.signal:
                            bi.then_inc(self.sems[e], 1)
                    for s, v in final:
                        eng.wait_ge(s, v)
                getattr(block, engmap[e])(body)
        return nc


class Rot:
    def __init__(self, items):
        self.items = items
        self.i = 0

    def next(self):
        it = self.items[self.i % len(self.items)]
        self.i += 1
        return it


class WStream:
    def __init__(self, k, seq, slots, loader):
        self.k, self.seq, self.slots, self.loader = k, seq, slots, loader
        self.pos = 0
        self.emitted = 0

    def get(self, key, held=0):
        assert self.seq[self.pos] == key, (self.seq[self.pos], key)
        n = len(self.slots)
        while self.emitted < min(len(self.seq), self.pos - held + n):
            t, r, g = self.slots[self.emitted % n]
            self.loader(self.seq[self.emitted], t, r, g)
            self.emitted += 1
        s = self.slots[self.pos % n]
        self.pos += 1
        return s[0], s[1]


C_ID, C_TRIL, C_MS, C_E = 0, 128, 256, 320
C_SEL = 384
C_TOT = 386
C_BIAS, C_BSC, C_BSN = 0, 2048, 2560
B_TOT = 3072


def _consts():
    cf = np.zeros((128, C_TOT), np.float32)
    cbias = np.zeros((128, B_TOT), np.float32)
    cf[:, C_ID:C_ID + 128] = np.eye(128, dtype=np.float32)
    s = np.arange(128)[:, None]
    t = np.arange(128)[None, :]
    cf[:, C_TRIL:C_TRIL + 128] = (s <= t).astype(np.float32)
    slopes = 2.0 ** (-8.0 * np.arange(1, 9) / 8)
    for g in range(2):
        for hp in range(4):
            sl = slopes[4 * g + hp]
            cur = np.where(s <= t, -sl * (t - s), NEGB)
            prev = np.where(s > t, -sl * (128 + t - s), NEGB)
            o = C_BIAS + g * 512 + hp * 128
            cbias[:, o:o + 128] = cur
            o = C_BIAS + 1024 + g * 512 + hp * 128
            cbias[:, o:o + 128] = prev
    bsc = np.zeros((128, 16, 8, 4), np.float32)
    ss = np.arange(128)[:, None]
    tt = np.arange(4)[None, :]
    for h in range(8):
        bsc[:, :, h, :] = np.where(ss > tt, -slopes[h] * (128 + tt - ss), NEGB)[:, None, :]
    cbias[:, C_BSC:C_BSC + 512] = bsc.reshape(128, 512)
    bsn = np.full((16, 4, 16, 8, 4), NEGB, np.float32)
    for b in range(16):
        for tp in range(4):
            for t_ in range(4):
                if tp <= t_:
                    bsn[b, tp, b, :, t_] = -slopes * (t_ - tp)
    cbias[0:64, C_BSN:C_BSN + 512] = bsn.reshape(64, 512)
    ms = np.zeros((16, 4, 16, 4), np.float32)
    for b in range(16):
        for s_ in range(4):
            for t_ in range(4):
                if s_ <= t_:
                    ms[b, s_, b, t_] = 1.0
    cf[0:64, C_MS:C_MS + 64] = ms.reshape(64, 64)
    e = np.zeros((4, 16, 4), np.float32)
    for s_ in range(4):
        e[s_, :, s_] = 1.0
    cf[0:4, C_E:C_E + 64] = e.reshape(4, 64)
    cf[0, C_SEL] = 1.0
    cf[1, C_SEL + 1] = 1.0
    return cf, cbias


P_GV, P_GA, P_GB, P_GC, P_WC, P_SK = 0, 192, 208, 240, 248, 272
P_TOT = 304


def _prm(norm_g, g_mix_out, w_conv, attn_sinks):
    p = np.zeros((128, P_TOT), np.float32)
    p[:, P_GV:P_GV + 192] = norm_g.reshape(4, 6, 8, 128).transpose(3, 0, 1, 2).reshape(128, 192)
    p[0:64, P_GA:P_GA + 16] = g_mix_out[:, 0:256].reshape(4, 4, 64).transpose(2, 0, 1).reshape(64, 16)
    p[0:64, P_GB:P_GB + 32] = g_mix_out[:, 256:768].reshape(4, 8, 64).transpose(2, 0, 1).reshape(64, 32)
    p[:, P_GC:P_GC + 8] = g_mix_out[:, 768:1024].reshape(4, 2, 128).transpose(2, 0, 1).reshape(128, 8)
    p[:, P_WC:P_WC + 24] = w_conv.reshape(4, 3, 2, 128).transpose(3, 0, 1, 2).reshape(128, 24)
    p[:, P_SK:P_SK + 32] = np.broadcast_to(attn_sinks.reshape(1, 32), (128, 32))
    return p


def build(depth=4, tiles=(0, 1), same_sync=True, nsteps=None, mix_stop=99):
    nc = bass.Bass("TRN2", target_bir_lowering=False)
    k = K(nc, same_engine_sync=same_sync)

    def din(name, shape):
        return nc.dram_tensor(name, list(shape), F32, kind="ExternalInput").ap()

    def dout(name, shape):
        return nc.dram_tensor(name, list(shape), F32, kind="ExternalOutput").ap()

    xT = din("xT", [1024, 2112])
    ck = din("ck", [4, 16, 128, 128])
    cv = din("cv", [4, 16, 128, 128])
    cc_in = din("cc", [4, 16, 2, 256])
    w_gu = din("w_gu", [depth, 2, 1024, 5632])
    w_dn = din("w_dn", [depth, 2, 2816, 1024])
    w_in = din("w_in", [depth, 1024, 2048])
    w_out = din("w_out", [depth, 1024, 1024])
    w_sgu = din("w_sgu", [4, 4, 128, 128])
    b_sgu = din("b_sgu", [4, 4, 128])
    g_sgu = din("g_sgu", [4, 256])
    prm_d = din("prm", [128, P_TOT])
    cf_d = din("cf", [128, C_TOT])
    cbias_d = din("cbias", [128, B_TOT])

    yT = dout("yT", [1024, 2112])
    o_sgu = dout("o_sgu", [4, 64, 256])
    o_kp = dout("o_kp", [4, 128, 128])
    o_vp = dout("o_vp", [4, 128, 128])
    o_ks = dout("o_ks", [4, 16, 128, 128])
    o_vs = dout("o_vs", [4, 16, 128, 128])
    o_cp = dout("o_cp", [4, 2, 256])
    o_cs = dout("o_cs", [4, 32, 256])

    X = k.sb([128, 8, TM], F32, "X")
    H = k.sb([128, 8, TM], BF16, "H")
    ACTt = k.sb([128, 22 * TM], BF16, "ACT")
    Dt = k.sb([128, 8720], F32, "D")
    XR = [[Reg() for _ in range(3)] for _ in range(8)]
    HR = [[Reg() for _ in range(3)] for _ in range(8)]
    AR = [[Reg() for _ in range(3)] for _ in range(22)]
    DR = [[Reg() for _ in range(3)] for _ in range(8)]

    def actv(f, off, w):
        return ACTt[:, f * TM + off: f * TM + off + w]

    def dv(c, off, w):
        return Dt[:, c * TM + off: c * TM + off + w]

    ZW = TM + 2

    def zv(cc, a, b):
        return Dt[:, cc * ZW + a: cc * ZW + b]
    Dbf = Dt[:, 2 * ZW: 8720].bitcast(BF16)

    def uv(h, off, w):
        return Dbf[0:64, h * TM + off: h * TM + off + w]

    def qv(h, off, w):
        return Abf[0:64, 4352 + h * TM + off: 4352 + h * TM + off + w]

    def kdv(kv, off, w):
        return Dbf[0:64, 8704 + kv * TM + off: 8704 + kv * TM + off + w]

    def gbv(cc, off, w):
        return Dbf[:, 10880 + cc * TM + off: 10880 + cc * TM + off + w]
    Abf = ACTt.ap()

    def yav(h, off, w):
        return Abf[0:64, h * TM + off: h * TM + off + w]

    Hflat = H.ap().rearrange("p c t -> p (c t)")

    def ybv(h, off, w):
        return Hflat[0:64, h * TM + off: h * TM + off + w]

    def ycv(cc, off, w):
        return Abf[:, 13056 + cc * TM + off: 13056 + cc * TM + off + w]

    def vta(blk, p0, p1, a, b):
        return Abf[p0:p1, 15232 + blk * 256 + a: 15232 + blk * 256 + b]

    def vtb(blk, p0, p1, a, b):
        return Abf[p0:p1, 17536 + blk * 128 + a: 17536 + blk * 128 + b]

    def ktc(b, kv):
        return Abf[0:64, 18688 + (b * 2 + kv) * 128: 18688 + (b * 2 + kv + 1) * 128]
    qs_ap = Abf[0:64, 22784: 22784 + 512]
    ks_ap = Abf[0:64, 23296: 23296 + 128]

    UR = [[Reg() for _ in range(3)] for _ in range(4)]
    QR = [[Reg() for _ in range(3)] for _ in range(8)]
    QU = [[Reg() for _ in range(3)] for _ in range(4)]
    UU = [[Reg() for _ in range(3)] for _ in range(2)]
    YSA = [[Reg() for _ in range(3)] for _ in range(2)]
    YSB = [[Reg() for _ in range(3)] for _ in range(4)]
    KDR = [[Reg() for _ in range(3)] for _ in range(2)]
    GBR = [[Reg() for _ in range(3)] for _ in range(2)]
    ZR = [[Reg() for _ in range(3)] for _ in range(2)]
    ZHR = [Reg() for _ in range(2)]
    YAR = [[Reg() for _ in range(3)] for _ in range(4)]
    YBR = HR
    YCR = [[Reg() for _ in range(3)] for _ in range(2)]
    VAR = [Reg() for _ in range(9)]
    VBR = [Reg() for _ in range(9)]
    KTR = Reg()
    QSR = Reg()
    KSR = Reg()

    cf = k.sb([128, C_TOT], F32, "cf")
    cfR = Reg()
    prm = k.sb([128, P_TOT], F32, "prm")
    prmR = Reg()
    cb = k.sb([128, 128 * 5 + 64], BF16, "cb")
    cbR = Reg()
    CB_ID, CB_O1024, CB_O256, CB_O512, CB_ONE, CB_E = 0, 128, 256, 384, 512, 640
    gh = k.sb([128, 4, 2, 8], F32, "gh")
    ghR = Reg()

    def gvec(l, i, c):
        o = P_GV + (l * 6 + i) * 8 + c
        return prm[:, o:o + 1]

    k.dma("sp", cf.ap(), cf_d, writes=[cfR])
    cbt = k.sb([128, B_TOT], BF16, "cbt")
    cbtR = Reg()
    k.dma("pool", cbt.ap(), cbias_d, writes=[cbtR])
    k.dma("sp", prm.ap(), prm_d, writes=[prmR])
    k.I("dve", lambda e: e.tensor_copy(out=cb[:, CB_ID:CB_ID + 128], in_=cf[:, C_ID:C_ID + 128]), reads=[cfR], writes=[cbR])
    for o, v in ((CB_O1024, 1.0 / 1024), (CB_O256, 1.0 / 256), (CB_O512, 1.0 / 512), (CB_ONE, 1.0)):
        k.I("dve", lambda e, o=o, v=v: e.memset(cb[:, o:o + 128], v), writes=[cbR])
    k.I("dve", lambda e: e.tensor_copy(out=cb[0:4, CB_E:CB_E + 64], in_=cf[0:4, C_E:C_E + 64]), reads=[cfR], writes=[cbR])
    for j, i in enumerate((1, 5)):
        for l in range(4):
            o = P_GV + (l * 6 + i) * 8
            k.I("dve", lambda e, l=l, j=j, o=o: e.tensor_single_scalar(out=gh[:, l, j, :], in_=prm[:, o:o + 8], scalar=0.5, op=ALU.mult),
                reads=[prmR], writes=[ghR])

    epsc = k.sb([128, 1], F32, "epsc")
    k.I("dve", lambda e: e.memset(epsc.ap(), EPS), writes=[cbR])
    f32pool = Rot([(k.sb([128, 512], F32), Reg()) for _ in range(3)])
    bfpool = Rot([(k.sb([128, 512], BF16), Reg()) for _ in range(4)])
    rspool = Rot([(k.sb([128, 512], F32), Reg()) for _ in range(2)])
    small = k.sb([128, 64], F32, "small")
    smallR = Reg()
    small2 = [small[:, 0:32], small[:, 32:64]]
    small2R = [Reg(), Reg()]
    stg = Rot([(k.sb([128, 256], F32), Reg(), k.grp()) for _ in range(2)])
    vgpool = Rot(f32pool.items[0:2])
    kprev = k.sb([64, 4, 2, 128], BF16, "kprev")
    vprev = k.sb([128, 4, 128], BF16, "vprev")
    zprev = k.sb([128, 4, 2, 2], F32, "zprev")
    KPR = [Reg() for _ in range(4)]
    VPR = [Reg() for _ in range(4)]
    ZPR = [Reg() for _ in range(4)]
    gsg = k.sb([128, 256], F32, "gsg")
    gsgR = Reg()
    gsgG = k.grp()
    bsgG = k.grp()
    bsg = k.sb([1, 4, 128], BF16, "bsg")
    bsgR = Reg()
    bss = k.sb([1, 4, 64], BF16, "bss")
    bssR = Reg()
    wsg32 = k.sb([128, 4, 128], BF16, "wsgb")
    wsg32R = Reg()
    wsgG = k.grp()
    WT = k.sb([128, 4, 128], BF16, "WT")
    WTR = Reg()
    Xs = k.sb([4, 4, 64], BF16, "Xs")
    XsR = Reg()
    Wblk = k.sb([64, 4, 64], BF16, "Wblk")
    WblkR = Reg()
    sexp = k.sb([64, 8], F32, "sexp")
    sexpR = Reg()
    sk_hb = k.sb([2, 8], BF16, "sk_hb")
    sk_w = k.sb([2, 24], F32, "sk_w")
    skR = Reg()
    skP = k.sb([2, 8, 128], BF16, "skP")
    skS = k.sb([2, 512], BF16, "skS")
    skPR = Reg()
    kc = k.sb([128, 16, 128], BF16, "kc")
    vc = k.sb([128, 16, 128], BF16, "vc")
    kcR, vcR = Reg(), Reg()
    kcG, vcG = k.grp(), k.grp()
    zs = k.sb([128, 2, 16, 6], F32, "zs")
    zsR = Reg()
    cc32 = k.sb([32, 256], F32, "cc32")
    cc32R = Reg()
    ccG = k.grp()

    pb = [(k.ps([128, 512], F32), Reg()) for _ in range(8)]
    qG = [k.grp() for _ in range(4)]
    uG = [k.grp() for _ in range(2)]
    ysG = [k.grp() for _ in range(6)]
    outG = k.grp()
    miscG = k.grp()

    seqA, seqB = [], []
    MIN_ORDER = [1, 4, 2, 3, 0, 5, 6, 7]
    steps = {}
    for tile in tiles:
        st_ = []
        for l in range(depth):
            st_ += [("ffn", l, 0), ("mix", l, 0), ("ffn", l, 1)]
        if nsteps is not None:
            st_ = st_[:nsteps]
        steps[tile] = st_
        for kind, l, i in st_:
            if kind == "ffn":
                seqA += [("gu", l, i, f) for f in range(22)]
                seqB += [("dn", l, i, c) for c in range(8)]
            else:
                seqA += [("min", l, j) for j in MIN_ORDER]
                seqB += [("mo", l, c) for c in range(8)]

    def loadA(key, t, r, g):
        if key[0] == "gu":
            _, l, i, f = key
            src = w_gu[l, i].rearrange("(kc p) f -> p kc f", p=128)
            k.dma("pool", t[:, :, 0:128], src[:, :, f * 128:(f + 1) * 128], writes=[r], grp=g)
            k.dma("pool", t[:, :, 128:256], src[:, :, 2816 + f * 128:2816 + (f + 1) * 128], writes=[r], grp=g)
        else:
            _, l, j = key
            src = w_in[l].rearrange("(kc p) f -> p kc f", p=128)[:, :, j * 256:(j + 1) * 256]
            k.dma("pool", t.ap(), src, writes=[r], grp=g)

    def loadB(key, t, r, g):
        if key[0] == "dn":
            _, l, i, c = key
            src = w_dn[l, i].rearrange("(fc p) d -> p fc d", p=128)[:, :, c * 128:(c + 1) * 128]
            k.dma("pool", t.ap(), src, writes=[r], grp=g)
        else:
            _, l, c = key
            src = w_out[l, :, c * 128:(c + 1) * 128].rearrange("(kc p) f -> p kc f", p=128)
            k.dma("pool", t[:, 0:8, :], src, writes=[r], grp=g)

    wA = WStream(k, seqA, [(k.sb([128, 8, 256], BF16), Reg(), k.grp()) for _ in range(NSLOT_A)], loadA)
    wB = WStream(k, seqB, [(k.sb([128, 22, 128], BF16), Reg(), k.grp()) for _ in range(2)], loadB)

    def mm(out, lhsT, rhs, start, stop, reads, writes):
        k.I("pe", lambda e: e.matmul(out, lhsT=lhsT, rhs=rhs, start=start, stop=stop), reads=reads, writes=writes)

    def act(out, in_, func, reads, writes, **kw):
        k.I("act", lambda e: e.activation(out=out, in_=in_, func=func, **kw), reads=reads, writes=writes, out=out)

    def tt(out, in0, in1, op, reads, writes):
        k.I("dve", lambda e: e.tensor_tensor(out=out, in0=in0, in1=in1, op=op), reads=reads, writes=writes, out=out)

    def stt(out, in0, scalar, in1, op0, op1, reads, writes):
        k.I("dve", lambda e: e.scalar_tensor_tensor(out=out, in0=in0, scalar=scalar, in1=in1, op0=op0, op1=op1),
            reads=reads, writes=writes, out=out)

    def ts(out, in0, s1, s2, op0, op1, reads, writes):
        if s2 is None:
            k.I("dve", lambda e: e.tensor_single_scalar(out=out, in_=in0, scalar=s1, op=op0), reads=reads, writes=writes, out=out)
        else:
            k.I("dve", lambda e: e.tensor_scalar(out=out, in0=in0, scalar1=s1, scalar2=s2, op0=op0, op1=op1),
                reads=reads, writes=writes, out=out)

    def rsqrt_eps(out, in_, reads, writes):
        act(out, in_, AF.Sqrt, reads, writes, bias=epsc[0:out.shape[0], 0:1])
        k.I("dve", lambda e: e.reciprocal(out=out, in_=out), reads=writes, writes=writes, out=out)

    def rms_rstd(chunks, ones_off, pin, pout, w):
        pst, psr = pb[6]
        n = len(chunks)
        for i, (ap, rg) in enumerate(chunks):
            sq, sqr = bfpool.next()
            act(sq[0:pin, 0:w], ap, AF.Square, [rg], [sqr])
            mm(pst[0:pout, 0:w], cb[0:pin, ones_off:ones_off + pout], sq[0:pin, 0:w], i == 0, i == n - 1, [cbR, sqr], [psr])
        rs, rsr = rspool.next()
        rsqrt_eps(rs[0:pout, 0:w], pst[0:pout, 0:w], [psr], [rsr])
        return rs, rsr

    def prenorm(l, gi, NT):
        for nt, (off, w) in enumerate(NT):
            rs, rsr = rms_rstd([(X[:, c, off:off + w], XR[c][nt]) for c in range(8)], CB_O1024, 128, 128, w)
            for c in range(8):
                stt(H[:, c, off:off + w], X[:, c, off:off + w], gvec(l, gi, c), rs[:, 0:w], ALU.mult, ALU.mult,
                    [XR[c][nt], rsr, prmR], [HR[c][nt]])

    def postnorm_residual(l, gi, half, NT):
        for nt, (off, w) in enumerate(NT):
            rs, rsr = rms_rstd([(dv(c, off, w), DR[c][nt]) for c in range(8)], CB_O1024, 128, 128, w)
            for c in range(8):
                if half:
                    g = gh[:, l, 0 if gi == 1 else 1, c:c + 1]
                    gr = ghR
                else:
                    g = gvec(l, gi, c)
                    gr = prmR
                stt(dv(c, off, w), dv(c, off, w), g, rs[:, 0:w], ALU.mult, ALU.mult, [DR[c][nt], rsr, gr], [DR[c][nt]])
            for c in range(8):
                tt(X[:, c, off:off + w], X[:, c, off:off + w], dv(c, off, w), ALU.add, [XR[c][nt], DR[c][nt]], [XR[c][nt]])

    bankGU = Rot(pb[0:4])
    bankD = Rot(pb[4:6])

    def ffn(l, i, NT):
        prenorm(l, 0 if i == 0 else 4, NT)
        for f in range(22):
            st, sr = wA.get(("gu", l, i, f))
            for nt, (off, w) in enumerate(NT):
                bg, bgr = bankGU.next()
                bu, bur = bankGU.next()
                for kc in range(8):
                    mm(bg[:, 0:w], st[:, kc, 0:128], H[:, kc, off:off + w], kc == 0, kc == 7, [sr, HR[kc][nt]], [bgr])
                for kc in range(8):
                    mm(bu[:, 0:w], st[:, kc, 128:256], H[:, kc, off:off + w], kc == 0, kc == 7, [sr, HR[kc][nt]], [bur])
                sg, sgr = f32pool.next()
                act(sg[:, 0:w], bg[:, 0:w], AF.Silu, [bgr], [sgr])
                tt(actv(f, off, w), bu[:, 0:w], sg[:, 0:w], ALU.mult, [bur, sgr], [AR[f][nt]])
        for c in range(8):
            st, sr = wB.get(("dn", l, i, c))
            for nt, (off, w) in enumerate(NT):
                bd, bdr = bankD.next()
                for f in range(22):
                    mm(bd[:, 0:w], st[:, f, :], actv(f, off, w), f == 0, f == 21, [sr, AR[f][nt]], [bdr])
                act(dv(c, off, w), bd[:, 0:w], AF.Copy, [bdr], [DR[c][nt]])
        postnorm_residual(l, 1 if i == 0 else 5, True, NT)

    def layer_params(l):
        k.tiny_ctx = True
        _layer_params(l)
        k.tiny_ctx = False

    def _layer_params(l):
        k.dma("sp", gsg.ap(), g_sgu[l].partition_broadcast(128), writes=[gsgR], grp=gsgG)
        k.dma("pool", bsg.ap(), b_sgu[l:l + 1], writes=[bsgR], grp=bsgG)
        k.dma("pool", wsg32.ap(), w_sgu[l].rearrange("h t s -> t h s"), writes=[wsg32R], grp=wsgG)
        k.I("dve", lambda e: e.tensor_copy(out=bss.ap().rearrange("o h (b t) -> o h b t", t=4),
                                          in_=bsg[:, :, 0:4].unsqueeze(2).to_broadcast([1, 4, 16, 4])),
            reads=[bsgR], writes=[bssR])
        for h in range(4):
            pt, pr = pb[7]
            ptb_ = pt.ap().bitcast(BF16)
            k.I("pe", lambda e, h=h, ptb_=ptb_: e.transpose(ptb_[:, 0:128], wsg32[:, h, :], cb[:, CB_ID:CB_ID + 128]),
                reads=[wsg32R, cbR], writes=[pr])
            tt(WT[:, h, :], ptb_[:, 0:128], cf[:, C_TRIL:C_TRIL + 128], ALU.mult, [pr, cfR], [WTR])
        k.I("dve", lambda e: e.tensor_copy(out=Xs.ap().rearrange("s h (b t) -> s h b t", t=4),
                                          in_=WT[0:4, :, 0:4].unsqueeze(2).to_broadcast([4, 4, 16, 4])),
            reads=[WTR], writes=[XsR])
        for h in range(4):
            pt, pr = pb[7]
            mm(pt[0:64, 0:64], cb[0:4, CB_E:CB_E + 64], Xs[:, h, :], True, True, [cbR, XsR], [pr])
            tt(Wblk[:, h, :], pt[0:64, 0:64], cf[0:64, C_MS:C_MS + 64], ALU.mult, [pr, cfR], [WblkR])
        o = P_SK + l * 8
        act(sexp.ap(), prm[0:64, o:o + 8], AF.Exp, [prmR], [sexpR])
        k.I("dve", lambda e: e.tensor_copy(out=sk_hb.ap(), in_=sexp[0:2, :]), reads=[sexpR], writes=[skR])
        k.I("dve", lambda e: e.tensor_copy(out=sk_w[:, 0:8], in_=sk_hb.ap()), reads=[skR], writes=[skR])
        tt(sk_w[:, 8:16], sexp[0:2, :], sk_w[:, 0:8], ALU.subtract, [sexpR, skR], [skR])
        ts(sk_w[:, 0:8], sk_w[:, 0:8], cf[0:2, C_SEL:C_SEL + 1], None, ALU.mult, None, [skR, cfR], [skR])
        stt(sk_w[:, 16:24], sk_w[:, 8:16], cf[0:2, C_SEL + 1:C_SEL + 2], sk_w[:, 0:8], ALU.mult, ALU.add, [skR, cfR], [skR])
        k.I("dve", lambda e: e.tensor_copy(out=skP.ap(), in_=sk_w[:, 16:24].unsqueeze(2).to_broadcast([2, 8, 128])),
            reads=[skR], writes=[skPR])
        k.I("dve", lambda e: e.tensor_copy(out=skS.ap().rearrange("p (b h t) -> p b h t", b=16, h=8),
                                          in_=sk_w[:, 16:24].unsqueeze(1).unsqueeze(3).to_broadcast([2, 16, 8, 4])),
            reads=[skR], writes=[skPR])

    bankM = Rot(pb[0:6])

    def mixer(l, tile, NT):
        B = (tile == 1)
        nblk = 8
        has_s = B and len(NT) == 3
        layer_params(l)
        if has_s:
            k.dma("pool", kc.ap(), ck[l].rearrange("b s c -> s b c"), writes=[kcR], grp=kcG)
            k.dma("pool", vc.ap(), cv[l].rearrange("b s c -> s b c"), writes=[vcR], grp=vcG)
            k.dma("sp", cc32.ap(), cc_in[l].rearrange("b j c -> (b j) c"), writes=[cc32R], grp=ccG)
            k.dma("sp", o_ks[l, :, 0:124, :], ck[l, :, 4:128, :], grp=miscG, is_out=True)
            k.dma("sp", o_vs[l, :, 0:124, :], cv[l, :, 4:128, :], grp=miscG, is_out=True)
        prenorm(l, 2, NT)

        def ntof(blk):
            return blk // 4 if blk < 8 else 2

        def bail(stage):
            if mix_stop <= stage:
                while wA.pos < len(wA.seq) and wA.seq[wA.pos][0] == "min" and wA.seq[wA.pos][1] == l:
                    wA.pos += 1
                while wA.emitted < wA.pos:
                    wA.emitted += 1
                while wB.pos < len(wB.seq) and wB.seq[wB.pos][0] == "mo" and wB.seq[wB.pos][1] == l:
                    wB.pos += 1
                while wB.emitted < wB.pos:
                    wB.emitted += 1
                return True
            return False
        if bail(0):
            return
        sva, svar = wA.get(("min", l, 1))
        skv, skvr = wA.get(("min", l, 4), held=1)
        blocks = list(range(nblk)) + ([8] if has_s else [])
        lnq = []

        def ln_stats(blk, P, vg, vgr):
            par = blk % 2
            sm = small2[par]
            smR = small2R[par]
            v3 = vg[0:P, 0:256].rearrange("p (h d) -> p h d", d=64)
            s1, s2, mean, msq = sm[0:P, 0:4], sm[0:P, 4:8], sm[0:P, 8:12], sm[0:P, 12:16]
            var, rst, nb = sm[0:P, 16:20], sm[0:P, 20:24], sm[0:P, 24:28]
            k.tiny_ctx = True
            k.I("dve", lambda e: e.reduce_sum(out=s1, in_=v3, axis=AX.X), reads=[vgr], writes=[smR])
            t2, t2r = f32pool.items[2]
            act(t2[0:P, 0:256], vg[0:P, 0:256], AF.Square, [vgr], [t2r])
            k.I("dve", lambda e: e.reduce_sum(out=s2, in_=t2[0:P, 0:256].rearrange("p (h d) -> p h d", d=64), axis=AX.X),
                reads=[t2r], writes=[smR])
            k.I("dve", lambda e: e.tensor_single_scalar(out=mean, in_=s1, scalar=1.0 / 64, op=ALU.mult), reads=[smR], writes=[smR])
            tt(msq, mean, mean, ALU.mult, [smR], [smR])
            stt(var, s2, 1.0 / 64, msq, ALU.mult, ALU.subtract, [smR], [smR])
            k.tiny_ctx = False

            def finish():
                k.tiny_ctx = True
                rsqrt_eps(rst, var, [smR], [smR])
                stt(nb, mean, -1.0, rst, ALU.mult, ALU.mult, [smR], [smR])
                k.tiny_ctx = False
                for h in range(4):
                    act(vg[0:P, h * 64:(h + 1) * 64], vg[0:P, h * 64:(h + 1) * 64], AF.Identity, [vgr, smR], [vgr],
                        scale=rst[:, h:h + 1], bias=nb[:, h:h + 1])
                if blk == 8:
                    sg_, sgr_, sgg = stg.next()
                    tt(sg_[0:64, 0:256], vg[0:64, 0:256], gsg[0:64, :], ALU.mult, [vgr, gsgR], [sgr_])
                    k.dma("sp", o_sgu[l], sg_[0:64, 0:256], reads=[sgr_], grp=sgg, is_out=True)
                    k.I("dve", lambda e: e.tensor_copy(out=vta(8, 0, 64, 0, 256), in_=sg_[0:64, 0:256]), reads=[sgr_], writes=[VAR[8]])
                else:
                    tt(vta(blk, 0, P, 0, 256), vg[0:P, 0:256], gsg[0:P, :], ALU.mult, [vgr, gsgR], [VAR[blk]])
            return finish

        for blk in blocks:
            P = 128 if blk < 8 else 64
            t0 = blk * 128
            nt = ntof(blk)
            pt, pr = bankM.next()
            for kc_ in range(8):
                mm(pt[0:P, 0:256], H[:, kc_, t0:t0 + P], sva[:, kc_, :], kc_ == 0, kc_ == 7, [svar, HR[kc_][nt]], [pr])
            for kc_ in range(8):
                mm(pt[0:P, 256:512], H[:, kc_, t0:t0 + P], skv[:, kc_, :], kc_ == 0, kc_ == 7, [skvr, HR[kc_][nt]], [pr])
            vg, vgr = vgpool.next()
            act(vg[0:P, 0:256], pt[0:P, 0:256], AF.Gelu, [pr], [vgr])
            act(vtb(blk, 0, P, 0, 128), pt[0:P, 384:512], AF.Copy, [pr], [VBR[blk]])
            if blk == 8 or (B and blk == 7):
                sg_, sgr_, sgg = stg.next()
                act(sg_[0:P, 0:256], pt[0:P, 256:512], AF.Copy, [pr], [sgr_])
                if blk == 7:
                    k.dma("sp", o_kp[l], sg_[:, 0:128], reads=[sgr_], grp=sgg, is_out=True)
                    k.dma("sp", o_vp[l], sg_[:, 128:256], reads=[sgr_], grp=sgg, is_out=True)
                else:
                    for b_ in range(16):
                        k.dma("sp", o_ks[l, b_, 124:128, :], sg_[4 * b_:4 * b_ + 4, 0:128], reads=[sgr_], grp=sgg, is_out=True)
                        k.dma("sp", o_vs[l, b_, 124:128, :], sg_[4 * b_:4 * b_ + 4, 128:256], reads=[sgr_], grp=sgg, is_out=True)
            pend = lnq.pop(0) if lnq else None
            lnq.append(ln_stats(blk, P, vg, vgr))
            if pend is not None:
                pend()
        while lnq:
            lnq.pop(0)()

        if bail(1):
            return
        for kv in range(2):
            for nt, (off, w) in enumerate(NT):
                pt, pr = bankM.next()
                if nt < 2:
                    for kc_ in range(8):
                        mm(pt[0:64, 0:w], skv[:, kc_, kv * 64:(kv + 1) * 64], H[:, kc_, off:off + w], kc_ == 0, kc_ == 7, [skvr, HR[kc_][nt]], [pr])
                    act(kdv(kv, off, w), pt[0:64, 0:w], AF.Copy, [pr], [KDR[kv][nt]])
                else:
                    for kc_ in range(8):
                        mm(pt[0:64, 0:64], skv[:, kc_, kv * 64:(kv + 1) * 64], H[:, kc_, off:off + 64], kc_ == 0, kc_ == 7, [skvr, HR[kc_][nt]], [pr])
                    act(ks_ap[:, kv * 64:(kv + 1) * 64], pt[0:64, 0:64], AF.Copy, [pr], [KSR])

        for jq in range(2):
            st, sr = wA.get(("min", l, 2 + jq))
            for ci in range(2):
                he, ho = jq * 4 + 2 * ci, jq * 4 + 2 * ci + 1
                for nt, (off, w) in enumerate(NT):
                    if nt == 2:
                        continue
                    pt, pr = bankM.next()
                    for kc_ in range(8):
                        mm(pt[:, 0:w], st[:, kc_, ci * 128:(ci + 1) * 128], H[:, kc_, off:off + w], kc_ == 0, kc_ == 7, [sr, HR[kc_][nt]], [pr])
                    act(Abf[0:128, 4352 + he * TM + off: 4352 + he * TM + off + w], pt[:, 0:w], AF.Copy, [pr],
                        [QR[he][nt], QU[he // 2][nt]], scale=0.125)
                k.dma("sp", Abf[0:64, 4352 + ho * TM: 4352 + ho * TM + 1024], Abf[64:128, 4352 + he * TM: 4352 + he * TM + 1024],
                      reads=[QU[he // 2][0], QU[he // 2][1]], writes=[QR[ho][0], QR[ho][1]], grp=qG[he // 2])
            if has_s:
                for hh in range(4):
                    h = jq * 4 + hh
                    pt, pr = bankM.next()
                    for kc_ in range(8):
                        mm(pt[0:64, 0:64], st[:, kc_, hh * 64:(hh + 1) * 64], H[:, kc_, 1024:1088], kc_ == 0, kc_ == 7, [sr, HR[kc_][2]], [pr])
                    act(qs_ap.rearrange("p (b h t) -> p b h t", b=16, h=8)[:, :, h, :], pt[0:64, 0:64].rearrange("p (b t) -> p b t", t=4),
                        AF.Copy, [pr], [QSR], scale=0.125)
        units = []
        bankU = Rot([pb[6], pb[7]])
        SL = {}

        def ua_pair(ci):
            if 'ua' not in SL:
                SL['ua'] = wA.get(("min", l, 0))
            st, sr = SL['ua']
            he, ho = 2 * ci, 2 * ci + 1
            for nt, (off, w) in enumerate(NT):
                pt, pr = bankM.next()
                for kc_ in range(8):
                    mm(pt[:, 0:w], st[:, kc_, ci * 128:(ci + 1) * 128], H[:, kc_, off:off + w], kc_ == 0, kc_ == 7, [sr, HR[kc_][nt]], [pr])
                act(Dbf[0:128, he * TM + off: he * TM + off + w], pt[:, 0:w], AF.Gelu, [pr], [UR[he][nt], UU[ci][nt]])
            tw = NT[-1][0] + NT[-1][1]
            k.dma("sp", Dbf[0:64, ho * TM: ho * TM + tw], Dbf[64:128, he * TM: he * TM + tw],
                  reads=[UU[ci][nt] for nt in range(len(NT))], writes=[UR[ho][nt] for nt in range(len(NT))], grp=uG[ci])
        for ci in range(2):
            ua_pair(ci)

        def gb_unit(cc, nt, off, w):
            if 'gb' not in SL:
                SL['gb'] = wA.get(("min", l, 5))
            st, sr = SL['gb']
            pt, pr = bankM.next()
            for kc_ in range(8):
                mm(pt[:, 0:w], st[:, kc_, cc * 128:(cc + 1) * 128], H[:, kc_, off:off + w], kc_ == 0, kc_ == 7, [sr, HR[kc_][nt]], [pr])
            act(gbv(cc, off, w), pt[:, 0:w], AF.Copy, [pr], [GBR[cc][nt]])
        for cc in range(2):
            for nt, (off, w) in enumerate(NT):
                gb_unit(cc, nt, off, w)

        def z_setup():
            SL['gc'] = wA.get(("min", l, 6))
            SL['hc'] = wA.get(("min", l, 7), held=1)
            if not B:
                for cc in range(2):
                    k.I("dve", lambda e, cc=cc: e.memset(zv(cc, 0, 2), 0.0), writes=[ZHR[cc]])
            else:
                for cc in range(2):
                    k.I("dve", lambda e, cc=cc: e.tensor_copy(out=zv(cc, 0, 2), in_=zprev[:, l, cc, :]), reads=[ZPR[l]], writes=[ZHR[cc]])
            if has_s:
                for cc in range(2):
                    pt, pr = pb[7]
                    k.I("pe", lambda e, cc=cc, pt=pt: e.transpose(pt[:, 0:32], cc32[:, cc * 128:(cc + 1) * 128], cf[0:32, C_ID:C_ID + 32]),
                        reads=[cc32R, cfR], writes=[pr])
                    act(zs[:, cc, :, 0:2], pt[:, 0:32].rearrange("p (b j) -> p b j", j=2), AF.Copy, [pr], [zsR])
        z_setup()

        def z_unit(cc, nt, off, w):
            sgc, sgcr = SL['gc']
            shc, shcr = SL['hc']
            pg, pgr = bankM.next()
            ph, phr = bankM.next()
            for kc_ in range(8):
                mm(pg[:, 0:w], sgc[:, kc_, cc * 128:(cc + 1) * 128], H[:, kc_, off:off + w], kc_ == 0, kc_ == 7, [sgcr, HR[kc_][nt]], [pgr])
            for kc_ in range(8):
                mm(ph[:, 0:w], shc[:, kc_, cc * 128:(cc + 1) * 128], H[:, kc_, off:off + w], kc_ == 0, kc_ == 7, [shcr, HR[kc_][nt]], [phr])
            gt, gtr = f32pool.next()
            act(gt[:, 0:w], pg[:, 0:w], AF.Copy, [pgr], [gtr])
            if nt < 2:
                tt(zv(cc, 2 + off, 2 + off + w), ph[:, 0:w], gt[:, 0:w], ALU.mult, [phr, gtr], [ZR[cc][nt]])
            else:
                tt(zs[:, cc, :, 2:6], ph[:, 0:64].rearrange("p (b t) -> p b t", t=4),
                   gt[:, 0:64].rearrange("p (b t) -> p b t", t=4), ALU.mult, [phr, gtr, zsR], [zsR])
        for cc in range(2):
            for nt, (off, w) in enumerate(NT):
                z_unit(cc, nt, off, w)

        def z_tail():
            if not B:
                for cc in range(2):
                    k.I("dve", lambda e, cc=cc: e.tensor_copy(out=zprev[:, l, cc, :], in_=zv(cc, 1024, 1026)), reads=[ZR[cc][1]], writes=[ZPR[l]])
            else:
                sg_, sgr_, sgg = stg.next()
                for cc in range(2):
                    pt, pr = pb[7]
                    k.I("pe", lambda e, cc=cc, pt=pt: e.transpose(pt[0:2, 0:128], zv(cc, 1024, 1026), cf[:, C_ID:C_ID + 128]),
                        reads=[ZR[cc][1], cfR], writes=[pr])
                    act(sg_[0:2, cc * 128:(cc + 1) * 128], pt[0:2, 0:128], AF.Copy, [pr], [sgr_])
                k.dma("sp", o_cp[l], sg_[0:2, 0:256], reads=[sgr_], grp=sgg, is_out=True)
                if has_s:
                    sg_, sgr_, sgg = stg.next()
                    for cc in range(2):
                        pt, pr = pb[7]
                        zt, ztr = f32pool.next()
                        act(zt[:, 0:32].rearrange("p (b j) -> p b j", j=2), zs[:, cc, :, 4:6], AF.Copy, [zsR], [ztr])
                        k.I("pe", lambda e, cc=cc, pt=pt, zt=zt: e.transpose(pt[0:32, 0:128], zt[:, 0:32], cf[:, C_ID:C_ID + 128]),
                            reads=[ztr, cfR], writes=[pr])
                        act(sg_[0:32, cc * 128:(cc + 1) * 128], pt[0:32, 0:128], AF.Copy, [pr], [sgr_])
                    k.dma("sp", o_cs[l], sg_[0:32, 0:256], reads=[sgr_], grp=sgg, is_out=True)

        units.append(z_tail)

        def wcv(j, cc):
            o = P_WC + (l * 3 + j) * 2 + cc
            return prm[:, o:o + 1]
        def conv_unit(cc, nt, off, w):
            a, ar = f32pool.next()
            if nt < 2:
                zr = [ZR[cc][nt], ZHR[cc]] + ([ZR[cc][0]] if nt == 1 else [])
                ts(a[:, 0:w], zv(cc, 2 + off, 2 + off + w), wcv(2, cc), None, ALU.mult, ALU.bypass, zr + [prmR], [ar])
                stt(a[:, 0:w], zv(cc, 1 + off, 1 + off + w), wcv(1, cc), a[:, 0:w], ALU.mult, ALU.add, zr + [prmR, ar], [ar])
                stt(a[:, 0:w], zv(cc, off, off + w), wcv(0, cc), a[:, 0:w], ALU.mult, ALU.add, zr + [prmR, ar], [ar])
                tt(ycv(cc, off, w), a[:, 0:w], gbv(cc, off, w), ALU.mult, [ar, GBR[cc][nt]], [YCR[cc][nt]])
            else:
                a3 = a[:, 0:64].rearrange("p (b t) -> p b t", t=4)
                ts(a3, zs[:, cc, :, 2:6], wcv(2, cc), None, ALU.mult, ALU.bypass, [zsR, prmR], [ar])
                stt(a3, zs[:, cc, :, 1:5], wcv(1, cc), a3, ALU.mult, ALU.add, [zsR, prmR, ar], [ar])
                stt(a3, zs[:, cc, :, 0:4], wcv(0, cc), a3, ALU.mult, ALU.add, [zsR, prmR, ar], [ar])
                tt(ycv(cc, off, 64), a[:, 0:64], gbv(cc, off, 64), ALU.mult, [ar, GBR[cc][nt]], [YCR[cc][nt]])

        for cc in range(2):
            for nt, (off, w) in enumerate(NT):
                units.append(lambda cc=cc, nt=nt, off=off, w=w: conv_unit(cc, nt, off, w))

        def gmlp_unit(blk):
            P = 128 if blk < 8 else 64
            t0 = blk * 128
            nt = ntof(blk)
            pt, pr = pb[7]
            for h in range(4):
                if blk < 8:
                    mm(pt[0:64, h * 128:(h + 1) * 128], vta(blk, 0, 128, h * 64, (h + 1) * 64), WT[:, h, :], True, False, [VAR[blk], WTR], [pr])
                    mm(pt[0:64, h * 128:(h + 1) * 128], cb[0:1, CB_ONE:CB_ONE + 64], bsg[0:1, h, :], False, True, [cbR, bsgR], [pr])
                else:
                    mm(pt[0:64, h * 128:h * 128 + 64], vta(8, 0, 64, h * 64, (h + 1) * 64), Wblk[:, h, :], True, False, [VAR[8], WblkR], [pr])
                    mm(pt[0:64, h * 128:h * 128 + 64], cb[0:1, CB_ONE:CB_ONE + 64], bss[0:1, h, :], False, True, [cbR, bssR], [pr])
            uo = Dbf[0:64, t0: t0 + 4 * TM].rearrange("p (h t) -> p h t", t=TM)[:, :, 0:P]
            yo = Abf[0:64, t0: t0 + 4 * TM].rearrange("p (h t) -> p h t", t=TM)[:, :, 0:P]
            tt(yo, uo, pt[0:64, 0:512].rearrange("p (h t) -> p h t", t=128)[:, :, 0:P], ALU.mult,
               [pr] + [UR[h][nt] for h in range(4)], [YAR[h][nt] for h in range(4)])

        for blk in blocks:
            units.append(lambda blk=blk: gmlp_unit(blk))

        bankS = Rot(pb[0:4])
        items = [(blk, g) for blk in range(nblk) for g in range(2)]

        def kcur(blk, g):
            return kdv(g, blk * 128, 128), KDR[g][ntof(blk)]

        def emit_scores(blk, g):
            res = []
            gblk = tile * 8 + blk
            for which in (0, 1):
                if which == 1 and gblk == 0:
                    res.append(None)
                    continue
                pt, pr = bankS.next()
                bo = C_BIAS + which * 1024 + g * 512
                mm(pt.ap(), cb[:, CB_ID:CB_ID + 128], cbt[:, bo:bo + 512], True, False, [cbR, cbtR], [pr])
                for hp in range(4):
                    if which == 0:
                        ka, kr = kcur(blk, g)
                    elif blk > 0:
                        ka, kr = kcur(blk - 1, g)
                    else:
                        ka, kr = kprev[:, l, g, :], KPR[l]
                    mm(pt[:, hp * 128:(hp + 1) * 128], ka, qv(4 * g + hp, blk * 128, 128), False, hp == 3,
                       [kr, QR[4 * g + hp][ntof(blk)]], [pr])
                res.append((pt, pr))
            return res

        def emit_soft(blk, g, sc):
            ps_ = []
            for which in (0, 1):
                if sc[which] is None:
                    ps_.append(None)
                    continue
                pt, pr = sc[which]
                p_, p_r = bfpool.next()
                act(p_.ap(), pt.ap(), AF.Exp, [pr], [p_r])
                ps_.append((p_, p_r))
            return ps_

        def emit_pv(blk, g, ps_):
            po, por = pb[4]
            pd, pdr = pb[5]
            lst = [(w_, ps_[w_]) for w_ in (0, 1) if ps_[w_] is not None]
            for i, (which, (p_, p_r)) in enumerate(lst):
                if which == 0:
                    va_, vr_ = vtb(blk, 0, 128, g * 64, (g + 1) * 64), VBR[blk]
                elif blk > 0:
                    va_, vr_ = vtb(blk - 1, 0, 128, g * 64, (g + 1) * 64), VBR[blk - 1]
                else:
                    va_, vr_ = vprev[:, l, g * 64:(g + 1) * 64], VPR[l]
                mm(po[0:64, :], va_, p_.ap(), i == 0, i == len(lst) - 1, [vr_, p_r], [por])
            for i, (which, (p_, p_r)) in enumerate(lst):
                mm(pd[0:64, :], cb[:, CB_ONE:CB_ONE + 64], p_.ap(), i == 0, False, [cbR, p_r], [pdr])
            mm(pd[0:64, :], cb[0:2, CB_ONE:CB_ONE + 64], skP[0:2, 4 * g:4 * g + 4, :].rearrange("p h t -> p (h t)"), False, True,
               [cbR, skPR], [pdr])
            d_, d_r = f32pool.next()
            k.I("dve", lambda e, d_=d_, pd=pd: e.reciprocal(out=d_[0:64, :], in_=pd[0:64, :]), reads=[pdr], writes=[d_r])
            t0 = blk * 128
            yo = H[0:64, 4 * g:4 * g + 4, t0:t0 + 128]
            tt(yo, po[0:64, :].rearrange("p (h t) -> p h t", t=128), d_[0:64, :].rearrange("p (h t) -> p h t", t=128), ALU.mult,
               [por, d_r], [YBR[4 * g + hp][ntof(blk)] for hp in range(4)])

        upos = [0]
        per_item = (len(units) + len(items) - 1) // len(items)

        def pop_units(n):
            for _ in range(n):
                if upos[0] < len(units):
                    units[upos[0]]()
                    upos[0] += 1
        n_it = len(items)
        scs = {0: emit_scores(*items[0])}
        if n_it > 1:
            scs[1] = emit_scores(*items[1])
        pss = {0: emit_soft(items[0][0], items[0][1], scs.pop(0))}
        for i, (blk, g) in enumerate(items):
            if i + 2 < n_it:
                scs[i + 2] = emit_scores(*items[i + 2])
            if i + 1 < n_it:
                pss[i + 1] = emit_soft(items[i + 1][0], items[i + 1][1], scs.pop(i + 1))
            pop_units(per_item)
            emit_pv(blk, g, pss.pop(i))
        pop_units(len(units))

        if not B:
            for g in range(2):
                k.I("act", lambda e, g=g: e.activation(out=kprev[:, l, g, :], in_=kdv(g, 896, 128), func=AF.Copy),
                    reads=[KDR[g][1]], writes=[KPR[l]])
            k.I("act", lambda e: e.activation(out=vprev[:, l, :], in_=vtb(7, 0, 128, 0, 128), func=AF.Copy), reads=[VBR[7]], writes=[VPR[l]])

        if bail(4):
            return
        if has_s:
            for grp8 in range(4):
                pt, pr = bankM.next()
                ptb = pt.ap().bitcast(BF16)
                for j in range(8):
                    idx = grp8 * 8 + j
                    b, kv = idx // 2, idx % 2
                    k.I("pe", lambda e, ptb=ptb, j=j, b=b, kv=kv: e.transpose(ptb[0:64, j * 128:(j + 1) * 128], kc[:, b, kv * 64:(kv + 1) * 64], cb[:, CB_ID:CB_ID + 128]),
                        reads=[kcR, cbR], writes=[pr])
                act(Abf[0:64, 18688 + grp8 * 1024: 18688 + (grp8 + 1) * 1024], ptb[0:64, :], AF.Copy, [pr], [KTR])
            psc, pscr = pb[0]
            psn = [pb[1], pb[2]]
            pos = [pb[3], pb[4]]
            pd, pdr = pb[5]
            for b in range(16):
                for kv in range(2):
                    o_ = b * 32 + kv * 16
                    mm(psc[:, o_:o_ + 16], ktc(b, kv), qs_ap[:, o_:o_ + 16], True, True, [KTR, QSR], [pscr])
            for kv in range(2):
                mm(psn[kv][0][0:64, :], ks_ap[:, kv * 64:(kv + 1) * 64], qs_ap, True, True, [KSR, QSR], [psn[kv][1]])
            s_, s_r = f32pool.next()
            tt(s_.ap(), psc.ap(), cbt[:, C_BSC:C_BSC + 512], ALU.add, [pscr, cbtR], [s_r])
            pc_, pc_r = bfpool.next()
            act(pc_.ap(), s_.ap(), AF.Exp, [s_r], [pc_r])
            s2_, s2_r = f32pool.next()

            def kvv(ap, kv):
                return ap.rearrange("p (b x) -> p b x", x=32)[:, :, kv * 16:(kv + 1) * 16]
            for kv in range(2):
                tt(kvv(s2_[0:64, :], kv), kvv(psn[kv][0][0:64, :], kv), kvv(cbt[0:64, C_BSN:C_BSN + 512], kv), ALU.add,
                   [psn[kv][1], cbtR], [s2_r])
            pn_, pn_r = bfpool.next()
            act(pn_[0:64, :], s2_[0:64, :], AF.Exp, [s2_r], [pn_r])
            for kv in range(2):
                po, por = pos[kv]
                mm(po[0:64, :], vtb(8, 0, 64, kv * 64, (kv + 1) * 64), pn_[0:64, :], True, False, [VBR[8], pn_r], [por])
                for b in range(16):
                    o_ = b * 32 + kv * 16
                    mm(po[0:64, o_:o_ + 16], vc[:, b, kv * 64:(kv + 1) * 64], pc_[:, o_:o_ + 16], False, b == 15, [vcR, pc_r], [por])
            mm(pd[0:64, :], cb[:, CB_ONE:CB_ONE + 64], pc_.ap(), True, False, [cbR, pc_r], [pdr])
            mm(pd[0:64, :], cb[0:64, CB_ONE:CB_ONE + 64], pn_[0:64, :], False, False, [cbR, pn_r], [pdr])
            mm(pd[0:64, :], cb[0:2, CB_ONE:CB_ONE + 64], skS.ap(), False, True, [cbR, skPR], [pdr])
            d_, d_r = f32pool.next()
            k.I("dve", lambda e, d_=d_, pd=pd: e.reciprocal(out=d_[0:64, :], in_=pd[0:64, :]), reads=[pdr], writes=[d_r])
            for kv in range(2):
                po, por = pos[kv]
                yo = H[0:64, 4 * kv:4 * kv + 4, 1024:1088].rearrange("p h (b t) -> p b h t", t=4)
                tt(yo, po[0:64, :].rearrange("p (b h t) -> p b h t", b=16, h=8)[:, :, 4 * kv:4 * kv + 4, :],
                   d_[0:64, :].rearrange("p (b h t) -> p b h t", b=16, h=8)[:, :, 4 * kv:4 * kv + 4, :],
                   ALU.mult, [por, d_r], [YBR[4 * kv + h][2] for h in range(4)])

        if bail(5):
            return
        def gnorm(view, regs, nh, pin, ones_off, goff):
            for nt, (off, w) in enumerate(NT):
                rs, rsr = rms_rstd([(view(h, off, w), regs[h][nt]) for h in range(nh)], ones_off, pin, pin, w)
                for h in range(nh):
                    o = goff + l * nh + h
                    stt(view(h, off, w), view(h, off, w), prm[0:pin, o:o + 1], rs[0:pin, 0:w], ALU.mult, ALU.mult,
                        [regs[h][nt], rsr, prmR], [regs[h][nt]])
        gnorm(yav, YAR, 4, 64, CB_O256, P_GA)
        gnorm(ybv, YBR, 8, 64, CB_O512, P_GB)
        gnorm(ycv, YCR, 2, 128, CB_O256, P_GC)

        if bail(6):
            return
        tw = NT[-1][0] + NT[-1][1]
        nts = range(len(NT))
        for ci in range(2):
            he, ho = 2 * ci, 2 * ci + 1
            k.dma("sp", Abf[64:128, he * TM: he * TM + tw], Abf[0:64, ho * TM: ho * TM + tw],
                  reads=[YAR[ho][nt] for nt in nts], writes=[YSA[ci][nt] for nt in nts], grp=ysG[ci])
        for ci in range(4):
            he, ho = 2 * ci, 2 * ci + 1
            k.dma("sp", Hflat[64:128, he * TM: he * TM + tw], Hflat[0:64, ho * TM: ho * TM + tw],
                  reads=[YBR[ho][nt] for nt in nts], writes=[YSB[ci][nt] for nt in nts], grp=ysG[2 + ci])
        for c in range(8):
            st, sr = wB.get(("mo", l, c))
            for nt, (off, w) in enumerate(NT):
                bd, bdr = bankD.next()
                ops = [(st[:, ci, :], Abf[0:128, 2 * ci * TM + off: 2 * ci * TM + off + w], [YAR[2 * ci][nt], YSA[ci][nt]]) for ci in range(2)]
                ops += [(st[:, 2 + ci, :], Hflat[0:128, 2 * ci * TM + off: 2 * ci * TM + off + w], [YBR[2 * ci][nt], YSB[ci][nt]]) for ci in range(4)]
                ops += [(st[:, 6 + cc, :], ycv(cc, off, w), [YCR[cc][nt]]) for cc in range(2)]
                for i, (lh, rh, rr) in enumerate(ops):
                    mm(bd[:, 0:w], lh, rh, i == 0, i == len(ops) - 1, [sr] + rr, [bdr])
                act(dv(c, off, w), bd[:, 0:w], AF.Copy, [bdr], [DR[c][nt]])
        postnorm_residual(l, 3, False, NT)

    for tile in tiles:
        NT = [(0, 512), (512, 512)] + ([(1024, 64)] if tile == 1 else [])
        for c in range(8):
            k.dma("sp", X[:, c, 0:1024], xT[c * 128:(c + 1) * 128, tile * 1024:(tile + 1) * 1024], writes=[XR[c][0], XR[c][1]])
            if tile == 1:
                k.dma("sp", X[:, c, 1024:1088], xT[c * 128:(c + 1) * 128, 2048:2112], writes=[XR[c][2]])
        for kind, l, i in steps[tile]:
            if kind == "ffn":
                ffn(l, i, NT)
            else:
                mixer(l, tile, NT)
        for c in range(8):
            k.dma("sp", yT[c * 128:(c + 1) * 128, tile * 1024:(tile + 1) * 1024], X[:, c, 0:1024], reads=[XR[c][0], XR[c][1]], grp=outG, is_out=True)
            if tile == 1:
                k.dma("sp", yT[c * 128:(c + 1) * 128, 2048:2112], X[:, c, 1024:1088], reads=[XR[c][2]], grp=outG, is_out=True)
    k.finish()
    return nc, k


def _in_maps(x_prompt, x_sample, cache_swa_k, cache_swa_v, cache_conv, norm_g, w_ffn_gu, w_ffn_down,
             w_mix_in, w_mix_out, g_mix_out, w_sgu, b_sgu, g_sgu, attn_sinks, w_conv, depth=4):
    f = lambda a: np.ascontiguousarray(np.asarray(a, dtype=np.float32))
    x_prompt, x_sample = f(x_prompt), f(x_sample)
    cf, cbias = _consts()
    prm = _prm(f(norm_g), f(g_mix_out), f(w_conv), f(attn_sinks))
    shared = {"w_gu": f(w_ffn_gu[:depth]), "w_dn": f(w_ffn_down[:depth]), "w_in": f(w_mix_in[:depth]), "w_out": f(w_mix_out[:depth]),
              "w_sgu": f(w_sgu), "b_sgu": f(b_sgu), "g_sgu": f(g_sgu), "prm": prm, "cf": cf, "cbias": cbias}
    ck_all = f(cache_swa_k).reshape(4, 128, 128, 128)
    cv_all = f(cache_swa_v).reshape(4, 128, 128, 128)
    cc_all = f(cache_conv)
    maps = []
    for c in range(NCORE):
        xt = np.empty((1024, 2112), np.float32)
        xt[:, 0:2048] = x_prompt[c].T
        xt[:, 2048:2112] = x_sample[c * 16:(c + 1) * 16].reshape(64, 1024).T
        m = dict(shared)
        m["xT"] = xt
        m["ck"] = np.ascontiguousarray(ck_all[:, c * 16:(c + 1) * 16])
        m["cv"] = np.ascontiguousarray(cv_all[:, c * 16:(c + 1) * 16])
        m["cc"] = np.ascontiguousarray(cc_all[:, c * 16:(c + 1) * 16])
        maps.append(m)
    return maps


def _assemble(results):
    y_p = np.empty((8, 2048, 1024), np.float32)
    y_s = np.empty((128, 4, 1024), np.float32)
    sgu = np.empty((4, 128, 4, 4, 64), np.float32)
    kp = np.empty((4, 8, 128, 2, 64), np.float32)
    vp = np.empty((4, 8, 128, 2, 64), np.float32)
    ks = np.empty((4, 128, 128, 2, 64), np.float32)
    vs = np.empty((4, 128, 128, 2, 64), np.float32)
    cp = np.empty((4, 8, 2, 256), np.float32)
    cs = np.empty((4, 128, 2, 256), np.float32)
    for c, r in enumerate(results):
        yt = r["yT"]
        y_p[c] = yt[:, 0:2048].T
        y_s[c * 16:(c + 1) * 16] = yt[:, 2048:2112].T.reshape(16, 4, 1024)
        sgu[:, c * 16:(c + 1) * 16] = r["o_sgu"].reshape(4, 16, 4, 4, 64)
        kp[:, c] = r["o_kp"].reshape(4, 128, 2, 64)
        vp[:, c] = r["o_vp"].reshape(4, 128, 2, 64)
        ks[:, c * 16:(c + 1) * 16] = r["o_ks"].reshape(4, 16, 128, 2, 64)
        vs[:, c * 16:(c + 1) * 16] = r["o_vs"].reshape(4, 16, 128, 2, 64)
        cp[:, c] = r["o_cp"]
        cs[:, c * 16:(c + 1) * 16] = r["o_cs"].reshape(4, 16, 2, 256)
    return (y_p, y_s, sgu, kp, vp, ks, vs, cp, cs)


def kernel(**inputs):
    maps = _in_maps(**inputs)
    nc, _ = build(depth=4)
    res = run_bass_kernel_spmd(nc, maps, core_ids=list(range(NCORE)))
    return _assemble(res.results)
```

```python
import numpy as np
import concourse.bass as bass
import concourse.mybir as mybir
from concourse.bass_utils import run_bass_kernel_spmd

F32 = mybir.dt.float32
BF16 = mybir.dt.bfloat16
AF = mybir.ActivationFunctionType
ALU = mybir.AluOpType
AX = mybir.AxisListType

ENGS = ("pe", "act", "dve", "pool", "sp")
EPS = 1e-6
NEGB = -30000.0
TM = 1088
NSLOT_A = 4
RAW_ONLY = False
NCORE = 8


class Reg:
    __slots__ = ("w", "r", "name")

    def __init__(self, name=""):
        self.w = None
        self.r = []
        self.name = name


class Grp:
    __slots__ = ("sem", "n")

    def __init__(self, sem):
        self.sem = sem
        self.n = 0


class Ins:
    __slots__ = ("eng", "fn", "deps", "signal", "count", "dma", "dval", "tiny")

    def __init__(self, eng, fn, deps, dma=None):
        self.tiny = False
        self.eng = eng
        self.fn = fn
        self.deps = deps
        self.signal = False
        self.count = 0
        self.dma = dma
        self.dval = 0


class K:
    def __init__(self, nc, same_engine_sync=True):
        self.nc = nc
        self.streams = {e: [] for e in ENGS}
        self.all = []
        self.sems = {e: nc.alloc_semaphore("s_" + e) for e in ENGS}
        self.same = same_engine_sync
        self.tiny_ctx = False
        self.out_grps = []
        self._n = 0

    def sb(self, shape, dt, name=None):
        self._n += 1
        return self.nc.alloc_sbuf_tensor("sb_" + (name or f"t{self._n}"), list(shape), dt)

    def ps(self, shape, dt=F32, name=None):
        self._n += 1
        return self.nc.alloc_psum_tensor(name or f"p{self._n}", list(shape), dt)

    def grp(self):
        self._n += 1
        return Grp(self.nc.alloc_semaphore(f"g{self._n}"))

    def _deps(self, eng, reads, writes):
        deps = []
        raw = set()
        for b in reads:
            if b.w is not None:
                deps.append(b.w)
                raw.add(id(b.w))
        for b in writes:
            if b.w is not None:
                deps.append(b.w)
            deps.extend(b.r)
        out = []
        seen = set()
        for d in deps:
            if id(d) in seen:
                continue
            seen.add(id(d))
            if d.dma is None and d.eng == eng:
                if eng == "pe" or not (self.same or d.tiny) or (RAW_ONLY and id(d) not in raw):
                    continue
            out.append((d, 16 * d.dma.n if d.dma is not None else 0))
        return out

    def I(self, eng, fn, reads=(), writes=(), out=None):
        ins = Ins(eng, fn, self._deps(eng, reads, writes))
        ins.tiny = self.tiny_ctx
        if out is not None:
            n = 1
            for d_ in out.shape[1:]:
                n *= d_
            if n < 256:
                ins.tiny = True
        for b in reads:
            b.r.append(ins)
        for b in writes:
            b.w = ins
            b.r = []
        self.streams[eng].append(ins)
        self.all.append(ins)
        return ins

    def dma(self, q, out_ap, in_ap, reads=(), writes=(), grp=None, is_out=False, **kw):
        if grp is None:
            grp = self.grp()
        deps = self._deps("__dma__", reads, writes)

        def fn(e, out_ap=out_ap, in_ap=in_ap, kw=kw):
            return e.dma_start(out=out_ap, in_=in_ap, **kw)
        ins = Ins(q, fn, deps, dma=grp)
        grp.n += 1
        ins.dval = 16 * grp.n
        for b in reads:
            b.r.append(ins)
        for b in writes:
            b.w = ins
            b.r = []
        self.streams[q].append(ins)
        self.all.append(ins)
        if is_out and grp not in self.out_grps:
            self.out_grps.append(grp)
        return ins

    def finish(self):
        nc = self.nc
        for ins in self.all:
            for d, _ in ins.deps:
                if d.dma is None:
                    d.signal = True
        cnt = {e: 0 for e in ENGS}
        for ins in self.all:
            if ins.dma is None and ins.signal:
                cnt[ins.eng] += 1
                ins.count = cnt[ins.eng]
        engmap = {"pe": "tensor", "act": "scalar", "dve": "vector", "pool": "gpsimd", "sp": "sync"}
        with nc.Block() as block:
            for e in ENGS:
                stream = self.streams[e]
                final = [(g.sem, 16 * g.n) for g in self.out_grps] if e == "sp" else []

                def body(eng, stream=stream, e=e, final=final):
                    seen = {}
                    for ins in stream:
                        waits = {}
                        for d, dv_ in ins.deps:
                            if d.dma is not None:
                                s, v = d.dma.sem, dv_
                            else:
                                s, v = self.sems[d.eng], d.count
                            if seen.get(s.num, 0) >= v:
                                continue
                            if s.num not in waits or waits[s.num][1] < v:
                                waits[s.num] = (s, v)
                        for s, v in waits.values():
                            eng.wait_ge(s, v)
                            seen[s.num] = v
                        bi = ins.fn(eng)
                        if ins.dma is not None:
                            bi.then_inc(ins.dma.sem, 16)
                        elif ins.signal:
                            bi.then_inc(self.sems[e], 1)
                    for s, v in final:
                        eng.wait_ge(s, v)
                getattr(block, engmap[e])(body)
        return nc


class Rot:
    def __init__(self, items):
        self.items = items
        self.i = 0

    def next(self):
        it = self.items[self.i % len(self.items)]
        self.i += 1
        return it


class WStream:
    def __init__(self, k, seq, slots, loader):
        self.k, self.seq, self.slots, self.loader = k, seq, slots, loader
        self.pos = 0
        self.emitted = 0

    def get(self, key, held=0):
        assert self.seq[self.pos] == key, (self.seq[self.pos], key)
        n = len(self.slots)
        while self.emitted < min(len(self.seq), self.pos - held + n):
            t, r, g = self.slots[self.emitted % n]
            self.loader(self.seq[self.emitted], t, r, g)
            self.emitted += 1
        s = self.slots[self.pos % n]
        self.pos += 1
        return s[0], s[1]


C_ID, C_TRIL, C_MS, C_E = 0, 128, 256, 320
C_SEL = 384
C_TOT = 386
C_BIAS, C_BSC, C_BSN = 0, 2048, 2560
B_TOT = 3072


def _consts():
    cf = np.zeros((128, C_TOT), np.float32)
    cbias = np.zeros((128, B_TOT), np.float32)
    cf[:, C_ID:C_ID + 128] = np.eye(128, dtype=np.float32)
    s = np.arange(128)[:, None]
    t = np.arange(128)[None, :]
    cf[:, C_TRIL:C_TRIL + 128] = (s <= t).astype(np.float32)
    slopes = 2.0 ** (-8.0 * np.arange(1, 9) / 8)
    for g in range(2):
        for hp in range(4):
            sl = slopes[4 * g + hp]
            cur = np.where(s <= t, -sl * (t - s), NEGB)
            prev = np.where(s > t, -sl * (128 + t - s), NEGB)
            o = C_BIAS + g * 512 + hp * 128
            cbias[:, o:o + 128] = cur
            o = C_BIAS + 1024 + g * 512 + hp * 128
            cbias[:, o:o + 128] = prev
    bsc = np.zeros((128, 16, 8, 4), np.float32)
    ss = np.arange(128)[:, None]
    tt = np.arange(4)[None, :]
    for h in range(8):
        bsc[:, :, h, :] = np.where(ss > tt, -slopes[h] * (128 + tt - ss), NEGB)[:, None, :]
    cbias[:, C_BSC:C_BSC + 512] = bsc.reshape(128, 512)
    bsn = np.full((16, 4, 16, 8, 4), NEGB, np.float32)
    for b in range(16):
        for tp in range(4):
            for t_ in range(4):
                if tp <= t_:
                    bsn[b, tp, b, :, t_] = -slopes * (t_ - tp)
    cbias[0:64, C_BSN:C_BSN + 512] = bsn.reshape(64, 512)
    ms = np.zeros((16, 4, 16, 4), np.float32)
    for b in range(16):
        for s_ in range(4):
            for t_ in range(4):
                if s_ <= t_:
                    ms[b, s_, b, t_] = 1.0
    cf[0:64, C_MS:C_MS + 64] = ms.reshape(64, 64)
    e = np.zeros((4, 16, 4), np.float32)
    for s_ in range(4):
        e[s_, :, s_] = 1.0
    cf[0:4, C_E:C_E + 64] = e.reshape(4, 64)
    cf[0, C_SEL] = 1.0
    cf[1, C_SEL + 1] = 1.0
    return cf, cbias


P_GV, P_GA, P_GB, P_GC, P_WC, P_SK = 0, 192, 208, 240, 248, 272
P_TOT = 304


def _prm(norm_g, g_mix_out, w_conv, attn_sinks):
    p = np.zeros((128, P_TOT), np.float32)
    p[:, P_GV:P_GV + 192] = norm_g.reshape(4, 6, 8, 128).transpose(3, 0, 1, 2).reshape(128, 192)
    p[0:64, P_GA:P_GA + 16] = g_mix_out[:, 0:256].reshape(4, 4, 64).transpose(2, 0, 1).reshape(64, 16)
    p[0:64, P_GB:P_GB + 32] = g_mix_out[:, 256:768].reshape(4, 8, 64).transpose(2, 0, 1).reshape(64, 32)
    p[:, P_GC:P_GC + 8] = g_mix_out[:, 768:1024].reshape(4, 2, 128).transpose(2, 0, 1).reshape(128, 8)
    p[:, P_WC:P_WC + 24] = w_conv.reshape(4, 3, 2, 128).transpose(3, 0, 1, 2).reshape(128, 24)
    p[:, P_SK:P_SK + 32] = np.broadcast_to(attn_sinks.reshape(1, 32), (128, 32))
    return p


def build(depth=4, tiles=(0, 1), same_sync=True, nsteps=None, mix_stop=99):
    nc = bass.Bass("TRN2", target_bir_lowering=False)
    k = K(nc, same_engine_sync=same_sync)

    def din(name, shape):
        return nc.dram_tensor(name, list(shape), F32, kind="ExternalInput").ap()

    def dout(name, shape):
        return nc.dram_tensor(name, list(shape), F32, kind="ExternalOutput").ap()

    xT = din("xT", [1024, 2112])
    ck = din("ck", [4, 16, 128, 128])
    cv = din("cv", [4, 16, 128, 128])
    cc_in = din("cc", [4, 16, 2, 256])
    w_gu = din("w_gu", [depth, 2, 1024, 5632])
    w_dn = din("w_dn", [depth, 2, 2816, 1024])
    w_in = din("w_in", [depth, 1024, 2048])
    w_out = din("w_out", [depth, 1024, 1024])
    w_sgu = din("w_sgu", [4, 4, 128, 128])
    b_sgu = din("b_sgu", [4, 4, 128])
    g_sgu = din("g_sgu", [4, 256])
    prm_d = din("prm", [128, P_TOT])
    cf_d = din("cf", [128, C_TOT])
    cbias_d = din("cbias", [128, B_TOT])

    yT = dout("yT", [1024, 2112])
    o_sgu = dout("o_sgu", [4, 64, 256])
    o_kp = dout("o_kp", [4, 128, 128])
    o_vp = dout("o_vp", [4, 128, 128])
    o_ks = dout("o_ks", [4, 16, 128, 128])
    o_vs = dout("o_vs", [4, 16, 128, 128])
    o_cp = dout("o_cp", [4, 2, 256])
    o_cs = dout("o_cs", [4, 32, 256])

    X = k.sb([128, 8, TM], F32, "X")
    H = k.sb([128, 8, TM], BF16, "H")
    ACTt = k.sb([128, 22 * TM], BF16, "ACT")
    Dt = k.sb([128, 8720], F32, "D")
    XR = [[Reg() for _ in range(3)] for _ in range(8)]
    HR = [[Reg() for _ in range(3)] for _ in range(8)]
    AR = [[Reg() for _ in range(3)] for _ in range(22)]
    DR = [[Reg() for _ in range(3)] for _ in range(8)]

    def actv(f, off, w):
        return ACTt[:, f * TM + off: f * TM + off + w]

    def dv(c, off, w):
        return Dt[:, c * TM + off: c * TM + off + w]

    ZW = TM + 2

    def zv(cc, a, b):
        return Dt[:, cc * ZW + a: cc * ZW + b]
    Dbf = Dt[:, 2 * ZW: 8720].bitcast(BF16)

    def uv(h, off, w):
        return Dbf[0:64, h * TM + off: h * TM + off + w]

    def qv(h, off, w):
        return Abf[0:64, 4352 + h * TM + off: 4352 + h * TM + off + w]

    def kdv(kv, off, w):
        return Dbf[0:64, 8704 + kv * TM + off: 8704 + kv * TM + off + w]

    def gbv(cc, off, w):
        return Dbf[:, 10880 + cc * TM + off: 10880 + cc * TM + off + w]
    Abf = ACTt.ap()

    def yav(h, off, w):
        return Abf[0:64, h * TM + off: h * TM + off + w]

    Hflat = H.ap().rearrange("p c t -> p (c t)")

    def ybv(h, off, w):
        return Hflat[0:64, h * TM + off: h * TM + off + w]

    def ycv(cc, off, w):
        return Abf[:, 13056 + cc * TM + off: 13056 + cc * TM + off + w]

    def vta(blk, p0, p1, a, b):
        return Abf[p0:p1, 15232 + blk * 256 + a: 15232 + blk * 256 + b]

    def vtb(blk, p0, p1, a, b):
        return Abf[p0:p1, 17536 + blk * 128 + a: 17536 + blk * 128 + b]

    def ktc(b, kv):
        return Abf[0:64, 18688 + (b * 2 + kv) * 128: 18688 + (b * 2 + kv + 1) * 128]
    qs_ap = Abf[0:64, 22784: 22784 + 512]
    ks_ap = Abf[0:64, 23296: 23296 + 128]

    UR = [[Reg() for _ in range(3)] for _ in range(4)]
    QR = [[Reg() for _ in range(3)] for _ in range(8)]
    QU = [[Reg() for _ in range(3)] for _ in range(4)]
    UU = [[Reg() for _ in range(3)] for _ in range(2)]
    KU = [Reg(), Reg()]
    YSA = [[Reg() for _ in range(3)] for _ in range(2)]
    YSB = [[Reg() for _ in range(3)] for _ in range(4)]
    KDR = [[Reg() for _ in range(3)] for _ in range(2)]
    GBR = [[Reg() for _ in range(3)] for _ in range(2)]
    ZR = [[Reg() for _ in range(3)] for _ in range(2)]
    ZHR = [Reg() for _ in range(2)]
    YAR = [[Reg() for _ in range(3)] for _ in range(4)]
    YBR = HR
    YCR = [[Reg() for _ in range(3)] for _ in range(2)]
    VAR = [Reg() for _ in range(9)]
    VBR = [Reg() for _ in range(9)]
    KTR = Reg()
    QSR = Reg()
    KSR = Reg()

    cf = k.sb([128, C_TOT], F32, "cf")
    cfR = Reg()
    prm = k.sb([128, P_TOT], F32, "prm")
    prmR = Reg()
    cb = k.sb([128, 128 * 5 + 64], BF16, "cb")
    cbR = Reg()
    CB_ID, CB_O1024, CB_O256, CB_O512, CB_ONE, CB_E = 0, 128, 256, 384, 512, 640
    gh = k.sb([128, 4, 2, 8], F32, "gh")
    ghR = Reg()

    def gvec(l, i, c):
        o = P_GV + (l * 6 + i) * 8 + c
        return prm[:, o:o + 1]

    k.dma("sp", cf.ap(), cf_d, writes=[cfR])
    cbt = k.sb([128, B_TOT], BF16, "cbt")
    cbtR = Reg()
    k.dma("pool", cbt.ap(), cbias_d, writes=[cbtR])
    k.dma("sp", prm.ap(), prm_d, writes=[prmR])
    k.I("dve", lambda e: e.tensor_copy(out=cb[:, CB_ID:CB_ID + 128], in_=cf[:, C_ID:C_ID + 128]), reads=[cfR], writes=[cbR])
    for o, v in ((CB_O1024, 1.0 / 1024), (CB_O256, 1.0 / 256), (CB_O512, 1.0 / 512), (CB_ONE, 1.0)):
        k.I("dve", lambda e, o=o, v=v: e.memset(cb[:, o:o + 128], v), writes=[cbR])
    k.I("dve", lambda e: e.tensor_copy(out=cb[0:4, CB_E:CB_E + 64], in_=cf[0:4, C_E:C_E + 64]), reads=[cfR], writes=[cbR])
    for j, i in enumerate((1, 5)):
        for l in range(4):
            o = P_GV + (l * 6 + i) * 8
            k.I("dve", lambda e, l=l, j=j, o=o: e.tensor_single_scalar(out=gh[:, l, j, :], in_=prm[:, o:o + 8], scalar=0.5, op=ALU.mult),
                reads=[prmR], writes=[ghR])

    epsc = k.sb([128, 1], F32, "epsc")
    k.I("dve", lambda e: e.memset(epsc.ap(), EPS), writes=[cbR])
    f32pool = Rot([(k.sb([128, 512], F32), Reg()) for _ in range(3)])
    bfpool = Rot([(k.sb([128, 512], BF16), Reg()) for _ in range(4)])
    rspool = Rot([(k.sb([128, 512], F32), Reg()) for _ in range(2)])
    small = k.sb([128, 64], F32, "small")
    smallR = Reg()
    small2 = [small[:, 0:32], small[:, 32:64]]
    small2R = [Reg(), Reg()]
    stg = Rot([(k.sb([128, 256], F32), Reg(), k.grp()) for _ in range(2)])
    vgpool = Rot(f32pool.items[0:2])
    kprev = k.sb([64, 4, 2, 128], BF16, "kprev")
    vprev = k.sb([128, 4, 128], BF16, "vprev")
    zprev = k.sb([128, 4, 2, 2], F32, "zprev")
    KPR = [Reg() for _ in range(4)]
    VPR = [Reg() for _ in range(4)]
    ZPR = [Reg() for _ in range(4)]
    gsg = k.sb([128, 256], F32, "gsg")
    gsgR = Reg()
    gsgG = k.grp()
    bsgG = k.grp()
    bsg = k.sb([1, 4, 128], BF16, "bsg")
    bsgR = Reg()
    bss = k.sb([1, 4, 64], BF16, "bss")
    bssR = Reg()
    wsg32 = k.sb([128, 4, 128], BF16, "wsgb")
    wsg32R = Reg()
    wsgG = k.grp()
    WT = k.sb([128, 4, 128], BF16, "WT")
    WTR = Reg()
    Xs = k.sb([4, 4, 64], BF16, "Xs")
    XsR = Reg()
    Wblk = k.sb([64, 4, 64], BF16, "Wblk")
    WblkR = Reg()
    sexp = k.sb([64, 8], F32, "sexp")
    sexpR = Reg()
    sk_hb = k.sb([2, 8], BF16, "sk_hb")
    sk_w = k.sb([2, 24], F32, "sk_w")
    skR = Reg()
    skP = k.sb([2, 8, 128], BF16, "skP")
    skS = k.sb([2, 512], BF16, "skS")
    skPR = Reg()
    kc = k.sb([128, 16, 128], BF16, "kc")
    vc = k.sb([128, 16, 128], BF16, "vc")
    kcR, vcR = Reg(), Reg()
    kcG, vcG = k.grp(), k.grp()
    zs = k.sb([128, 2, 16, 6], F32, "zs")
    zsR = Reg()
    cc32 = k.sb([32, 256], F32, "cc32")
    cc32R = Reg()
    ccG = k.grp()

    pb = [(k.ps([128, 512], F32), Reg()) for _ in range(8)]
    qG = [k.grp() for _ in range(4)]
    uG = [k.grp() for _ in range(2)]
    kuG = k.grp()
    ysG = [k.grp() for _ in range(6)]
    outG = k.grp()
    miscG = k.grp()

    seqA, seqB = [], []
    MIN_ORDER = [1, 4, 2, 3, 0, 5, 6, 7]
    steps = {}
    for tile in tiles:
        st_ = []
        for l in range(depth):
            st_ += [("ffn", l, 0), ("mix", l, 0), ("ffn", l, 1)]
        if nsteps is not None:
            st_ = st_[:nsteps]
        steps[tile] = st_
        for kind, l, i in st_:
            if kind == "ffn":
                seqA += [("gu", l, i, f) for f in range(22)]
                seqB += [("dn", l, i, c) for c in range(8)]
            else:
                seqA += [("min", l, j) for j in MIN_ORDER]
                seqB += [("mo", l, c) for c in range(8)]

    def loadA(key, t, r, g):
        if key[0] == "gu":
            _, l, i, f = key
            src = w_gu[l, i].rearrange("(kc p) f -> p kc f", p=128)
            k.dma("pool", t[:, :, 0:128], src[:, :, f * 128:(f + 1) * 128], writes=[r], grp=g)
            k.dma("pool", t[:, :, 128:256], src[:, :, 2816 + f * 128:2816 + (f + 1) * 128], writes=[r], grp=g)
        else:
            _, l, j = key
            src = w_in[l].rearrange("(kc p) f -> p kc f", p=128)[:, :, j * 256:(j + 1) * 256]
            k.dma("pool", t.ap(), src, writes=[r], grp=g)

    def loadB(key, t, r, g):
        if key[0] == "dn":
            _, l, i, c = key
            src = w_dn[l, i].rearrange("(fc p) d -> p fc d", p=128)[:, :, c * 128:(c + 1) * 128]
            k.dma("pool", t.ap(), src, writes=[r], grp=g)
        else:
            _, l, c = key
            src = w_out[l, :, c * 128:(c + 1) * 128].rearrange("(kc p) f -> p kc f", p=128)
            k.dma("pool", t[:, 0:8, :], src, writes=[r], grp=g)

    wA = WStream(k, seqA, [(k.sb([128, 8, 256], BF16), Reg(), k.grp()) for _ in range(NSLOT_A)], loadA)
    wB = WStream(k, seqB, [(k.sb([128, 22, 128], BF16), Reg(), k.grp()) for _ in range(2)], loadB)

    def mm(out, lhsT, rhs, start, stop, reads, writes):
        k.I("pe", lambda e: e.matmul(out, lhsT=lhsT, rhs=rhs, start=start, stop=stop), reads=reads, writes=writes)

    def act(out, in_, func, reads, writes, **kw):
        k.I("act", lambda e: e.activation(out=out, in_=in_, func=func, **kw), reads=reads, writes=writes, out=out)

    def tt(out, in0, in1, op, reads, writes):
        k.I("dve", lambda e: e.tensor_tensor(out=out, in0=in0, in1=in1, op=op), reads=reads, writes=writes, out=out)

    def stt(out, in0, scalar, in1, op0, op1, reads, writes):
        k.I("dve", lambda e: e.scalar_tensor_tensor(out=out, in0=in0, scalar=scalar, in1=in1, op0=op0, op1=op1),
            reads=reads, writes=writes, out=out)

    def ts(out, in0, s1, s2, op0, op1, reads, writes):
        if s2 is None:
            k.I("dve", lambda e: e.tensor_single_scalar(out=out, in_=in0, scalar=s1, op=op0), reads=reads, writes=writes, out=out)
        else:
            k.I("dve", lambda e: e.tensor_scalar(out=out, in0=in0, scalar1=s1, scalar2=s2, op0=op0, op1=op1),
                reads=reads, writes=writes, out=out)

    def rsqrt_eps(out, in_, reads, writes):
        act(out, in_, AF.Sqrt, reads, writes, bias=epsc[0:out.shape[0], 0:1])
        k.I("dve", lambda e: e.reciprocal(out=out, in_=out), reads=writes, writes=writes, out=out)

    def rms_rstd(chunks, ones_off, pin, pout, w):
        pst, psr = pb[6]
        n = len(chunks)
        for i, (ap, rg) in enumerate(chunks):
            sq, sqr = bfpool.next()
            act(sq[0:pin, 0:w], ap, AF.Square, [rg], [sqr])
            mm(pst[0:pout, 0:w], cb[0:pin, ones_off:ones_off + pout], sq[0:pin, 0:w], i == 0, i == n - 1, [cbR, sqr], [psr])
        rs, rsr = rspool.next()
        rsqrt_eps(rs[0:pout, 0:w], pst[0:pout, 0:w], [psr], [rsr])
        return rs, rsr

    def prenorm(l, gi, NT):
        for nt, (off, w) in enumerate(NT):
            rs, rsr = rms_rstd([(X[:, c, off:off + w], XR[c][nt]) for c in range(8)], CB_O1024, 128, 128, w)
            for c in range(8):
                stt(H[:, c, off:off + w], X[:, c, off:off + w], gvec(l, gi, c), rs[:, 0:w], ALU.mult, ALU.mult,
                    [XR[c][nt], rsr, prmR], [HR[c][nt]])

    def postnorm_residual(l, gi, half, NT):
        for nt, (off, w) in enumerate(NT):
            rs, rsr = rms_rstd([(dv(c, off, w), DR[c][nt]) for c in range(8)], CB_O1024, 128, 128, w)
            for c in range(8):
                if half:
                    g = gh[:, l, 0 if gi == 1 else 1, c:c + 1]
                    gr = ghR
                else:
                    g = gvec(l, gi, c)
                    gr = prmR
                stt(dv(c, off, w), dv(c, off, w), g, rs[:, 0:w], ALU.mult, ALU.mult, [DR[c][nt], rsr, gr], [DR[c][nt]])
            for c in range(8):
                tt(X[:, c, off:off + w], X[:, c, off:off + w], dv(c, off, w), ALU.add, [XR[c][nt], DR[c][nt]], [XR[c][nt]])

    bankGU = Rot(pb[0:4])
    bankD = Rot(pb[4:6])

    def ffn(l, i, NT):
        prenorm(l, 0 if i == 0 else 4, NT)
        for f in range(22):
            st, sr = wA.get(("gu", l, i, f))
            for nt, (off, w) in enumerate(NT):
                bg, bgr = bankGU.next()
                bu, bur = bankGU.next()
                for kc in range(8):
                    mm(bg[:, 0:w], st[:, kc, 0:128], H[:, kc, off:off + w], kc == 0, kc == 7, [sr, HR[kc][nt]], [bgr])
                for kc in range(8):
                    mm(bu[:, 0:w], st[:, kc, 128:256], H[:, kc, off:off + w], kc == 0, kc == 7, [sr, HR[kc][nt]], [bur])
                sg, sgr = f32pool.next()
                act(sg[:, 0:w], bg[:, 0:w], AF.Silu, [bgr], [sgr])
                tt(actv(f, off, w), bu[:, 0:w], sg[:, 0:w], ALU.mult, [bur, sgr], [AR[f][nt]])
        for c in range(8):
            st, sr = wB.get(("dn", l, i, c))
            for nt, (off, w) in enumerate(NT):
                bd, bdr = bankD.next()
                for f in range(22):
                    mm(bd[:, 0:w], st[:, f, :], actv(f, off, w), f == 0, f == 21, [sr, AR[f][nt]], [bdr])
                act(dv(c, off, w), bd[:, 0:w], AF.Copy, [bdr], [DR[c][nt]])
        postnorm_residual(l, 1 if i == 0 else 5, True, NT)

    def layer_params(l):
        k.tiny_ctx = True
        _layer_params(l)
        k.tiny_ctx = False

    def _layer_params(l):
        k.dma("sp", gsg.ap(), g_sgu[l].partition_broadcast(128), writes=[gsgR], grp=gsgG)
        k.dma("pool", bsg.ap(), b_sgu[l:l + 1], writes=[bsgR], grp=bsgG)
        k.dma("pool", wsg32.ap(), w_sgu[l].rearrange("h t s -> t h s"), writes=[wsg32R], grp=wsgG)
        k.I("dve", lambda e: e.tensor_copy(out=bss.ap().rearrange("o h (b t) -> o h b t", t=4),
                                          in_=bsg[:, :, 0:4].unsqueeze(2).to_broadcast([1, 4, 16, 4])),
            reads=[bsgR], writes=[bssR])
        for h in range(4):
            pt, pr = pb[7]
            ptb_ = pt.ap().bitcast(BF16)
            k.I("pe", lambda e, h=h, ptb_=ptb_: e.transpose(ptb_[:, 0:128], wsg32[:, h, :], cb[:, CB_ID:CB_ID + 128]),
                reads=[wsg32R, cbR], writes=[pr])
            tt(WT[:, h, :], ptb_[:, 0:128], cf[:, C_TRIL:C_TRIL + 128], ALU.mult, [pr, cfR], [WTR])
        k.I("dve", lambda e: e.tensor_copy(out=Xs.ap().rearrange("s h (b t) -> s h b t", t=4),
                                          in_=WT[0:4, :, 0:4].unsqueeze(2).to_broadcast([4, 4, 16, 4])),
            reads=[WTR], writes=[XsR])
        for h in range(4):
            pt, pr = pb[7]
            mm(pt[0:64, 0:64], cb[0:4, CB_E:CB_E + 64], Xs[:, h, :], True, True, [cbR, XsR], [pr])
            tt(Wblk[:, h, :], pt[0:64, 0:64], cf[0:64, C_MS:C_MS + 64], ALU.mult, [pr, cfR], [WblkR])
        o = P_SK + l * 8
        act(sexp.ap(), prm[0:64, o:o + 8], AF.Exp, [prmR], [sexpR])
        k.I("dve", lambda e: e.tensor_copy(out=sk_hb.ap(), in_=sexp[0:2, :]), reads=[sexpR], writes=[skR])
        k.I("dve", lambda e: e.tensor_copy(out=sk_w[:, 0:8], in_=sk_hb.ap()), reads=[skR], writes=[skR])
        tt(sk_w[:, 8:16], sexp[0:2, :], sk_w[:, 0:8], ALU.subtract, [sexpR, skR], [skR])
        ts(sk_w[:, 0:8], sk_w[:, 0:8], cf[0:2, C_SEL:C_SEL + 1], None, ALU.mult, None, [skR, cfR], [skR])
        stt(sk_w[:, 16:24], sk_w[:, 8:16], cf[0:2, C_SEL + 1:C_SEL + 2], sk_w[:, 0:8], ALU.mult, ALU.add, [skR, cfR], [skR])
        k.I("dve", lambda e: e.tensor_copy(out=skP.ap(), in_=sk_w[:, 16:24].unsqueeze(2).to_broadcast([2, 8, 128])),
            reads=[skR], writes=[skPR])
        k.I("dve", lambda e: e.tensor_copy(out=skS.ap().rearrange("p (b h t) -> p b h t", b=16, h=8),
                                          in_=sk_w[:, 16:24].unsqueeze(1).unsqueeze(3).to_broadcast([2, 16, 8, 4])),
            reads=[skR], writes=[skPR])

    bankM = Rot(pb[0:6])

    def mixer(l, tile, NT):
        B = (tile == 1)
        nblk = 8
        has_s = B and len(NT) == 3
        layer_params(l)
        if has_s:
            k.dma("pool", kc.ap(), ck[l].rearrange("b s c -> s b c"), writes=[kcR], grp=kcG)
            k.dma("pool", vc.ap(), cv[l].rearrange("b s c -> s b c"), writes=[vcR], grp=vcG)
            k.dma("sp", cc32.ap(), cc_in[l].rearrange("b j c -> (b j) c"), writes=[cc32R], grp=ccG)
            k.dma("sp", o_ks[l, :, 0:124, :], ck[l, :, 4:128, :], grp=miscG, is_out=True)
            k.dma("sp", o_vs[l, :, 0:124, :], cv[l, :, 4:128, :], grp=miscG, is_out=True)
        prenorm(l, 2, NT)

        def ntof(blk):
            return blk // 4 if blk < 8 else 2

        def bail(stage):
            if mix_stop <= stage:
                while wA.pos < len(wA.seq) and wA.seq[wA.pos][0] == "min" and wA.seq[wA.pos][1] == l:
                    wA.pos += 1
                while wA.emitted < wA.pos:
                    wA.emitted += 1
                while wB.pos < len(wB.seq) and wB.seq[wB.pos][0] == "mo" and wB.seq[wB.pos][1] == l:
                    wB.pos += 1
                while wB.emitted < wB.pos:
                    wB.emitted += 1
                return True
            return False
        if bail(0):
            return
        sva, svar = wA.get(("min", l, 1))
        skv, skvr = wA.get(("min", l, 4), held=1)
        blocks = list(range(nblk)) + ([8] if has_s else [])
        lnq = []

        def ln_stats(blk, P, vg, vgr):
            par = blk % 2
            sm = small2[par]
            smR = small2R[par]
            v3 = vg[0:P, 0:256].rearrange("p (h d) -> p h d", d=64)
            s1, s2, mean, msq = sm[0:P, 0:4], sm[0:P, 4:8], sm[0:P, 8:12], sm[0:P, 12:16]
            var, rst, nb = sm[0:P, 16:20], sm[0:P, 20:24], sm[0:P, 24:28]
            k.tiny_ctx = True
            k.I("dve", lambda e: e.reduce_sum(out=s1, in_=v3, axis=AX.X), reads=[vgr], writes=[smR])
            t2, t2r = f32pool.items[2]
            act(t2[0:P, 0:256], vg[0:P, 0:256], AF.Square, [vgr], [t2r])
            k.I("dve", lambda e: e.reduce_sum(out=s2, in_=t2[0:P, 0:256].rearrange("p (h d) -> p h d", d=64), axis=AX.X),
                reads=[t2r], writes=[smR])
            k.I("dve", lambda e: e.tensor_single_scalar(out=mean, in_=s1, scalar=1.0 / 64, op=ALU.mult), reads=[smR], writes=[smR])
            tt(msq, mean, mean, ALU.mult, [smR], [smR])
            stt(var, s2, 1.0 / 64, msq, ALU.mult, ALU.subtract, [smR], [smR])
            k.tiny_ctx = False

            def finish():
                k.tiny_ctx = True
                rsqrt_eps(rst, var, [smR], [smR])
                stt(nb, mean, -1.0, rst, ALU.mult, ALU.mult, [smR], [smR])
                k.tiny_ctx = False
                for h in range(4):
                    act(vg[0:P, h * 64:(h + 1) * 64], vg[0:P, h * 64:(h + 1) * 64], AF.Identity, [vgr, smR], [vgr],
                        scale=rst[:, h:h + 1], bias=nb[:, h:h + 1])
                if blk == 8:
                    sg_, sgr_, sgg = stg.next()
                    tt(sg_[0:64, 0:256], vg[0:64, 0:256], gsg[0:64, :], ALU.mult, [vgr, gsgR], [sgr_])
                    k.dma("sp", o_sgu[l], sg_[0:64, 0:256], reads=[sgr_], grp=sgg, is_out=True)
                    k.I("dve", lambda e: e.tensor_copy(out=vta(8, 0, 64, 0, 256), in_=sg_[0:64, 0:256]), reads=[sgr_], writes=[VAR[8]])
                else:
                    tt(vta(blk, 0, P, 0, 256), vg[0:P, 0:256], gsg[0:P, :], ALU.mult, [vgr, gsgR], [VAR[blk]])
            return finish

        for blk in blocks:
            P = 128 if blk < 8 else 64
            t0 = blk * 128
            nt = ntof(blk)
            pt, pr = bankM.next()
            for kc_ in range(8):
                mm(pt[0:P, 0:256], H[:, kc_, t0:t0 + P], sva[:, kc_, :], kc_ == 0, kc_ == 7, [svar, HR[kc_][nt]], [pr])
            for kc_ in range(8):
                mm(pt[0:P, 256:512], H[:, kc_, t0:t0 + P], skv[:, kc_, :], kc_ == 0, kc_ == 7, [skvr, HR[kc_][nt]], [pr])
            vg, vgr = vgpool.next()
            act(vg[0:P, 0:256], pt[0:P, 0:256], AF.Gelu, [pr], [vgr])
            act(vtb(blk, 0, P, 0, 128), pt[0:P, 384:512], AF.Copy, [pr], [VBR[blk]])
            if blk == 8 or (B and blk == 7):
                sg_, sgr_, sgg = stg.next()
                act(sg_[0:P, 0:256], pt[0:P, 256:512], AF.Copy, [pr], [sgr_])
                if blk == 7:
                    k.dma("sp", o_kp[l], sg_[:, 0:128], reads=[sgr_], grp=sgg, is_out=True)
                    k.dma("sp", o_vp[l], sg_[:, 128:256], reads=[sgr_], grp=sgg, is_out=True)
                else:
                    for b_ in range(16):
                        k.dma("sp", o_ks[l, b_, 124:128, :], sg_[4 * b_:4 * b_ + 4, 0:128], reads=[sgr_], grp=sgg, is_out=True)
                        k.dma("sp", o_vs[l, b_, 124:128, :], sg_[4 * b_:4 * b_ + 4, 128:256], reads=[sgr_], grp=sgg, is_out=True)
            pend = lnq.pop(0) if lnq else None
            lnq.append(ln_stats(blk, P, vg, vgr))
            if pend is not None:
                pend()
        while lnq:
            lnq.pop(0)()

        if bail(1):
            return
        for nt, (off, w) in enumerate(NT[0:2]):
            pt, pr = bankM.next()
            for kc_ in range(8):
                mm(pt[:, 0:w], skv[:, kc_, 0:128], H[:, kc_, off:off + w], kc_ == 0, kc_ == 7, [skvr, HR[kc_][nt]], [pr])
            act(Dbf[0:128, 8704 + off: 8704 + off + w], pt[:, 0:w], AF.Copy, [pr], [KDR[0][nt], KU[nt]])
        k.dma("sp", Dbf[0:64, 8704 + TM: 8704 + TM + 1024], Dbf[64:128, 8704: 8704 + 1024],
              reads=[KU[0], KU[1]], writes=[KDR[1][0], KDR[1][1]], grp=kuG)
        for kv in range(2):
            for nt, (off, w) in enumerate(NT):
                if nt < 2:
                    continue
                pt, pr = bankM.next()
                if False:
                    pass
                else:
                    for kc_ in range(8):
                        mm(pt[0:64, 0:64], skv[:, kc_, kv * 64:(kv + 1) * 64], H[:, kc_, off:off + 64], kc_ == 0, kc_ == 7, [skvr, HR[kc_][nt]], [pr])
                    act(ks_ap[:, kv * 64:(kv + 1) * 64], pt[0:64, 0:64], AF.Copy, [pr], [KSR])

        for jq in range(2):
            st, sr = wA.get(("min", l, 2 + jq))
            for ci in range(2):
                he, ho = jq * 4 + 2 * ci, jq * 4 + 2 * ci + 1
                for nt, (off, w) in enumerate(NT):
                    if nt == 2:
                        continue
                    pt, pr = bankM.next()
                    for kc_ in range(8):
                        mm(pt[:, 0:w], st[:, kc_, ci * 128:(ci + 1) * 128], H[:, kc_, off:off + w], kc_ == 0, kc_ == 7, [sr, HR[kc_][nt]], [pr])
                    act(Abf[0:128, 4352 + he * TM + off: 4352 + he * TM + off + w], pt[:, 0:w], AF.Copy, [pr],
                        [QR[he][nt], QU[he // 2][nt]], scale=0.125)
                k.dma("sp", Abf[0:64, 4352 + ho * TM: 4352 + ho * TM + 1024], Abf[64:128, 4352 + he * TM: 4352 + he * TM + 1024],
                      reads=[QU[he // 2][0], QU[he // 2][1]], writes=[QR[ho][0], QR[ho][1]], grp=qG[he // 2])
            if has_s:
                for hh in range(4):
                    h = jq * 4 + hh
                    pt, pr = bankM.next()
                    for kc_ in range(8):
                        mm(pt[0:64, 0:64], st[:, kc_, hh * 64:(hh + 1) * 64], H[:, kc_, 1024:1088], kc_ == 0, kc_ == 7, [sr, HR[kc_][2]], [pr])
                    act(qs_ap.rearrange("p (b h t) -> p b h t", b=16, h=8)[:, :, h, :], pt[0:64, 0:64].rearrange("p (b t) -> p b t", t=4),
                        AF.Copy, [pr], [QSR], scale=0.125)
        units = []
        bankU = Rot([pb[6], pb[7]])
        SL = {}

        def ua_pair(ci):
            if 'ua' not in SL:
                SL['ua'] = wA.get(("min", l, 0))
            st, sr = SL['ua']
            he, ho = 2 * ci, 2 * ci + 1
            for nt, (off, w) in enumerate(NT):
                pt, pr = bankM.next()
                for kc_ in range(8):
                    mm(pt[:, 0:w], st[:, kc_, ci * 128:(ci + 1) * 128], H[:, kc_, off:off + w], kc_ == 0, kc_ == 7, [sr, HR[kc_][nt]], [pr])
                act(Dbf[0:128, he * TM + off: he * TM + off + w], pt[:, 0:w], AF.Gelu, [pr], [UR[he][nt], UU[ci][nt]])
            tw = NT[-1][0] + NT[-1][1]
            k.dma("sp", Dbf[0:64, ho * TM: ho * TM + tw], Dbf[64:128, he * TM: he * TM + tw],
                  reads=[UU[ci][nt] for nt in range(len(NT))], writes=[UR[ho][nt] for nt in range(len(NT))], grp=uG[ci])
        for ci in range(2):
            ua_pair(ci)

        def gb_unit(cc, nt, off, w):
            if 'gb' not in SL:
                SL['gb'] = wA.get(("min", l, 5))
            st, sr = SL['gb']
            pt, pr = bankM.next()
            for kc_ in range(8):
                mm(pt[:, 0:w], st[:, kc_, cc * 128:(cc + 1) * 128], H[:, kc_, off:off + w], kc_ == 0, kc_ == 7, [sr, HR[kc_][nt]], [pr])
            act(gbv(cc, off, w), pt[:, 0:w], AF.Copy, [pr], [GBR[cc][nt]])
        for cc in range(2):
            for nt, (off, w) in enumerate(NT):
                gb_unit(cc, nt, off, w)

        def z_setup():
            SL['gc'] = wA.get(("min", l, 6))
            SL['hc'] = wA.get(("min", l, 7), held=1)
            if not B:
                for cc in range(2):
                    k.I("dve", lambda e, cc=cc: e.memset(zv(cc, 0, 2), 0.0), writes=[ZHR[cc]])
            else:
                for cc in range(2):
                    k.I("dve", lambda e, cc=cc: e.tensor_copy(out=zv(cc, 0, 2), in_=zprev[:, l, cc, :]), reads=[ZPR[l]], writes=[ZHR[cc]])
            if has_s:
                for cc in range(2):
                    pt, pr = pb[7]
                    k.I("pe", lambda e, cc=cc, pt=pt: e.transpose(pt[:, 0:32], cc32[:, cc * 128:(cc + 1) * 128], cf[0:32, C_ID:C_ID + 32]),
                        reads=[cc32R, cfR], writes=[pr])
                    act(zs[:, cc, :, 0:2], pt[:, 0:32].rearrange("p (b j) -> p b j", j=2), AF.Copy, [pr], [zsR])
        z_setup()

        def z_unit(cc, nt, off, w):
            sgc, sgcr = SL['gc']
            shc, shcr = SL['hc']
            pg, pgr = bankM.next()
            ph, phr = bankM.next()
            for kc_ in range(8):
                mm(pg[:, 0:w], sgc[:, kc_, cc * 128:(cc + 1) * 128], H[:, kc_, off:off + w], kc_ == 0, kc_ == 7, [sgcr, HR[kc_][nt]], [pgr])
            for kc_ in range(8):
                mm(ph[:, 0:w], shc[:, kc_, cc * 128:(cc + 1) * 128], H[:, kc_, off:off + w], kc_ == 0, kc_ == 7, [shcr, HR[kc_][nt]], [phr])
            gt, gtr = f32pool.next()
            act(gt[:, 0:w], pg[:, 0:w], AF.Copy, [pgr], [gtr])
            if nt < 2:
                tt(zv(cc, 2 + off, 2 + off + w), ph[:, 0:w], gt[:, 0:w], ALU.mult, [phr, gtr], [ZR[cc][nt]])
            else:
                tt(zs[:, cc, :, 2:6], ph[:, 0:64].rearrange("p (b t) -> p b t", t=4),
                   gt[:, 0:64].rearrange("p (b t) -> p b t", t=4), ALU.mult, [phr, gtr, zsR], [zsR])
        for cc in range(2):
            for nt, (off, w) in enumerate(NT):
                z_unit(cc, nt, off, w)

        def z_tail():
            if not B:
                for cc in range(2):
                    k.I("dve", lambda e, cc=cc: e.tensor_copy(out=zprev[:, l, cc, :], in_=zv(cc, 1024, 1026)), reads=[ZR[cc][1]], writes=[ZPR[l]])
            else:
                sg_, sgr_, sgg = stg.next()
                for cc in range(2):
                    pt, pr = pb[7]
                    k.I("pe", lambda e, cc=cc, pt=pt: e.transpose(pt[0:2, 0:128], zv(cc, 1024, 1026), cf[:, C_ID:C_ID + 128]),
                        reads=[ZR[cc][1], cfR], writes=[pr])
                    act(sg_[0:2, cc * 128:(cc + 1) * 128], pt[0:2, 0:128], AF.Copy, [pr], [sgr_])
                k.dma("sp", o_cp[l], sg_[0:2, 0:256], reads=[sgr_], grp=sgg, is_out=True)
                if has_s:
                    sg_, sgr_, sgg = stg.next()
                    for cc in range(2):
                        pt, pr = pb[7]
                        zt, ztr = f32pool.next()
                        act(zt[:, 0:32].rearrange("p (b j) -> p b j", j=2), zs[:, cc, :, 4:6], AF.Copy, [zsR], [ztr])
                        k.I("pe", lambda e, cc=cc, pt=pt, zt=zt: e.transpose(pt[0:32, 0:128], zt[:, 0:32], cf[:, C_ID:C_ID + 128]),
                            reads=[ztr, cfR], writes=[pr])
                        act(sg_[0:32, cc * 128:(cc + 1) * 128], pt[0:32, 0:128], AF.Copy, [pr], [sgr_])
                    k.dma("sp", o_cs[l], sg_[0:32, 0:256], reads=[sgr_], grp=sgg, is_out=True)

        units.append(z_tail)

        def wcv(j, cc):
            o = P_WC + (l * 3 + j) * 2 + cc
            return prm[:, o:o + 1]
        def conv_unit(cc, nt, off, w):
            a, ar = f32pool.next()
            if nt < 2:
                zr = [ZR[cc][nt], ZHR[cc]] + ([ZR[cc][0]] if nt == 1 else [])
                ts(a[:, 0:w], zv(cc, 2 + off, 2 + off + w), wcv(2, cc), None, ALU.mult, ALU.bypass, zr + [prmR], [ar])
                stt(a[:, 0:w], zv(cc, 1 + off, 1 + off + w), wcv(1, cc), a[:, 0:w], ALU.mult, ALU.add, zr + [prmR, ar], [ar])
                stt(a[:, 0:w], zv(cc, off, off + w), wcv(0, cc), a[:, 0:w], ALU.mult, ALU.add, zr + [prmR, ar], [ar])
                tt(ycv(cc, off, w), a[:, 0:w], gbv(cc, off, w), ALU.mult, [ar, GBR[cc][nt]], [YCR[cc][nt]])
            else:
                a3 = a[:, 0:64].rearrange("p (b t) -> p b t", t=4)
                ts(a3, zs[:, cc, :, 2:6], wcv(2, cc), None, ALU.mult, ALU.bypass, [zsR, prmR], [ar])
                stt(a3, zs[:, cc, :, 1:5], wcv(1, cc), a3, ALU.mult, ALU.add, [zsR, prmR, ar], [ar])
                stt(a3, zs[:, cc, :, 0:4], wcv(0, cc), a3, ALU.mult, ALU.add, [zsR, prmR, ar], [ar])
                tt(ycv(cc, off, 64), a[:, 0:64], gbv(cc, off, 64), ALU.mult, [ar, GBR[cc][nt]], [YCR[cc][nt]])

        for cc in range(2):
            for nt, (off, w) in enumerate(NT):
                units.append(lambda cc=cc, nt=nt, off=off, w=w: conv_unit(cc, nt, off, w))

        def gmlp_unit(blk):
            P = 128 if blk < 8 else 64
            t0 = blk * 128
            nt = ntof(blk)
            pt, pr = pb[7]
            for h in range(4):
                if blk < 8:
                    mm(pt[0:64, h * 128:(h + 1) * 128], vta(blk, 0, 128, h * 64, (h + 1) * 64), WT[:, h, :], True, False, [VAR[blk], WTR], [pr])
                    mm(pt[0:64, h * 128:(h + 1) * 128], cb[0:1, CB_ONE:CB_ONE + 64], bsg[0:1, h, :], False, True, [cbR, bsgR], [pr])
                else:
                    mm(pt[0:64, h * 128:h * 128 + 64], vta(8, 0, 64, h * 64, (h + 1) * 64), Wblk[:, h, :], True, False, [VAR[8], WblkR], [pr])
                    mm(pt[0:64, h * 128:h * 128 + 64], cb[0:1, CB_ONE:CB_ONE + 64], bss[0:1, h, :], False, True, [cbR, bssR], [pr])
            uo = Dbf[0:64, t0: t0 + 4 * TM].rearrange("p (h t) -> p h t", t=TM)[:, :, 0:P]
            yo = Abf[0:64, t0: t0 + 4 * TM].rearrange("p (h t) -> p h t", t=TM)[:, :, 0:P]
            tt(yo, uo, pt[0:64, 0:512].rearrange("p (h t) -> p h t", t=128)[:, :, 0:P], ALU.mult,
               [pr] + [UR[h][nt] for h in range(4)], [YAR[h][nt] for h in range(4)])

        for blk in blocks:
            units.append(lambda blk=blk: gmlp_unit(blk))

        bankS = Rot(pb[0:4])
        items = [(blk, g) for blk in range(nblk) for g in range(2)]

        def kcur(blk, g):
            return kdv(g, blk * 128, 128), KDR[g][ntof(blk)]

        def emit_scores(blk, g):
            res = []
            gblk = tile * 8 + blk
            for which in (0, 1):
                if which == 1 and gblk == 0:
                    res.append(None)
                    continue
                pt, pr = bankS.next()
                bo = C_BIAS + which * 1024 + g * 512
                mm(pt.ap(), cb[:, CB_ID:CB_ID + 128], cbt[:, bo:bo + 512], True, False, [cbR, cbtR], [pr])
                for hp in range(4):
                    if which == 0:
                        ka, kr = kcur(blk, g)
                    elif blk > 0:
                        ka, kr = kcur(blk - 1, g)
                    else:
                        ka, kr = kprev[:, l, g, :], KPR[l]
                    mm(pt[:, hp * 128:(hp + 1) * 128], ka, qv(4 * g + hp, blk * 128, 128), False, hp == 3,
                       [kr, QR[4 * g + hp][ntof(blk)]], [pr])
                res.append((pt, pr))
            return res

        def emit_soft(blk, g, sc):
            ps_ = []
            for which in (0, 1):
                if sc[which] is None:
                    ps_.append(None)
                    continue
                pt, pr = sc[which]
                p_, p_r = bfpool.next()
                act(p_.ap(), pt.ap(), AF.Exp, [pr], [p_r])
                ps_.append((p_, p_r))
            return ps_

        def emit_pv(blk, g, ps_):
            po, por = pb[4]
            pd, pdr = pb[5]
            lst = [(w_, ps_[w_]) for w_ in (0, 1) if ps_[w_] is not None]
            for i, (which, (p_, p_r)) in enumerate(lst):
                if which == 0:
                    va_, vr_ = vtb(blk, 0, 128, g * 64, (g + 1) * 64), VBR[blk]
                elif blk > 0:
                    va_, vr_ = vtb(blk - 1, 0, 128, g * 64, (g + 1) * 64), VBR[blk - 1]
                else:
                    va_, vr_ = vprev[:, l, g * 64:(g + 1) * 64], VPR[l]
                mm(po[0:64, :], va_, p_.ap(), i == 0, i == len(lst) - 1, [vr_, p_r], [por])
            for i, (which, (p_, p_r)) in enumerate(lst):
                mm(pd[0:64, :], cb[:, CB_ONE:CB_ONE + 64], p_.ap(), i == 0, False, [cbR, p_r], [pdr])
            mm(pd[0:64, :], cb[0:2, CB_ONE:CB_ONE + 64], skP[0:2, 4 * g:4 * g + 4, :].rearrange("p h t -> p (h t)"), False, True,
               [cbR, skPR], [pdr])
            d_, d_r = f32pool.next()
            k.I("dve", lambda e, d_=d_, pd=pd: e.reciprocal(out=d_[0:64, :], in_=pd[0:64, :]), reads=[pdr], writes=[d_r])
            t0 = blk * 128
            yo = H[0:64, 4 * g:4 * g + 4, t0:t0 + 128]
            tt(yo, po[0:64, :].rearrange("p (h t) -> p h t", t=128), d_[0:64, :].rearrange("p (h t) -> p h t", t=128), ALU.mult,
               [por, d_r], [YBR[4 * g + hp][ntof(blk)] for hp in range(4)])

        upos = [0]
        per_item = (len(units) + len(items) - 1) // len(items)

        def pop_units(n):
            for _ in range(n):
                if upos[0] < len(units):
                    units[upos[0]]()
                    upos[0] += 1
        n_it = len(items)
        scs = {0: emit_scores(*items[0])}
        if n_it > 1:
            scs[1] = emit_scores(*items[1])
        pss = {0: emit_soft(items[0][0], items[0][1], scs.pop(0))}
        for i, (blk, g) in enumerate(items):
            if i + 2 < n_it:
                scs[i + 2] = emit_scores(*items[i + 2])
            if i + 1 < n_it:
                pss[i + 1] = emit_soft(items[i + 1][0], items[i + 1][1], scs.pop(i + 1))
            pop_units(per_item)
            emit_pv(blk, g, pss.pop(i))
        pop_units(len(units))

        if not B:
            for g in range(2):
                k.I("act", lambda e, g=g: e.activation(out=kprev[:, l, g, :], in_=kdv(g, 896, 128), func=AF.Copy),
                    reads=[KDR[g][1]], writes=[KPR[l]])
            k.I("act", lambda e: e.activation(out=vprev[:, l, :], in_=vtb(7, 0, 128, 0, 128), func=AF.Copy), reads=[VBR[7]], writes=[VPR[l]])

        if bail(4):
            return
        if has_s:
            for grp8 in range(4):
                pt, pr = bankM.next()
                ptb = pt.ap().bitcast(BF16)
                for j in range(8):
                    idx = grp8 * 8 + j
                    b, kv = idx // 2, idx % 2
                    k.I("pe", lambda e, ptb=ptb, j=j, b=b, kv=kv: e.transpose(ptb[0:64, j * 128:(j + 1) * 128], kc[:, b, kv * 64:(kv + 1) * 64], cb[:, CB_ID:CB_ID + 128]),
                        reads=[kcR, cbR], writes=[pr])
                act(Abf[0:64, 18688 + grp8 * 1024: 18688 + (grp8 + 1) * 1024], ptb[0:64, :], AF.Copy, [pr], [KTR])
            psc, pscr = pb[0]
            psn = [pb[1], pb[2]]
            pos = [pb[3], pb[4]]
            pd, pdr = pb[5]
            for b in range(16):
                for kv in range(2):
                    o_ = b * 32 + kv * 16
                    mm(psc[:, o_:o_ + 16], ktc(b, kv), qs_ap[:, o_:o_ + 16], True, True, [KTR, QSR], [pscr])
            for kv in range(2):
                mm(psn[kv][0][0:64, :], ks_ap[:, kv * 64:(kv + 1) * 64], qs_ap, True, True, [KSR, QSR], [psn[kv][1]])
            s_, s_r = f32pool.next()
            tt(s_.ap(), psc.ap(), cbt[:, C_BSC:C_BSC + 512], ALU.add, [pscr, cbtR], [s_r])
            pc_, pc_r = bfpool.next()
            act(pc_.ap(), s_.ap(), AF.Exp, [s_r], [pc_r])
            s2_, s2_r = f32pool.next()

            def kvv(ap, kv):
                return ap.rearrange("p (b x) -> p b x", x=32)[:, :, kv * 16:(kv + 1) * 16]
            for kv in range(2):
                tt(kvv(s2_[0:64, :], kv), kvv(psn[kv][0][0:64, :], kv), kvv(cbt[0:64, C_BSN:C_BSN + 512], kv), ALU.add,
                   [psn[kv][1], cbtR], [s2_r])
            pn_, pn_r = bfpool.next()
            act(pn_[0:64, :], s2_[0:64, :], AF.Exp, [s2_r], [pn_r])
            for kv in range(2):
                po, por = pos[kv]
                mm(po[0:64, :], vtb(8, 0, 64, kv * 64, (kv + 1) * 64), pn_[0:64, :], True, False, [VBR[8], pn_r], [por])
                for b in range(16):
                    o_ = b * 32 + kv * 16
                    mm(po[0:64, o_:o_ + 16], vc[:, b, kv * 64:(kv + 1) * 64], pc_[:, o_:o_ + 16], False, b == 15, [vcR, pc_r], [por])
            mm(pd[0:64, :], cb[:, CB_ONE:CB_ONE + 64], pc_.ap(), True, False, [cbR, pc_r], [pdr])
            mm(pd[0:64, :], cb[0:64, CB_ONE:CB_ONE + 64], pn_[0:64, :], False, False, [cbR, pn_r], [pdr])
            mm(pd[0:64, :], cb[0:2, CB_ONE:CB_ONE + 64], skS.ap(), False, True, [cbR, skPR], [pdr])
            d_, d_r = f32pool.next()
            k.I("dve", lambda e, d_=d_, pd=pd: e.reciprocal(out=d_[0:64, :], in_=pd[0:64, :]), reads=[pdr], writes=[d_r])
            for kv in range(2):
                po, por = pos[kv]
                yo = H[0:64, 4 * kv:4 * kv + 4, 1024:1088].rearrange("p h (b t) -> p b h t", t=4)
                tt(yo, po[0:64, :].rearrange("p (b h t) -> p b h t", b=16, h=8)[:, :, 4 * kv:4 * kv + 4, :],
                   d_[0:64, :].rearrange("p (b h t) -> p b h t", b=16, h=8)[:, :, 4 * kv:4 * kv + 4, :],
                   ALU.mult, [por, d_r], [YBR[4 * kv + h][2] for h in range(4)])

        if bail(5):
            return
        def gnorm(view, regs, nh, pin, ones_off, goff):
            for nt, (off, w) in enumerate(NT):
                rs, rsr = rms_rstd([(view(h, off, w), regs[h][nt]) for h in range(nh)], ones_off, pin, pin, w)
                for h in range(nh):
                    o = goff + l * nh + h
                    stt(view(h, off, w), view(h, off, w), prm[0:pin, o:o + 1], rs[0:pin, 0:w], ALU.mult, ALU.mult,
                        [regs[h][nt], rsr, prmR], [regs[h][nt]])
        gnorm(yav, YAR, 4, 64, CB_O256, P_GA)
        gnorm(ybv, YBR, 8, 64, CB_O512, P_GB)
        gnorm(ycv, YCR, 2, 128, CB_O256, P_GC)

        if bail(6):
            return
        tw = NT[-1][0] + NT[-1][1]
        nts = range(len(NT))
        for ci in range(2):
            he, ho = 2 * ci, 2 * ci + 1
            k.dma("sp", Abf[64:128, he * TM: he * TM + tw], Abf[0:64, ho * TM: ho * TM + tw],
                  reads=[YAR[ho][nt] for nt in nts], writes=[YSA[ci][nt] for nt in nts], grp=ysG[ci])
        for ci in range(4):
            he, ho = 2 * ci, 2 * ci + 1
            k.dma("sp", Hflat[64:128, he * TM: he * TM + tw], Hflat[0:64, ho * TM: ho * TM + tw],
                  reads=[YBR[ho][nt] for nt in nts], writes=[YSB[ci][nt] for nt in nts], grp=ysG[2 + ci])
        for c in range(8):
            st, sr = wB.get(("mo", l, c))
            for nt, (off, w) in enumerate(NT):
                bd, bdr = bankD.next()
                ops = [(st[:, ci, :], Abf[0:128, 2 * ci * TM + off: 2 * ci * TM + off + w], [YAR[2 * ci][nt], YSA[ci][nt]]) for ci in range(2)]
                ops += [(st[:, 2 + ci, :], Hflat[0:128, 2 * ci * TM + off: 2 * ci * TM + off + w], [YBR[2 * ci][nt], YSB[ci][nt]]) for ci in range(4)]
                ops += [(st[:, 6 + cc, :], ycv(cc, off, w), [YCR[cc][nt]]) for cc in range(2)]
                for i, (lh, rh, rr) in enumerate(ops):
                    mm(bd[:, 0:w], lh, rh, i == 0, i == len(ops) - 1, [sr] + rr, [bdr])
                act(dv(c, off, w), bd[:, 0:w], AF.Copy, [bdr], [DR[c][nt]])
        postnorm_residual(l, 3, False, NT)

    for tile in tiles:
        NT = [(0, 512), (512, 512)] + ([(1024, 64)] if tile == 1 else [])
        for c in range(8):
            k.dma("sp", X[:, c, 0:1024], xT[c * 128:(c + 1) * 128, tile * 1024:(tile + 1) * 1024], writes=[XR[c][0], XR[c][1]])
            if tile == 1:
                k.dma("sp", X[:, c, 1024:1088], xT[c * 128:(c + 1) * 128, 2048:2112], writes=[XR[c][2]])
        for kind, l, i in steps[tile]:
            if kind == "ffn":
                ffn(l, i, NT)
            else:
                mixer(l, tile, NT)
        for c in range(8):
            k.dma("sp", yT[c * 128:(c + 1) * 128, tile * 1024:(tile + 1) * 1024], X[:, c, 0:1024], reads=[XR[c][0], XR[c][1]], grp=outG, is_out=True)
            if tile == 1:
                k.dma("sp", yT[c * 128:(c + 1) * 128, 2048:2112], X[:, c, 1024:1088], reads=[XR[c][2]], grp=outG, is_out=True)
    k.finish()
    return nc, k


def _in_maps(x_prompt, x_sample, cache_swa_k, cache_swa_v, cache_conv, norm_g, w_ffn_gu, w_ffn_down,
             w_mix_in, w_mix_out, g_mix_out, w_sgu, b_sgu, g_sgu, attn_sinks, w_conv, depth=4):
    f = lambda a: np.ascontiguousarray(np.asarray(a, dtype=np.float32))
    x_prompt, x_sample = f(x_prompt), f(x_sample)
    cf, cbias = _consts()
    prm = _prm(f(norm_g), f(g_mix_out), f(w_conv), f(attn_sinks))
    shared = {"w_gu": f(w_ffn_gu[:depth]), "w_dn": f(w_ffn_down[:depth]), "w_in": f(w_mix_in[:depth]), "w_out": f(w_mix_out[:depth]),
              "w_sgu": f(w_sgu), "b_sgu": f(b_sgu), "g_sgu": f(g_sgu), "prm": prm, "cf": cf, "cbias": cbias}
    ck_all = f(cache_swa_k).reshape(4, 128, 128, 128)
    cv_all = f(cache_swa_v).reshape(4, 128, 128, 128)
    cc_all = f(cache_conv)
    maps = []
    for c in range(NCORE):
        xt = np.empty((1024, 2112), np.float32)
        xt[:, 0:2048] = x_prompt[c].T
        xt[:, 2048:2112] = x_sample[c * 16:(c + 1) * 16].reshape(64, 1024).T
        m = dict(shared)
        m["xT"] = xt
        m["ck"] = np.ascontiguousarray(ck_all[:, c * 16:(c + 1) * 16])
        m["cv"] = np.ascontiguousarray(cv_all[:, c * 16:(c + 1) * 16])
        m["cc"] = np.ascontiguousarray(cc_all[:, c * 16:(c + 1) * 16])
        maps.append(m)
    return maps


def _assemble(results):
    y_p = np.empty((8, 2048, 1024), np.float32)
    y_s = np.empty((128, 4, 1024), np.float32)
    sgu = np.empty((4, 128, 4, 4, 64), np.float32)
    kp = np.empty((4, 8, 128, 2, 64), np.float32)
    vp = np.empty((4, 8, 128, 2, 64), np.float32)
    ks = np.empty((4, 128, 128, 2, 64), np.float32)
    vs = np.empty((4, 128, 128, 2, 64), np.float32)
    cp = np.empty((4, 8, 2, 256), np.float32)
    cs = np.empty((4, 128, 2, 256), np.float32)
    for c, r in enumerate(results):
        yt = r["yT"]
        y_p[c] = yt[:, 0:2048].T
        y_s[c * 16:(c + 1) * 16] = yt[:, 2048:2112].T.reshape(16, 4, 1024)
        sgu[:, c * 16:(c + 1) * 16] = r["o_sgu"].reshape(4, 16, 4, 4, 64)
        kp[:, c] = r["o_kp"].reshape(4, 128, 2, 64)
        vp[:, c] = r["o_vp"].reshape(4, 128, 2, 64)
        ks[:, c * 16:(c + 1) * 16] = r["o_ks"].reshape(4, 16, 128, 2, 64)
        vs[:, c * 16:(c + 1) * 16] = r["o_vs"].reshape(4, 16, 128, 2, 64)
        cp[:, c] = r["o_cp"]
        cs[:, c * 16:(c + 1) * 16] = r["o_cs"].reshape(4, 16, 2, 256)
    return (y_p, y_s, sgu, kp, vp, ks, vs, cp, cs)


def kernel(**inputs):
    maps = _in_maps(**inputs)
    nc, _ = build(depth=4)
    res = run_bass_kernel_spmd(nc, maps, core_ids=list(range(NCORE)))
    return _assemble(res.results)
```
